# Optimizing a Trainium2 kernel written in Bass

```python
import math
import jax, jax.numpy as jnp
from jax import lax
import numpy as np

D_MODEL = 1024
BATCH = 8
SEQ = 2048
DEPTH = 1

CHUNK = 64
N_META = 16
META_PAD = CHUNK - N_META

GDN_HEADS = 8
GDN_DK = 128
GDN_DV = 128
GDN_CONV = 4
GDN_QK_W = GDN_HEADS * GDN_DK
GDN_V_W = GDN_HEADS * GDN_DV

SWA_HEADS = 16
SWA_KV_HEADS = 4
SWA_GROUPS = SWA_HEADS // SWA_KV_HEADS
SWA_HD = 64
SWA_Q_W = SWA_HEADS * SWA_HD
SWA_KV_W = SWA_KV_HEADS * SWA_HD
WINDOW = 128
WINDOW_CHUNKS = WINDOW // CHUNK

N_BRANCH = 2
IN_WIDTHS = (GDN_QK_W, GDN_QK_W, GDN_V_W, GDN_V_W, GDN_HEADS, GDN_HEADS,
             SWA_Q_W, SWA_KV_W, SWA_KV_W, SWA_Q_W, D_MODEL, D_MODEL)
IN_VALUE_SEGMENTS = (2, 8)

DEEPNORM_ALPHA = (2.0 * DEPTH) ** 0.25
DEEPNORM_BETA = (8.0 * DEPTH) ** -0.25
LN_EPS = 1e-5
RMS_EPS = 1e-6
L2_EPS = 1e-6

kernel_name = "hybrid_gdn_swa_sink_alibi_metatoken_deepnorm"


def layer_norm(x, w, b):
    xf = x.astype(jnp.float32)
    mu = jnp.mean(xf, axis=-1, keepdims=True)
    var = jnp.mean(jnp.square(xf - mu), axis=-1, keepdims=True)
    y = (xf - mu) * lax.rsqrt(var + LN_EPS) * w.astype(jnp.float32) + b.astype(jnp.float32)
    return y.astype(x.dtype)


def l2_normalize(x):
    return x * lax.rsqrt(jnp.sum(jnp.square(x), axis=-1, keepdims=True) + L2_EPS)


def causal_depthwise_conv(x, w):
    ch = x.shape[-1]
    return lax.conv_general_dilated(
        x, w[:, None, :].astype(x.dtype), window_strides=(1,),
        padding=[(GDN_CONV - 1, 0)], dimension_numbers=('NWC', 'WIO', 'NWC'),
        feature_group_count=ch)


def alibi_slopes(n_heads):
    return jnp.exp2(-8.0 * jnp.arange(1, n_heads + 1, dtype=jnp.float32) / n_heads)


def gated_deltanet(q, k, v, beta_raw, a_raw, conv_w, a_log, dt_bias):
    f32 = jnp.float32
    b, L, _ = q.shape
    qkv = jax.nn.silu(causal_depthwise_conv(jnp.concatenate([q, k, v], axis=-1), conv_w))
    q, k, v = jnp.split(qkv.astype(f32), [GDN_QK_W, 2 * GDN_QK_W], axis=-1)
    q = l2_normalize(q.reshape(b, L, GDN_HEADS, GDN_DK)) * (GDN_DK ** -0.5)
    k = l2_normalize(k.reshape(b, L, GDN_HEADS, GDN_DK))
    v = v.reshape(b, L, GDN_HEADS, GDN_DV)
    beta = jax.nn.sigmoid(beta_raw.astype(f32))
    g = -jnp.exp(a_log.astype(f32)) * jax.nn.softplus(a_raw.astype(f32) + dt_bias.astype(f32))

    pad4 = ((0, 0), (META_PAD, 0), (0, 0), (0, 0))
    pad3 = ((0, 0), (META_PAD, 0), (0, 0))
    q, k, v = jnp.pad(q, pad4), jnp.pad(k, pad4), jnp.pad(v, pad4)
    beta, g = jnp.pad(beta, pad3), jnp.pad(g, pad3)
    n_ch = (L + META_PAD) // CHUNK

    def to_chunks(t):
        return t.reshape(b, n_ch, CHUNK, GDN_HEADS, -1).transpose(0, 3, 1, 2, 4)

    q, k, v = to_chunks(q), to_chunks(k), to_chunks(v)
    beta = beta.reshape(b, n_ch, CHUNK, GDN_HEADS).transpose(0, 3, 1, 2)
    g_cum = jnp.cumsum(g.reshape(b, n_ch, CHUNK, GDN_HEADS).transpose(0, 3, 1, 2), axis=-1)

    tri = jnp.tril(jnp.ones((CHUNK, CHUNK), dtype=bool))
    strict = jnp.tril(jnp.ones((CHUNK, CHUNK), dtype=bool), k=-1)
    decay = jnp.exp(jnp.where(tri, g_cum[..., :, None] - g_cum[..., None, :], -jnp.inf))

    k_beta = k * beta[..., None]
    v_beta = v * beta[..., None]
    lower = jnp.where(strict, jnp.einsum('bhnid,bhnjd->bhnij', k_beta, k) * decay, 0.0)
    rhs = jnp.concatenate([v_beta, k_beta * jnp.exp(g_cum)[..., None]], axis=-1)
    sol = lax.linalg.triangular_solve(lower + jnp.eye(CHUNK, dtype=f32), rhs,
                                      left_side=True, lower=True, unit_diagonal=True)
    u, w = sol[..., :GDN_DV], sol[..., GDN_DV:]
    attn = jnp.einsum('bhnid,bhnjd->bhnij', q, k) * decay
    q_dec = q * jnp.exp(g_cum)[..., None]
    k_dec = k * jnp.exp(g_cum[..., -1:] - g_cum)[..., None]
    g_last = jnp.exp(g_cum[..., -1])

    def step(S, inp):
        u_c, w_c, attn_c, q_c, k_c, gl_c = inp
        v_new = u_c - jnp.einsum('bhik,bhkv->bhiv', w_c, S)
        o_c = jnp.einsum('bhik,bhkv->bhiv', q_c, S) + jnp.einsum('bhij,bhjv->bhiv', attn_c, v_new)
        S = S * gl_c[..., None, None] + jnp.einsum('bhik,bhiv->bhkv', k_c, v_new)
        return S, o_c

    xs = tuple(jnp.moveaxis(t, 2, 0) for t in (u, w, attn, q_dec, k_dec, g_last))
    S0 = jnp.zeros((b, GDN_HEADS, GDN_DK, GDN_DV), f32)
    _, o = lax.scan(step, S0, xs)
    o = o.transpose(1, 0, 3, 2, 4).reshape(b, n_ch * CHUNK, GDN_HEADS, GDN_DV)
    return o[:, META_PAD:]


def sliding_window_sink_attention(q, k, v, sinks):
    f32 = jnp.float32
    b, L, _ = q.shape
    n_ch = (L + META_PAD) // CHUNK
    pad = ((0, 0), (META_PAD, 0), (0, 0))
    q = jnp.pad(q, pad).reshape(b, n_ch, CHUNK, SWA_KV_HEADS, SWA_GROUPS, SWA_HD).transpose(0, 3, 4, 1, 2, 5)
    k = jnp.pad(k, pad).reshape(b, n_ch, CHUNK, SWA_KV_HEADS, SWA_HD).transpose(0, 3, 1, 2, 4)
    v = jnp.pad(v, pad).reshape(b, n_ch, CHUNK, SWA_KV_HEADS, SWA_HD).transpose(0, 3, 1, 2, 4)

    def band(t):
        meta = jnp.broadcast_to(t[:, :, :1, META_PAD:], (b, SWA_KV_HEADS, n_ch, N_META, SWA_HD))
        tp = jnp.pad(t, ((0, 0), (0, 0), (WINDOW_CHUNKS, 0), (0, 0), (0, 0)))
        shifted = [tp[:, :, j:j + n_ch] for j in range(WINDOW_CHUNKS + 1)]
        return jnp.concatenate([meta] + shifted, axis=3)

    k_w, v_w = band(k), band(v)
    scores = jnp.einsum('bkgncd,bknsd->bkgncs', q, k_w).astype(f32) * (SWA_HD ** -0.5)

    c_idx = jnp.arange(n_ch)[:, None]
    t_q = c_idx * CHUNK + jnp.arange(CHUNK)[None, :] - META_PAD
    off = jnp.arange((WINDOW_CHUNKS + 1) * CHUNK)[None, :]
    key_chunk = c_idx - WINDOW_CHUNKS + off // CHUNK
    t_band = key_chunk * CHUNK + off % CHUNK - META_PAD
    t_k = jnp.concatenate([jnp.broadcast_to(jnp.arange(N_META)[None, :], (n_ch, N_META)), t_band], axis=1)
    valid = jnp.concatenate([jnp.ones((n_ch, N_META), dtype=bool), key_chunk >= 1], axis=1)
    dist = jnp.abs(t_q[:, :, None] - t_k[:, None, :]).astype(f32)

    slopes = alibi_slopes(SWA_HEADS).reshape(SWA_KV_HEADS, SWA_GROUPS)
    scores = scores - slopes[:, :, None, None, None] * dist
    scores = jnp.where(valid[:, None, :], scores, -jnp.inf)
    sink = jnp.broadcast_to(sinks.astype(f32).reshape(SWA_KV_HEADS, SWA_GROUPS, 1, 1, 1),
                            scores.shape[:-1] + (1,))
    probs = jax.nn.softmax(jnp.concatenate([scores, sink], axis=-1), axis=-1)[..., :-1]
    o = jnp.einsum('bkgncs,bknsd->bkgncd', probs.astype(v_w.dtype), v_w)
    o = o.transpose(0, 3, 4, 1, 2, 5).reshape(b, n_ch * CHUNK, SWA_Q_W)
    return o[:, META_PAD:]


def hybrid_layer(h, w_in, b_gate, conv_w, a_log, dt_bias, gdn_norm_w, sinks,
                 w_proj_a, w_proj_b, w_out, ln_w, ln_b):
    b, L, _ = h.shape
    offsets, acc = [], 0
    for wd in IN_WIDTHS[:-1]:
        acc += wd
        offsets.append(acc)
    (q_a, k_a, v_a, z_a, beta_a, dec_a, q_b, k_b, v_b, z_b, gate_a, gate_b) = jnp.split(
        h @ w_in, offsets, axis=-1)

    o_a = gated_deltanet(q_a, k_a, v_a, beta_a, dec_a, conv_w, a_log, dt_bias)
    o_a = o_a * lax.rsqrt(jnp.mean(jnp.square(o_a), axis=-1, keepdims=True) + RMS_EPS) * gdn_norm_w.astype(jnp.float32)
    y_a = (o_a.reshape(b, L, GDN_V_W) * jax.nn.silu(z_a.astype(jnp.float32))).astype(h.dtype)

    y_b = sliding_window_sink_attention(q_b, k_b, v_b, sinks) * jax.nn.silu(z_b)

    g_a = jax.nn.sigmoid(gate_a + b_gate[:D_MODEL])
    g_b = jax.nn.sigmoid(gate_b + b_gate[D_MODEL:])
    mixed = g_a * (y_a @ w_proj_a) + g_b * (y_b @ w_proj_b)
    out = mixed @ w_out
    return layer_norm(DEEPNORM_ALPHA * h + out, ln_w, ln_b)


def setup_inputs(seed: int = 0) -> dict:
    key = jax.random.key(seed)
    ks = jax.random.split(key, 16)
    f32 = jnp.float32
    x = jax.random.normal(ks[0], (BATCH, SEQ, D_MODEL), f32)
    meta_tokens = jax.random.normal(ks[1], (N_META, D_MODEL), f32)
    seg_keys = jax.random.split(ks[2], len(IN_WIDTHS))
    segs = []
    for i, (sk, wd) in enumerate(zip(seg_keys, IN_WIDTHS)):
        scale = D_MODEL ** -0.5 * (DEEPNORM_BETA if i in IN_VALUE_SEGMENTS else 1.0)
        segs.append(jax.random.normal(sk, (DEPTH, D_MODEL, wd), f32) * scale)
    w_in = jnp.concatenate(segs, axis=2)
    b_gate = 0.01 * jax.random.normal(ks[3], (DEPTH, N_BRANCH * D_MODEL), f32)
    conv_w = jax.random.normal(ks[4], (DEPTH, GDN_CONV, 2 * GDN_QK_W + GDN_V_W), f32) * (GDN_CONV ** -0.5)
    a_log = jnp.log(jax.random.uniform(ks[5], (DEPTH, GDN_HEADS), f32, minval=1.0, maxval=16.0))
    dt = jnp.exp(jax.random.uniform(ks[6], (DEPTH, GDN_HEADS), f32,
                                    minval=math.log(1e-3), maxval=math.log(1e-1)))
    dt_bias = dt + jnp.log(-jnp.expm1(-dt))
    gdn_norm_w = 1.0 + 0.02 * jax.random.normal(ks[7], (DEPTH, GDN_DV), f32)
    sinks = jax.random.normal(ks[8], (DEPTH, SWA_HEADS), f32)
    w_proj_a = jax.random.normal(ks[9], (DEPTH, GDN_V_W, D_MODEL), f32) * (GDN_V_W ** -0.5 * DEEPNORM_BETA)
    w_proj_b = jax.random.normal(ks[10], (DEPTH, SWA_Q_W, D_MODEL), f32) * (SWA_Q_W ** -0.5 * DEEPNORM_BETA)
    w_out = jax.random.normal(ks[11], (DEPTH, D_MODEL, D_MODEL), f32) * (D_MODEL ** -0.5 * DEEPNORM_BETA)
    ln_w = 1.0 + 0.02 * jax.random.normal(ks[12], (DEPTH, D_MODEL), f32)
    ln_b = 0.02 * jax.random.normal(ks[13], (DEPTH, D_MODEL), f32)
    return {"x": x, "meta_tokens": meta_tokens, "w_in": w_in, "b_gate": b_gate,
            "conv_w": conv_w, "a_log": a_log, "dt_bias": dt_bias, "gdn_norm_w": gdn_norm_w,
            "sinks": sinks, "w_proj_a": w_proj_a, "w_proj_b": w_proj_b, "w_out": w_out,
            "ln_w": ln_w, "ln_b": ln_b}


def reference(x, meta_tokens, w_in, b_gate, conv_w, a_log, dt_bias, gdn_norm_w, sinks,
              w_proj_a, w_proj_b, w_out, ln_w, ln_b):
    b = x.shape[0]
    meta = jnp.broadcast_to(meta_tokens.astype(x.dtype)[None], (b, N_META, D_MODEL))
    h = jnp.concatenate([meta, x], axis=1)
    for l in range(DEPTH):
        h = hybrid_layer(h, w_in[l], b_gate[l], conv_w[l], a_log[l], dt_bias[l], gdn_norm_w[l],
                         sinks[l], w_proj_a[l], w_proj_b[l], w_out[l], ln_w[l], ln_b[l])
    return h[:, N_META:]
```

```python
import contextlib
import numpy as np
import concourse.bass as bass
import concourse.mybir as mybir
from concourse.bass_utils import run_bass_kernel_spmd

F32 = mybir.dt.float32
BF16 = mybir.dt.bfloat16
AF = mybir.ActivationFunctionType
ALU = mybir.AluOpType
AX = mybir.AxisListType

ENGS = ("pe", "act", "dve", "pool", "sp")


class Op:
    __slots__ = ("eng", "fn", "deps", "sig", "dma_sem", "dma_cnt", "idx", "ndma")

    def __init__(self, eng, fn):
        self.eng = eng
        self.fn = fn
        self.deps = set()
        self.sig = None
        self.dma_sem = None
        self.ndma = 0


class _Dummy:
    def then_inc(self, *a, **k):
        return self


class _Rec:
    def __init__(self):
        self.calls = []

    def __getattr__(self, name):
        def m(*a, **k):
            self.calls.append((name, a, k))
            return _Dummy()
        return m


def _free_size(ap):
    n = 1
    for d in tuple(ap.shape)[1:]:
        n *= int(d)
    return n


def _cost_us(eng, calls):
    t = 0.0
    for (nm, a, k) in calls:
        out = k.get("out", a[0] if a else None)
        fs = _free_size(out) if out is not None else 64
        if eng == "pe":
            rhs = k.get("rhs", None)
            f32 = rhs is not None and rhs.dtype == F32
            t += max(0.075, fs * (0.0024 if f32 else 0.00062))
        elif eng == "act":
            t += 0.15 + 0.00095 * fs
        elif eng == "dve":
            t += 0.10 + 0.0011 * fs
        elif eng == "pool":
            t += 0.35 + 0.0016 * fs
        else:
            t += 0.1
    return t


class Chain(list):
    def op(self, eng, fn, reads=(), writes=()):
        rec = _Rec()
        fn(rec)
        self.append(("op", eng, rec.calls, None, tuple(reads), tuple(writes), _cost_us(eng, rec.calls)))

    def dma(self, eng, fn, stream, reads=(), writes=()):
        rec = _Rec()
        fn(rec, None)
        nbytes = 0
        for (nm, a, k) in rec.calls:
            o = k.get("out")
            nbytes += _free_size(o) * 128 * 4
        self.append(("dma", eng, rec.calls, stream, tuple(reads), tuple(writes), 2.0 + nbytes / 150e3))


class Sched:
    def __init__(self):
        self.eng_free = {e: 0.0 for e in ENGS}
        self.w_fin = {}
        self.r_fin = {}
        self.LAT = 0.2
        self.EPS = 0.4

    def run(self, chains, add):
        chains = [c for c in chains if len(c)]
        pos = [0] * len(chains)
        left = sum(len(c) for c in chains)
        while left:
            cands = []
            for k, c in enumerate(chains):
                if pos[k] >= len(c):
                    continue
                it = c[pos[k]]
                eng, reads, writes = it[1], it[4], it[5]
                rdy = 0.0
                for r in reads:
                    rdy = max(rdy, self.w_fin.get(r, 0.0))
                for r in writes:
                    rdy = max(rdy, self.w_fin.get(r, 0.0), self.r_fin.get(r, 0.0))
                st = max(rdy + self.LAT, self.eng_free[eng])
                fr = (pos[k] + 0.5) / len(c)
                cands.append((st, fr, k))
            mn = min(x[0] for x in cands)
            st, fr, k = min((x for x in cands if x[0] <= mn + self.EPS), key=lambda x: x[1])
            it = chains[k][pos[k]]
            pos[k] += 1
            left -= 1
            eng, reads, writes, dur = it[1], it[4], it[5], it[6]
            if it[0] == "dma":
                self.eng_free[eng] = st + 0.1
                fin = st + dur
            else:
                fin = st + dur
                self.eng_free[eng] = fin
            for r in reads:
                self.r_fin[r] = max(self.r_fin.get(r, 0.0), fin)
            for r in writes:
                self.w_fin[r] = fin
                self.r_fin[r] = 0.0
            add(it)


def interleave(lists):
    lists = [l for l in lists if len(l)]
    out = Chain()
    pos = [0] * len(lists)
    total = sum(len(l) for l in lists)
    for _ in range(total):
        best, bf = None, None
        for k, l in enumerate(lists):
            if pos[k] < len(l):
                fr = (pos[k] + 0.5) / len(l)
                if bf is None or fr < bf:
                    best, bf = k, fr
        out.append(lists[best][pos[best]])
        pos[best] += 1
    return out


class Prog:
    def __init__(self):
        self.ops = []
        self.last_w = {}
        self.readers = {}
        self.dma_streams = {}
        self.fence_deps = set()
        self.fenced = set(ENGS)
        self.last_on = {}
        self.dma_since_fence = []

    def fence(self):
        self.fence_deps = set(self.last_on.values()) | set(self.dma_since_fence)
        self.dma_since_fence = []
        self.fenced = set()

    def _add(self, op, reads, writes):
        idx = len(self.ops)
        op.idx = idx
        for r in reads:
            w = self.last_w.get(r)
            if w is not None:
                op.deps.add(w)
        for r in writes:
            w = self.last_w.get(r)
            if w is not None:
                op.deps.add(w)
            for rd in self.readers.get(r, ()):
                op.deps.add(rd)
        for r in reads:
            self.readers.setdefault(r, []).append(idx)
        for r in writes:
            self.last_w[r] = idx
            self.readers[r] = []
        if op.eng not in self.fenced:
            op.deps |= self.fence_deps
            self.fenced.add(op.eng)
        op.deps.discard(idx)
        self.last_on[op.eng] = idx
        if op.dma_sem is not None:
            self.dma_since_fence.append(idx)
        self.ops.append(op)
        return op

    def op(self, eng, fn, reads=(), writes=()):
        rec = _Rec()
        fn(rec)
        return self._add(Op(eng, rec.calls), reads, writes)

    def add_item(self, it):
        kind, eng, calls, stream, reads, writes = it[:6]
        o = Op(eng, calls)
        if kind == "dma":
            o.dma_sem = stream
            o.ndma = len(calls)
            c = self.dma_streams.get(stream, 0) + o.ndma
            self.dma_streams[stream] = c
            o.dma_cnt = c
        return self._add(o, reads, writes)

    def dma(self, eng, fn, stream, ndma=1, reads=(), writes=()):
        rec = _Rec()
        fn(rec, None)
        o = Op(eng, rec.calls)
        ndma = len(rec.calls)
        o.dma_sem = stream
        o.ndma = ndma
        c = self.dma_streams.get(stream, 0) + ndma
        self.dma_streams[stream] = c
        o.dma_cnt = c
        return self._add(o, reads, writes)

    def emit(self, nc, final_wait_eng="sp", limit=None):
        ops = self.ops if limit is None else self.ops[:limit]
        final_cnt = {}
        for o in ops:
            if o.dma_sem is not None:
                final_cnt[o.dma_sem] = max(final_cnt.get(o.dma_sem, 0), o.dma_cnt)
        need = [False] * len(ops)
        for o in ops:
            for d in o.deps:
                p = ops[d]
                if p.dma_sem is None and p.eng == "pe" and o.eng == "pe" and o.dma_sem is None:
                    continue
                need[d] = True
        cnt = {e: 0 for e in ENGS}
        for o in ops:
            if o.dma_sem is not None:
                o.sig = ("dma:" + o.dma_sem, 16 * o.dma_cnt)
            elif need[o.idx]:
                cnt[o.eng] += 1
                o.sig = (o.eng, cnt[o.eng])
        with contextlib.ExitStack() as st:
            sems = {}
            for e in ENGS:
                sems[e] = st.enter_context(nc.semaphore("s_" + e))
            for s in self.dma_streams:
                sems["dma:" + s] = st.enter_context(nc.semaphore("d_" + s))
            block = st.enter_context(nc.Block())

            def stream(eng_name):
                def body(eng):
                    waited = {}
                    for o in ops:
                        if o.eng != eng_name:
                            continue
                        reqs = {}
                        for d in o.deps:
                            p = ops[d]
                            if p.sig is None:
                                continue
                            if p.dma_sem is None and p.eng == "pe" and eng_name == "pe" and o.dma_sem is None:
                                continue
                            k, v = p.sig
                            if reqs.get(k, 0) < v:
                                reqs[k] = v
                        for k, v in reqs.items():
                            if waited.get(k, 0) < v:
                                eng.wait_ge(sems[k], v)
                                waited[k] = v
                        if o.dma_sem is not None:
                            for (nm, a, k) in o.fn:
                                getattr(eng, nm)(*a, **k).then_inc(sems["dma:" + o.dma_sem], 16)
                        else:
                            ins = None
                            for (nm, a, k) in o.fn:
                                ins = getattr(eng, nm)(*a, **k)
                            if o.sig is not None:
                                ins.then_inc(sems[o.sig[0]], 1)
                    if eng_name == final_wait_eng:
                        for s, c in final_cnt.items():
                            eng.wait_ge(sems["dma:" + s], 16 * c)
                return body

            block.tensor(stream("pe"))
            block.scalar(stream("act"))
            block.vector(stream("dve"))
            block.gpsimd(stream("pool"))
            block.sync(stream("sp"))


D = 1024
SEQ = 2048
NMETA = 16
C = 64
NCH = 33
LP = NCH * C
HOFF = 3
HCOLS = HOFF + LP
XCOL = HOFF + C
MCOL = HOFF + 48
NH_A = 8
DEEPNORM_ALPHA = 2.0 ** 0.25
LN_EPS = 1e-5
RMS_EPS = 1e-6
L2_EPS = 1e-6
NEG = -30000.0


def _blocks(n, b=512):
    out = []
    s = 0
    while s < n:
        out.append((s, min(b, n - s)))
        s += b
    return out


_MARKS = {}


def build_program(debug=False, stop=None):
    nc = bass.Bass("TRN2", target_bir_lowering=False)
    marks = {}

    def din(name, shape):
        return nc.dram_tensor(name, list(shape), F32, kind="ExternalInput").ap()

    xT = din("xT", [D, SEQ])
    xtok = din("xtok", [SEQ, D])
    metaT = din("metaT", [D, NMETA])
    wg = din("wg", [32, 128, 8, 128])
    wbd = din("wbd", [128, 8, 16])
    wkv = din("wkv", [8, 128, 8, 128])
    wqz = din("wqz", [16, 128, 8, 128])
    wgate = din("wgate", [16, 128, 8, 128])
    wpa = din("wpa", [8, 128, 8, 128])
    wpb = din("wpb", [8, 128, 8, 128])
    wout = din("wout", [128, 8, D])
    c64_d = din("c64", [64, 5, 64])
    ident_d = din("ident", [128, 128])
    cw_d = din("cw", [128, 24, 4])
    gnw_d = din("gnw", [128, 1])
    bg_d = din("bg", [128, 16])
    alog_d = din("alog", [64, 8])
    dtb_d = din("dtb", [64, 8])
    sk_d = din("sk", [1, 4, 512])
    lnw_d = din("lnw", [128, D])
    lnb_d = din("lnb", [128, D])
    biasP_d = din("biasP", [128, 4, 512])
    biasO_d = din("biasO", [128, 4, 512])
    lm_d = din("lm", [3, 16, 16])
    rm_d = din("rm", [3, 4, 512])
    out_d = nc.dram_tensor("out", [SEQ, D], F32, kind="ExternalOutput").ap()
    if debug:
        dbg_ya = nc.dram_tensor("dbg_ya", [128, 8, SEQ], F32, kind="ExternalOutput").ap()
        dbg_yb = nc.dram_tensor("dbg_yb", [128, 8, SEQ], F32, kind="ExternalOutput").ap()

    P = Prog()
    with contextlib.ExitStack() as top:
        def sbuf(st, name, shape, dt):
            return st.enter_context(nc.sbuf_tensor("sb_" + name, list(shape), dt))

        ps = [top.enter_context(nc.psum_tensor("ps%d" % i, [128, 512], F32)) for i in range(8)]

        hT = sbuf(top, "hT", [128, 8, HCOLS], BF16)
        yaT = sbuf(top, "yaT", [128, 8, SEQ], BF16)
        identb = sbuf(top, "identb", [128, 128], BF16)
        onesb = sbuf(top, "onesb", [128, 128], BF16)
        c64f = sbuf(top, "c64f", [64, 5, 64], F32)
        c64b = sbuf(top, "c64b", [64, 5, 64], BF16)
        gnw = sbuf(top, "gnw", [128, 1], F32)
        bg = sbuf(top, "bg", [128, 16], F32)

        ld_n = [0]

        def ld(eng, dst, src, stream, reg):
            if stream == "ld_c":
                ld_n[0] += 1
                stream = "ldc%d" % ld_n[0]
            P.dma(eng, lambda e, s: e.dma_start(out=dst, in_=src).then_inc(s, 16), stream, writes=[reg])

        P.op("dve", lambda e: e.memset(hT[:, :, 0:MCOL], 0.0), writes=["hT_z"])
        ld("pool", hT[:, :, MCOL:XCOL], metaT.rearrange("(k p) t -> p k t", p=128), "ld_hm", "hT_m")
        for kc in range(8):
            ld("pool", hT[:, kc, XCOL:XCOL + 512], xT[kc * 128:(kc + 1) * 128, 0:512], "ld_ha%d" % kc, "hTa_%d" % kc)
        for kc in range(8):
            ld("pool", hT[:, kc, XCOL + 512:HCOLS], xT[kc * 128:(kc + 1) * 128, 512:SEQ], "ld_h%d" % kc, "hT_%d" % kc)
        ld("pool", identb[:], ident_d, "ld_c", "identb")
        ld("pool", c64b[:], c64_d, "ld_c", "c64b")
        ld("sp", c64f[:], c64_d, "ld_c", "c64f")
        ld("sp", gnw[:], gnw_d, "ld_c", "gnw")
        ld("sp", bg[:], bg_d, "ld_c", "bg")
        P.op("dve", lambda e: e.memset(onesb[:], 1.0), writes=["onesb"])
        hscr = sbuf(top, "hscr", [128, 8], F32)
        P.op("dve", lambda e: e.memset(hscr[:, 0:4], 0.0), reads=["hT_z", "hT_m"] + ["hTa_%d" % k for k in range(8)], writes=["hTa"])
        P.op("dve", lambda e: e.memset(hscr[:, 4:8], 0.0), reads=["hTa"] + ["hT_%d" % k for k in range(8)], writes=["hT"])

        TRI, SLM, I64, NEGT, NEGD = 0, 1, 2, 3, 4
        marks["0"] = len(P.ops)

        wk0_top = sbuf(top, "wk0", [128, 8, 128], BF16)
        wv0_top = sbuf(top, "wv0", [128, 8, 128], BF16)
        with contextlib.ExitStack() as g:
            NTQ = 576
            NCQ = 9
            cw = sbuf(g, "cw", [128, 24, 4], F32)
            alog = sbuf(g, "alog", [64, 8], F32)
            dtb = sbuf(g, "dtb", [64, 8], F32)
            negA = sbuf(g, "negA", [64, 8], F32)
            wbd_sb = sbuf(g, "wbd_sb", [128, 8, 16], BF16)
            negT8 = sbuf(g, "negT8", [64, 8, 64], BF16)
            negD8 = sbuf(g, "negD8", [64, 8, 64], BF16)
            wts = [sbuf(g, "wts%d" % i, [128, 4, 8, 128], BF16) for i in range(3)]
            pre = [sbuf(g, "pre%d" % i, [128, NTQ + 3], F32) for i in range(3)]
            acc = [sbuf(g, "acc%d" % i, [128, NTQ], F32) for i in range(3)]
            sqb = [sbuf(g, "sq%d" % i, [128, NTQ], BF16) for i in range(2)]
            rnb = [sbuf(g, "rn%d" % i, [128, 512], F32) for i in range(2)]
            qn = [sbuf(g, "qn%d" % i, [128, NTQ], BF16) for i in range(2)]
            kn = [sbuf(g, "kn%d" % i, [128, NTQ], BF16) for i in range(2)]
            vT = [sbuf(g, "vT%d" % i, [128, NTQ], BF16) for i in range(2)]
            zs = [sbuf(g, "zs%d" % i, [128, NTQ], BF16) for i in range(3)]
            qd = [sbuf(g, "qd%d" % i, [128, NTQ], BF16) for i in range(3)]
            ke = [sbuf(g, "ke%d" % i, [64, NCQ, 128], BF16) for i in range(2)]
            vtok = [sbuf(g, "vtok%d" % i, [64, NCQ, 128], BF16) for i in range(2)]
            kd = [sbuf(g, "kd%d" % i, [64, NCQ, 128], BF16) for i in range(3)]
            X0s = [sbuf(g, "X0s%d" % i, [64, NCQ * 64], BF16) for i in range(2)]
            XT0s = [sbuf(g, "XT0s%d" % i, [64, NCQ * 64], BF16) for i in range(2)]
            R0s = [sbuf(g, "R0s%d" % i, [64, NCQ * 64], BF16) for i in range(2)]
            gt = {}
            for nm in ("beta", "nbeta", "gg", "gtmp", "gcs", "egc", "ekd"):
                gt[nm] = [sbuf(g, "%s%d" % (nm, i), [64, NCQ, 8], F32) for i in range(2)]
            gt["glast"] = [sbuf(g, "glast%d" % i, [128, NCQ, 8], F32) for i in range(2)]
            gt["ggh"] = [sbuf(g, "ggh%d" % i, [64, NCQ, 8], BF16) for i in range(2)]
            gt["ggl"] = [sbuf(g, "ggl%d" % i, [64, NCQ, 8], BF16) for i in range(2)]
            GMh = sbuf(g, "GMh", [64, 8, 64], BF16)
            GMl = sbuf(g, "GMl", [64, 8, 64], BF16)
            GM2h = sbuf(g, "GM2h", [64, 8, 64], BF16)
            GM2l = sbuf(g, "GM2l", [64, 8, 64], BF16)
            Bd = sbuf(g, "Bd", [64, 8, 64], BF16)
            Dg = sbuf(g, "Dg", [64, 8, 64], BF16)
            decT = sbuf(g, "decT", [64, 512], F32)
            dec = sbuf(g, "dec", [64, 512], F32)
            t1 = sbuf(g, "t1", [64, 512], F32)
            t2 = sbuf(g, "t2", [64, 512], F32)
            Xb = [sbuf(g, "X%d" % i, [64, 512], BF16) for i in range(2)]
            XTb = [sbuf(g, "XT%d" % i, [64, 512], BF16) for i in range(2)]
            Rb = [sbuf(g, "R%d" % i, [64, 512], BF16) for i in range(2)]
            R5 = sbuf(g, "R5", [64, 512], F32)
            Y = sbuf(g, "Y", [64, NCQ * 64], BF16)
            attnT = [sbuf(g, "attnT%d" % i, [64, NCQ * 64], BF16) for i in range(3)]
            WT = [sbuf(g, "WT%d" % i, [128, NCQ * 64], BF16) for i in range(2)]
            U = [sbuf(g, "U%d" % i, [64, NCQ, 128], F32) for i in range(2)]
            S = sbuf(g, "S", [128, 8, 128], F32)
            Sb = sbuf(g, "Sb", [128, 128], BF16)
            vn = [sbuf(g, "vn%d" % i, [64, 128], BF16) for i in range(2)]
            osb = sbuf(g, "osb", [128, 512], F32)
            osq = sbuf(g, "osq", [128, 512], BF16)
            orn = sbuf(g, "orn", [128, 512], F32)

            ld("sp", cw[:], cw_d, "ld_c", "cw")
            ld("sp", alog[:], alog_d, "ld_c", "alog")
            ld("sp", dtb[:], dtb_d, "ld_c", "dtb")
            ld("pool", wbd_sb[:], wbd, "ld_c", "wbd")
            P.op("act", lambda e: e.activation(out=negA[:], in_=alog[:], func=AF.Exp), reads=["alog"], writes=["negA"])
            P.op("dve", lambda e: e.tensor_scalar(out=negA[:], in0=negA[:], scalar1=-1.0, scalar2=None, op0=ALU.mult),
                 reads=["negA"], writes=["negA"])
            P.op("dve", lambda e: e.tensor_copy(out=negT8[:], in_=c64f[:, NEGT, :].unsqueeze(1).to_broadcast([64, 8, 64])),
                 reads=["c64f"], writes=["negT8"])
            P.op("dve", lambda e: e.tensor_copy(out=negD8[:], in_=c64f[:, NEGD, :].unsqueeze(1).to_broadcast([64, 8, 64])),
                 reads=["c64f"], writes=["negD8"])
            P.op("dve", lambda e: e.memset(S[:], 0.0), writes=["S"])

            QUARTERS = ((0, 9), (9, 8), (17, 8), (25, 8))
            items = [(qi, hh) for qi in range(4) for hh in range(NH_A)]

            def mk_rr(banks):
                st_ = [0]

                def nb_():
                    b = banks[st_[0] % len(banks)]
                    st_[0] += 1
                    return b
                return nb_
            bankA = mk_rr((0, 1))
            bankB = mk_rr((2,))
            bankB2 = mk_rr((3, 4))

            def gates(qi):
                c0, NC = QUARTERS[qi]
                qp = qi % 2
                col0 = HOFF + c0 * C
                ch = Chain()
                beta, nbeta, gg, gtmp, gcs, egc, ekd, glast = (gt[n][qp] for n in
                                                               ("beta", "nbeta", "gg", "gtmp", "gcs", "egc", "ekd", "glast"))
                sfx = str(qp)
                pb = 1
                for c in range(NC):
                    def f(e, c=c):
                        r = None
                        for kc in range(8):
                            r = e.matmul(ps[pb][0:64, c * 16:(c + 1) * 16],
                                         lhsT=hT[:, kc, col0 + c * 64: col0 + (c + 1) * 64],
                                         rhs=wbd_sb[:, kc, :], start=(kc == 0), stop=(kc == 7))
                        return r
                    ch.op("pe", f, reads=["hTa" if qi == 0 else "hT", "wbd"], writes=["ps1"])
                bdv = ps[pb][0:64, 0:NC * 16].rearrange("p (c k) -> p c k", k=16)
                ch.op("act", lambda e: e.activation(out=beta[:, 0:NC, :], in_=bdv[:, :, 0:8], func=AF.Sigmoid),
                      reads=["ps1"], writes=["beta" + sfx])
                ch.op("dve", lambda e: e.tensor_tensor(out=gtmp[:, 0:NC, :], in0=bdv[:, :, 8:16],
                                                       in1=dtb[:].unsqueeze(1).to_broadcast([64, NC, 8]), op=ALU.add),
                      reads=["ps1", "dtb"], writes=["gtmp" + sfx])
                ch.op("act", lambda e: e.activation(out=gtmp[:, 0:NC, :], in_=gtmp[:, 0:NC, :], func=AF.Exp),
                      reads=["gtmp" + sfx], writes=["gtmp" + sfx])
                ch.op("act", lambda e: e.activation(out=gtmp[:, 0:NC, :], in_=gtmp[:, 0:NC, :], func=AF.Ln, bias=1.0, scale=1.0),
                      reads=["gtmp" + sfx], writes=["gtmp" + sfx])
                ch.op("dve", lambda e: e.tensor_tensor(out=gg[:, 0:NC, :], in0=gtmp[:, 0:NC, :],
                                                       in1=negA[:].unsqueeze(1).to_broadcast([64, NC, 8]), op=ALU.mult),
                      reads=["gtmp" + sfx, "negA"], writes=["gg" + sfx])
                ch.op("dve", lambda e: e.tensor_scalar(out=nbeta[:, 0:NC, :], in0=beta[:, 0:NC, :], scalar1=-1.0,
                                                       scalar2=None, op0=ALU.mult),
                      reads=["beta" + sfx], writes=["nbeta" + sfx])
                ggh, ggl = gt["ggh"][qp], gt["ggl"][qp]
                ch.op("dve", lambda e: e.tensor_copy(out=ggh[:, 0:NC, :], in_=gg[:, 0:NC, :]), reads=["gg" + sfx], writes=["ggh" + sfx])
                ch.op("dve", lambda e: e.tensor_tensor(out=ggl[:, 0:NC, :], in0=gg[:, 0:NC, :], in1=ggh[:, 0:NC, :], op=ALU.subtract),
                      reads=["gg" + sfx, "ggh" + sfx], writes=["ggl" + sfx])
                gghf = ggh[:, 0:NC, :].rearrange("p c k -> p (c k)")
                gglf = ggl[:, 0:NC, :].rearrange("p c k -> p (c k)")

                def f(e):
                    e.matmul(ps[1][0:64, 0:NC * 8], lhsT=c64b[:, TRI, :], rhs=gghf, start=True, stop=False)
                    return e.matmul(ps[1][0:64, 0:NC * 8], lhsT=c64b[:, TRI, :], rhs=gglf, start=False, stop=True)
                ch.op("pe", f, reads=["ggh" + sfx, "ggl" + sfx, "c64b"], writes=["ps1"])
                gcv = ps[1][0:64, 0:NC * 8].rearrange("p (c k) -> p c k", k=8)
                ch.op("act", lambda e: e.activation(out=gcs[:, 0:NC, :], in_=gcv, func=AF.Identity),
                      reads=["ps1"], writes=["gcs" + sfx])
                ch.op("act", lambda e: e.activation(out=egc[:, 0:NC, :], in_=gcv, func=AF.Exp),
                      reads=["ps1"], writes=["egc" + sfx])

                def f(e):
                    e.matmul(ps[1][:, 0:NC * 8], lhsT=onesb[0:64, :], rhs=gghf, start=True, stop=False)
                    return e.matmul(ps[1][:, 0:NC * 8], lhsT=onesb[0:64, :], rhs=gglf, start=False, stop=True)
                ch.op("pe", f, reads=["ggh" + sfx, "ggl" + sfx, "onesb"], writes=["ps1"])
                totv = ps[1][:, 0:NC * 8].rearrange("p (c k) -> p c k", k=8)
                ch.op("dve", lambda e: e.tensor_tensor(out=ekd[:, 0:NC, :], in0=totv[0:64], in1=gcs[:, 0:NC, :], op=ALU.subtract),
                      reads=["ps1", "gcs" + sfx], writes=["ekd" + sfx])
                ch.op("act", lambda e: e.activation(out=ekd[:, 0:NC, :], in_=ekd[:, 0:NC, :], func=AF.Exp),
                      reads=["ekd" + sfx], writes=["ekd" + sfx])
                ch.op("act", lambda e: e.activation(out=glast[:, 0:NC, :], in_=totv, func=AF.Exp),
                      reads=["ps1"], writes=["glast" + sfx])
                return ch

            def wload(i):
                qi, hh = items[i]
                k3 = i % 3
                ch = Chain()
                ch.dma("pool", lambda e, s: e.dma_start(out=wts[k3][:], in_=wg[hh * 4:(hh + 1) * 4].rearrange("x p k c -> p x k c")).then_inc(s, 16),
                       "ld_ws%d" % k3, writes=["wts%d" % k3])
                return ch

            for it in wload(0):
                P.add_item(it)

            def stageA(i):
                qi, hh = items[i]
                c0, NC = QUARTERS[qi]
                NT = NC * C
                col0 = HOFF + c0 * C
                a_ = i % 2
                w_ = (i % 3) * 4
                head = Chain()
                if i + 1 < len(items):
                    head.extend(wload(i + 1))
                chains = []
                XB = (0, 1, 0)
                for X in range(3):
                    ch = Chain()
                    wX = wts[i % 3][:, X]
                    prb = pre[X]
                    prn = "pre%d" % X
                    a = acc[X]
                    an = "acc%d" % X
                    ti = X * 8 + hh
                    for (s0, n) in _blocks(NT + 3):
                        b = XB[X]

                        def f(e, b=b, s0=s0, n=n):
                            r = None
                            for kc in range(8):
                                r = e.matmul(ps[b][:, 0:n], lhsT=wX[:, kc, :],
                                             rhs=hT[:, kc, col0 - 3 + s0: col0 - 3 + s0 + n],
                                             start=(kc == 0), stop=(kc == 7))
                            return r
                        ch.op("pe", f, reads=["hTa" if qi == 0 else "hT", "wts%d" % (i % 3)], writes=["ps%d" % b])
                        ch.op("act", lambda e, b=b, s0=s0, n=n: e.activation(out=prb[:, s0:s0 + n], in_=ps[b][:, 0:n], func=AF.Identity),
                              reads=["ps%d" % b], writes=[prn])
                    ch.op("act", lambda e: e.activation(out=a[:, 0:NT], in_=prb[:, 3:3 + NT], func=AF.Identity, scale=cw[:, ti, 3:4]),
                          reads=[prn, "cw"], writes=[an])
                    for j in (2, 1, 0):
                        ch.op("dve", lambda e, j=j: e.scalar_tensor_tensor(
                            out=a[:, 0:NT], in0=prb[:, j:j + NT], scalar=cw[:, ti, j:j + 1], in1=a[:, 0:NT],
                            op0=ALU.mult, op1=ALU.add), reads=[prn, "cw", an], writes=[an])
                    if X == 2:
                        ch.op("act", lambda e: e.activation(out=vT[a_][:, 0:NT], in_=a[:, 0:NT], func=AF.Silu),
                              reads=[an], writes=["vT%d" % a_])
                    else:
                        ch.op("act", lambda e: e.activation(out=a[:, 0:NT], in_=a[:, 0:NT], func=AF.Silu), reads=[an], writes=[an])
                        sq_ = sqb[X]
                        rn_ = rnb[X]
                        ch.op("pool", lambda e: e.tensor_tensor(out=sq_[:, 0:NT], in0=a[:, 0:NT], in1=a[:, 0:NT], op=ALU.mult),
                              reads=[an], writes=["sq%d" % X])
                        for (s0, n) in _blocks(NT):
                            ch.op("pe", lambda e, s0=s0, n=n: e.matmul(ps[XB[X]][:, 0:n], lhsT=onesb[:, :], rhs=sq_[:, s0:s0 + n],
                                                                      start=True, stop=True),
                                  reads=["sq%d" % X, "onesb"], writes=["ps%d" % XB[X]])
                            ch.op("act", lambda e, n=n: e.activation(out=rn_[:, 0:n], in_=ps[XB[X]][:, 0:n], func=AF.Ln, bias=L2_EPS, scale=1.0),
                                  reads=["ps%d" % XB[X]], writes=["rn%d" % X])
                            ch.op("act", lambda e, n=n: e.activation(out=rn_[:, 0:n], in_=rn_[:, 0:n], func=AF.Exp, scale=-0.5),
                                  reads=["rn%d" % X], writes=["rn%d" % X])
                            if X == 0:
                                ch.op("dve", lambda e, s0=s0, n=n: e.scalar_tensor_tensor(
                                    out=qn[a_][:, s0:s0 + n], in0=a[:, s0:s0 + n], scalar=128.0 ** -0.5, in1=rn_[:, 0:n],
                                    op0=ALU.mult, op1=ALU.mult), reads=[an, "rn0"], writes=["qn%d" % a_])
                            else:
                                ch.op("dve", lambda e, s0=s0, n=n: e.tensor_tensor(
                                    out=kn[a_][:, s0:s0 + n], in0=a[:, s0:s0 + n], in1=rn_[:, 0:n], op=ALU.mult),
                                    reads=[an, "rn1"], writes=["kn%d" % a_])
                    chains.append(ch)
                kch = chains[1]
                if hh == 0:
                    kch = Chain(list(chains[1]) + list(gates(qi)))
                return [head, Chain(list(chains[0]) + list(chains[2])), kch]

            def stageB1(i):
                qi, hh = items[i]
                c0, NC = QUARTERS[qi]
                NT = NC * C
                col0 = HOFF + c0 * C
                a_ = i % 2
                t_ = i % 3
                qp = qi % 2
                sfx = str(qp)
                beta, nbeta, gg, egc, ekd = (gt[n][qp] for n in ("beta", "nbeta", "gg", "egc", "ekd"))
                qn_, kn_, vT_, zs_, qd_, kd_ = qn[a_], kn[a_], vT[a_], zs[t_], qd[t_], kd[t_]
                attnT_ = attnT[t_]
                ke_, vtok_ = ke[a_], vtok[a_]
                sa = str(a_)
                st = str(t_)
                ch = Chain()
                wz_ = wts[i % 3][:, 3]
                for (s0, n) in _blocks(NT):
                    b = bankB()

                    def f(e, b=b, s0=s0, n=n):
                        r = None
                        for kc in range(8):
                            r = e.matmul(ps[b][:, 0:n], lhsT=wz_[:, kc, :], rhs=hT[:, kc, col0 + s0: col0 + s0 + n],
                                         start=(kc == 0), stop=(kc == 7))
                        return r
                    ch.op("pe", f, reads=["hTa" if qi == 0 else "hT", "wts%d" % (i % 3)], writes=["ps%d" % b])
                    ch.op("act", lambda e, b=b, s0=s0, n=n: e.activation(out=zs_[:, s0:s0 + n], in_=ps[b][:, 0:n], func=AF.Silu),
                          reads=["ps%d" % b], writes=["zs" + st])
                for cb in range(0, NC, 4):
                    ncb = min(4, NC - cb)
                    b = bankB()

                    def f(e, b=b, cb=cb, ncb=ncb):
                        r = None
                        for j in range(ncb):
                            c = cb + j
                            r = e.matmul(ps[b][0:64, j * 128:(j + 1) * 128], lhsT=kn_[:, c * 64:(c + 1) * 64], rhs=identb[:, :],
                                         start=True, stop=True)
                        return r
                    ch.op("pe", f, reads=["kn" + sa, "identb"], writes=["ps%d" % b])
                    pv = ps[b][0:64, 0:ncb * 128].rearrange("p (c k) -> p c k", k=128)
                    ch.op("dve", lambda e, pv=pv, cb=cb, ncb=ncb: e.tensor_tensor(
                        out=ke_[:, cb:cb + ncb, :], in0=pv, in1=egc[:, cb:cb + ncb, hh:hh + 1].to_broadcast([64, ncb, 128]),
                        op=ALU.mult), reads=["ps%d" % b, "egc" + sfx], writes=["ke" + sa])
                    ch.op("dve", lambda e, pv=pv, cb=cb, ncb=ncb: e.tensor_tensor(
                        out=kd_[:, cb:cb + ncb, :], in0=pv, in1=ekd[:, cb:cb + ncb, hh:hh + 1].to_broadcast([64, ncb, 128]),
                        op=ALU.mult), reads=["ps%d" % b, "ekd" + sfx], writes=["kd" + st])
                    b = bankB()

                    def f(e, b=b, cb=cb, ncb=ncb):
                        r = None
                        for j in range(ncb):
                            c = cb + j
                            r = e.matmul(ps[b][0:64, j * 128:(j + 1) * 128], lhsT=vT_[:, c * 64:(c + 1) * 64], rhs=identb[:, :],
                                         start=True, stop=True)
                        return r
                    ch.op("pe", f, reads=["vT" + sa, "identb"], writes=["ps%d" % b])
                    pv2 = ps[b][0:64, 0:ncb * 128].rearrange("p (c k) -> p c k", k=128)
                    ch.op("act", lambda e, pv2=pv2, cb=cb, ncb=ncb: e.activation(out=vtok_[:, cb:cb + ncb, :], in_=pv2, func=AF.Identity),
                          reads=["ps%d" % b], writes=["vtok" + sa])
                for cb in range(0, NC, 8):
                    nb = min(8, NC - cb)
                    W = nb * 64
                    ggh, ggl = gt["ggh"][qp], gt["ggl"][qp]
                    for (dst, dn, tab, src, sn_) in ((GMh, "GMh", TRI, ggh, "ggh"), (GMl, "GMl", TRI, ggl, "ggl")):
                        ch.op("pool", lambda e, dst=dst, tab=tab, src=src: e.tensor_tensor(
                            out=dst[:, 0:nb, :], in0=c64f[:, tab, :].unsqueeze(1).to_broadcast([64, nb, 64]),
                            in1=src[:, cb:cb + nb, hh:hh + 1].to_broadcast([64, nb, 64]), op=ALU.mult),
                            reads=["c64f", sn_ + sfx], writes=[dn])
                    ch.op("pool", lambda e, nb=nb, cb=cb: e.tensor_tensor(
                        out=Bd[:, 0:nb, :], in0=c64f[:, I64, :].unsqueeze(1).to_broadcast([64, nb, 64]),
                        in1=beta[:, cb:cb + nb, hh:hh + 1].to_broadcast([64, nb, 64]), op=ALU.mult),
                        reads=["c64f", "beta" + sfx], writes=["Bd"])
                    ch.op("pool", lambda e, nb=nb, cb=cb: e.tensor_tensor(
                        out=Dg[:, 0:nb, :], in0=c64f[:, I64, :].unsqueeze(1).to_broadcast([64, nb, 64]),
                        in1=egc[:, cb:cb + nb, hh:hh + 1].to_broadcast([64, nb, 64]), op=ALU.mult),
                        reads=["c64f", "egc" + sfx], writes=["Dg"])
                    GMhf = GMh[:, 0:nb, :].rearrange("p c k -> p (c k)")
                    GMlf = GMl[:, 0:nb, :].rearrange("p c k -> p (c k)")
                    Bdf = Bd[:, 0:nb, :].rearrange("p c k -> p (c k)")
                    Dgf = Dg[:, 0:nb, :].rearrange("p c k -> p (c k)")
                    nT8 = negT8[:, 0:nb, :].rearrange("p c k -> p (c k)")
                    bDT = bankB()

                    def f(e, b=bDT, W=W, nT8=nT8):
                        e.matmul(ps[b][0:64, 0:W], lhsT=c64b[:, SLM, :], rhs=GMhf, start=True, stop=False)
                        e.matmul(ps[b][0:64, 0:W], lhsT=c64b[:, SLM, :], rhs=GMlf, start=False, stop=False)
                        return e.matmul(ps[b][0:64, 0:W], lhsT=c64b[:, I64, :], rhs=nT8, start=False, stop=True)
                    ch.op("pe", f, reads=["c64b", "GMh", "GMl", "negT8"], writes=["ps%d" % bDT])
                    ch.op("act", lambda e, b=bDT, W=W: e.activation(out=decT[:, 0:W], in_=ps[b][0:64, 0:W], func=AF.Exp),
                          reads=["ps%d" % bDT], writes=["decT"])
                    bKK = bankB()

                    def f(e, b=bKK, cb=cb, nb=nb):
                        r = None
                        for j in range(nb):
                            c = cb + j
                            r = e.matmul(ps[b][0:64, j * 64:(j + 1) * 64], lhsT=kn_[:, c * 64:(c + 1) * 64],
                                         rhs=kn_[:, c * 64:(c + 1) * 64], start=True, stop=True)
                        return r
                    ch.op("pe", f, reads=["kn" + sa], writes=["ps%d" % bKK])
                    ch.op("dve", lambda e, b=bKK, W=W: e.tensor_tensor(out=t1[:, 0:W], in0=ps[b][0:64, 0:W], in1=decT[:, 0:W], op=ALU.mult),
                          reads=["ps%d" % bKK, "decT"], writes=["t1"])
                    bBR = bankB()
                    ch.op("pe", lambda e, b=bBR, W=W, Bdf=Bdf: e.matmul(ps[b][0:64, 0:W], lhsT=c64b[:, SLM, :], rhs=Bdf, start=True, stop=True),
                          reads=["c64b", "Bd"], writes=["ps%d" % bBR])
                    X0, XT0, R0 = X0s[a_], XT0s[a_], R0s[a_]
                    o0 = cb * 64
                    ch.op("dve", lambda e, b=bBR, W=W: e.scalar_tensor_tensor(
                        out=X0[:, o0:o0 + W], in0=t1[:, 0:W], scalar=-1.0, in1=ps[b][0:64, 0:W], op0=ALU.mult, op1=ALU.mult),
                        reads=["t1", "ps%d" % bBR], writes=["X0s" + sa])
                    bXT = bankB()

                    def f(e, b=bXT, nb=nb, o0=o0):
                        r = None
                        for j in range(nb):
                            r = e.matmul(ps[b][0:64, j * 64:(j + 1) * 64], lhsT=X0[:, o0 + j * 64: o0 + (j + 1) * 64],
                                         rhs=c64b[:, I64, :], start=True, stop=True)
                        return r
                    ch.op("pe", f, reads=["X0s" + sa, "c64b"], writes=["ps%d" % bXT])
                    ch.op("act", lambda e, b=bXT, W=W, o0=o0: e.activation(out=XT0[:, o0:o0 + W], in_=ps[b][0:64, 0:W], func=AF.Identity),
                          reads=["ps%d" % bXT], writes=["XT0s" + sa])
                    ch.op("pool", lambda e, nb=nb: e.tensor_tensor(
                        out=R0[:, o0:o0 + nb * 64].rearrange("p (c k) -> p c k", k=64),
                        in0=X0[:, o0:o0 + nb * 64].rearrange("p (c k) -> p c k", k=64),
                        in1=c64f[:, I64, :].unsqueeze(1).to_broadcast([64, nb, 64]), op=ALU.add),
                        reads=["X0s" + sa, "c64f"], writes=["R0s" + sa])
                    bQK = bankB()

                    def f(e, b=bQK, cb=cb, nb=nb):
                        r = None
                        for j in range(nb):
                            c = cb + j
                            r = e.matmul(ps[b][0:64, j * 64:(j + 1) * 64], lhsT=kn_[:, c * 64:(c + 1) * 64],
                                         rhs=qn_[:, c * 64:(c + 1) * 64], start=True, stop=True)
                        return r
                    ch.op("pe", f, reads=["kn" + sa, "qn" + sa], writes=["ps%d" % bQK])
                    ch.op("dve", lambda e, b=bQK, W=W, cb=cb: e.tensor_tensor(
                        out=attnT_[:, cb * 64: cb * 64 + W], in0=ps[b][0:64, 0:W], in1=decT[:, 0:W], op=ALU.mult),
                        reads=["ps%d" % bQK, "decT"], writes=["attnT" + st])
                    bEG = bankB()
                    ch.op("pe", lambda e, b=bEG, W=W, Dgf=Dgf: e.matmul(ps[b][:, 0:W], lhsT=onesb[0:64, :], rhs=Dgf, start=True, stop=True),
                          reads=["onesb", "Dg"], writes=["ps%d" % bEG])
                    ch.op("dve", lambda e, b=bEG, W=W, cb=cb: e.tensor_tensor(
                        out=qd_[:, cb * 64: cb * 64 + W], in0=ps[b][:, 0:W], in1=qn_[:, cb * 64: cb * 64 + W], op=ALU.mult),
                        reads=["ps%d" % bEG, "qn" + sa], writes=["qd" + st])
                return ch

            def stageB2(i):
                qi, hh = items[i]
                c0, NC = QUARTERS[qi]
                a_ = i % 2
                qp = qi % 2
                sfx = str(qp)
                beta = gt["beta"][qp]
                ke_, vtok_ = ke[a_], vtok[a_]
                WT_, U_ = WT[a_], U[a_]
                sa = str(a_)
                ch = Chain()
                for cb in range(0, NC, 8):
                    nb = min(8, NC - cb)
                    W = nb * 64
                    o0 = cb * 64
                    cur = 0
                    for lvl in range(1, 6):
                        if lvl == 1:
                            Xp, XTp, Rp = X0s[a_][:, o0:o0 + W], XT0s[a_][:, o0:o0 + W], R0s[a_][:, o0:o0 + W]
                            rXp, rXTp, rRp = "X0s" + sa, "XT0s" + sa, "R0s" + sa
                        else:
                            Xp, XTp, Rp = Xb[cur], XTb[cur], Rb[cur]
                            rXp, rXTp, rRp = "X%d" % cur, "XT%d" % cur, "R%d" % cur
                        Xn, XTn, Rn = Xb[1 - cur], XTb[1 - cur], Rb[1 - cur]
                        nn = str(1 - cur)
                        if lvl <= 4:
                            b = bankB2()

                            def f(e, b=b, nb=nb, Xp=Xp, XTp=XTp):
                                r = None
                                for j in range(nb):
                                    r = e.matmul(ps[b][0:64, j * 64:(j + 1) * 64], lhsT=XTp[:, j * 64:(j + 1) * 64],
                                                 rhs=Xp[:, j * 64:(j + 1) * 64], start=True, stop=True)
                                return r
                            ch.op("pe", f, reads=[rXp, rXTp], writes=["ps%d" % b])
                            ch.op("act", lambda e, b=b, W=W, Xn=Xn: e.activation(out=Xn[:, 0:W], in_=ps[b][0:64, 0:W], func=AF.Identity),
                                  reads=["ps%d" % b], writes=["X" + nn])
                        b = bankB2()

                        def f(e, b=b, nb=nb, Xp=Xp, XTp=XTp):
                            r = None
                            for j in range(nb):
                                r = e.matmul(ps[b][0:64, j * 64:(j + 1) * 64], lhsT=Xp[:, j * 64:(j + 1) * 64],
                                             rhs=XTp[:, j * 64:(j + 1) * 64], start=True, stop=True)
                            return r
                        ch.op("pe", f, reads=[rXp, rXTp], writes=["ps%d" % b])
                        ch.op("act", lambda e, b=b, W=W, XTn=XTn: e.activation(out=XTn[:, 0:W], in_=ps[b][0:64, 0:W], func=AF.Identity),
                              reads=["ps%d" % b], writes=["XT" + nn])
                        b = bankB2()

                        def f(e, b=b, nb=nb, XTn=XTn, Rp=Rp):
                            r = None
                            for j in range(nb):
                                r = e.matmul(ps[b][0:64, j * 64:(j + 1) * 64], lhsT=XTn[:, j * 64:(j + 1) * 64],
                                             rhs=Rp[:, j * 64:(j + 1) * 64], start=True, stop=True)
                            return r
                        ch.op("pe", f, reads=["XT" + nn, rRp], writes=["ps%d" % b])
                        if lvl < 5:
                            ch.op("dve", lambda e, b=b, W=W, Rn=Rn, Rp=Rp: e.tensor_tensor(
                                out=Rn[:, 0:W], in0=ps[b][0:64, 0:W], in1=Rp[:, 0:W], op=ALU.add),
                                reads=["ps%d" % b, rRp], writes=["R" + nn])
                        else:
                            ch.op("dve", lambda e, b=b, W=W, Rp=Rp: e.tensor_tensor(
                                out=R5[:, 0:W], in0=ps[b][0:64, 0:W], in1=Rp[:, 0:W], op=ALU.add),
                                reads=["ps%d" % b, rRp], writes=["R5"])
                            ch.op("dve", lambda e, nb=nb, cb=cb: e.tensor_tensor(
                                out=Y[:, cb * 64:(cb + nb) * 64].rearrange("p (c k) -> p c k", k=64),
                                in0=R5[:, 0:nb * 64].rearrange("p (c k) -> p c k", k=64),
                                in1=beta[:, cb:cb + nb, hh:hh + 1].to_broadcast([64, nb, 64]), op=ALU.mult),
                                reads=["R5", "beta" + sfx], writes=["Y"])
                        cur = 1 - cur
                    b = bankB2()

                    def f(e, b=b, cb=cb, nb=nb):
                        r = None
                        for j in range(nb):
                            c = cb + j
                            r = e.matmul(ps[b][:, j * 64:(j + 1) * 64], lhsT=ke_[:, c, :], rhs=Y[:, c * 64:(c + 1) * 64],
                                         start=True, stop=True)
                        return r
                    ch.op("pe", f, reads=["ke" + sa, "Y"], writes=["ps%d" % b])
                    ch.op("act", lambda e, b=b, W=W, cb=cb: e.activation(out=WT_[:, cb * 64: cb * 64 + W], in_=ps[b][:, 0:W], func=AF.Identity),
                          reads=["ps%d" % b], writes=["WT" + sa])
                    for c4 in range(0, nb, 4):
                        n4 = min(4, nb - c4)
                        b = bankB2()

                        def f(e, b=b, cb=cb, c4=c4, n4=n4):
                            r = None
                            for j in range(n4):
                                c = cb + c4 + j
                                r = e.matmul(ps[b][0:64, j * 128:(j + 1) * 128], lhsT=Y[:, c * 64:(c + 1) * 64], rhs=vtok_[:, c, :],
                                             start=True, stop=True)
                            return r
                        ch.op("pe", f, reads=["Y", "vtok" + sa], writes=["ps%d" % b])
                        ch.op("act", lambda e, b=b, cb=cb, c4=c4, n4=n4: e.activation(
                            out=U_[:, cb + c4: cb + c4 + n4, :], in_=ps[b][0:64, 0:n4 * 128].rearrange("p (c k) -> p c k", k=128),
                            func=AF.Identity), reads=["ps%d" % b], writes=["U" + sa])
                return ch

            def stageC(i):
                qi, hh = items[i]
                c0, NC = QUARTERS[qi]
                a_ = i % 2
                qp = qi % 2
                sa = str(a_)
                glast = gt["glast"][qp]
                t_ = i % 3
                st = str(t_)
                zs_, qd_, kd_, attnT_, WT_, U_ = zs[t_], qd[t_], kd[t_], attnT[t_], WT[a_], U[a_]
                ch = Chain()
                Sh = S[:, hh, :]
                Sn = "S%d" % hh
                ch.op("act", lambda e: e.activation(out=Sb[:, :], in_=Sh, func=AF.Identity), reads=["S", Sn], writes=["Sb"])
                for c in range(NC):
                    cg = c0 + c
                    v = vn[c % 2]
                    vname = "vn%d" % (c % 2)
                    ch.op("pe", lambda e, c=c: e.matmul(ps[6][0:64, 0:128], lhsT=WT_[:, c * 64:(c + 1) * 64], rhs=Sb[:, :], start=True, stop=True),
                          reads=["WT" + sa, "Sb"], writes=["ps6"])
                    ch.op("dve", lambda e, c=c, v=v: e.tensor_tensor(out=v[:, :], in0=U_[:, c, :], in1=ps[6][0:64, 0:128], op=ALU.subtract),
                          reads=["U" + sa, "ps6"], writes=[vname])
                    if cg >= 1:
                        oc = (cg - 1) % 8

                        def f(e, c=c, v=v, oc=oc):
                            e.matmul(ps[7][:, oc * 64:(oc + 1) * 64], lhsT=Sb[:, :], rhs=qd_[:, c * 64:(c + 1) * 64], start=True, stop=False)
                            return e.matmul(ps[7][:, oc * 64:(oc + 1) * 64], lhsT=v[:, :], rhs=attnT_[:, c * 64:(c + 1) * 64],
                                            start=False, stop=True)
                        ch.op("pe", f, reads=["Sb", "qd" + st, vname, "attnT" + st], writes=["ps7"])
                    ch.op("pe", lambda e, c=c, v=v: e.matmul(ps[5][:, 0:128], lhsT=kd_[:, c, :], rhs=v[:, :], start=True, stop=True),
                          reads=["kd" + st, vname], writes=["ps5"])
                    ch.op("dve", lambda e, c=c: e.scalar_tensor_tensor(
                        out=Sb[:, :], in0=Sh, scalar=glast[:, c, hh:hh + 1], in1=ps[5][:, 0:128], op0=ALU.mult, op1=ALU.add),
                        reads=["S", Sn, "glast%d" % qp, "ps5"], writes=["Sb"])
                    ch.op("dve", lambda e, c=c: e.scalar_tensor_tensor(
                        out=Sh, in0=Sh, scalar=glast[:, c, hh:hh + 1], in1=ps[5][:, 0:128], op0=ALU.mult, op1=ALU.add),
                        reads=["S", Sn, "glast%d" % qp, "ps5"], writes=[Sn])
                    if cg >= 1 and ((cg - 1) % 8 == 7 or c == NC - 1):
                        ng = (cg - 1) % 8 + 1
                        Wg = ng * 64
                        r0 = (cg - ng) * 64
                        z0 = (c - ng + 1) * 64
                        ch.op("act", lambda e, Wg=Wg: e.activation(out=osb[:, 0:Wg], in_=ps[7][:, 0:Wg], func=AF.Identity),
                              reads=["ps7"], writes=["osb"])
                        ch.op("pool", lambda e, Wg=Wg: e.tensor_tensor(out=osq[:, 0:Wg], in0=osb[:, 0:Wg], in1=osb[:, 0:Wg], op=ALU.mult),
                              reads=["osb"], writes=["osq"])
                        ch.op("pe", lambda e, Wg=Wg: e.matmul(ps[6][:, 0:Wg], lhsT=onesb[:, :], rhs=osq[:, 0:Wg], start=True, stop=True),
                              reads=["osq", "onesb"], writes=["ps6"])
                        ch.op("act", lambda e, Wg=Wg: e.activation(out=orn[:, 0:Wg], in_=ps[6][:, 0:Wg], func=AF.Ln,
                                                                   bias=RMS_EPS, scale=1.0 / 128.0),
                              reads=["ps6"], writes=["orn"])
                        ch.op("act", lambda e, Wg=Wg: e.activation(out=orn[:, 0:Wg], in_=orn[:, 0:Wg], func=AF.Exp, scale=-0.5),
                              reads=["orn"], writes=["orn"])
                        ch.op("dve", lambda e, Wg=Wg: e.scalar_tensor_tensor(
                            out=osb[:, 0:Wg], in0=osb[:, 0:Wg], scalar=gnw[:, 0:1], in1=orn[:, 0:Wg], op0=ALU.mult, op1=ALU.mult),
                            reads=["osb", "gnw", "orn"], writes=["osb"])
                        ch.op("dve", lambda e, Wg=Wg, r0=r0, z0=z0: e.tensor_tensor(
                            out=yaT[:, hh, r0:r0 + Wg], in0=osb[:, 0:Wg], in1=zs_[:, z0:z0 + Wg], op=ALU.mult),
                            reads=["osb", "zs" + st], writes=["yaT"])
                return ch

            nI = len(items)
            sched = Sched()
            for step in range(nI + 3):
                lists = []
                if step < nI:
                    la = stageA(step)
                    sched.run([la[0]], P.add_item)
                    lists.extend(la[1:])
                if 0 <= step - 1 < nI:
                    lists.append(stageB1(step - 1))
                if 0 <= step - 2 < nI:
                    lists.append(stageB2(step - 2))
                if 0 <= step - 3 < nI:
                    lists.append(stageC(step - 3))
                sched.run(lists, P.add_item)
            ld("pool", wk0_top[:], wkv[0], "ld_wk0", "wk0")
            ld("pool", wv0_top[:], wkv[1], "ld_wv0", "wv0")
        marks["G"] = len(P.ops)
        P.fence()
        ybT = sbuf(top, "ybT", [128, 8, SEQ], BF16)
        wf0_top = [sbuf(top, "wf%d" % i, [128, 8, 128], BF16) for i in range(4)]

        with contextlib.ExitStack() as s_:
            biasP = sbuf(s_, "biasP", [128, 4, 512], BF16)
            biasO = sbuf(s_, "biasO", [128, 4, 512], BF16)
            lm = sbuf(s_, "lm", [3, 16, 16], BF16)
            rm = sbuf(s_, "rm", [3, 4, 512], BF16)
            skf = sbuf(s_, "skf", [33, 4, 512], F32)
            esk = sbuf(s_, "esk", [33, 4, 512], BF16)
            wk_ = [wk0_top, sbuf(s_, "wk1", [128, 8, 128], BF16)]
            wv_ = [wv0_top, sbuf(s_, "wv1", [128, 8, 128], BF16)]
            wq2 = [sbuf(s_, "wq2_%d" % i, [128, 8, 128], BF16) for i in range(4)]
            wz2 = [sbuf(s_, "wz2_%d" % i, [128, 8, 128], BF16) for i in range(4)]
            kTr = sbuf(s_, "kTr", [128, SEQ], BF16)
            kTm = sbuf(s_, "kTm", [128, 16], BF16)
            vtk = sbuf(s_, "vtk", [128, 16, 128], BF16)
            vmt = sbuf(s_, "vmt", [33, 128], BF16)
            qTh = [sbuf(s_, "qTh%d" % i, [128, 2, SEQ], BF16) for i in range(2)]
            zsb = sbuf(s_, "zsb", [128, 2, SEQ], BF16)
            PTp = [sbuf(s_, "PTp%d" % i, [128, 512], BF16) for i in range(2)]
            PTo = [sbuf(s_, "PTo%d" % i, [128, 512], BF16) for i in range(2)]
            PTm = [sbuf(s_, "PTm%d" % i, [33, 512], BF16) for i in range(2)]
            rden = sbuf(s_, "rden", [128, 512], F32)
            tmpo = sbuf(s_, "tmpo", [128, 2, 128], F32)

            ld("pool", biasP[:], biasP_d, "ld_c", "biasP")
            ld("pool", biasO[:], biasO_d, "ld_c", "biasO")
            ld("pool", lm[:], lm_d, "ld_c", "lm")
            ld("pool", rm[:], rm_d, "ld_c", "rm")
            ld("sp", skf[32:33, :, :], sk_d, "ld_c", "skf")
            P.op("act", lambda e: e.activation(out=esk[32:33, :, :], in_=skf[32:33, :, :], func=AF.Exp), reads=["skf"], writes=["esk"])
            P.op("dve", lambda e: e.memset(vmt[:, :], 0.0), writes=["vmt"])
            for pp_ in range(2):
                P.op("dve", lambda e, pp_=pp_: e.memset(PTm[pp_][:, :], 0.0), writes=["PTm%d" % pp_])
            P.op("pool", lambda e: e.memset(qTh[0][64:128, :, :], 0.0), writes=["qTz0"])
            P.op("pool", lambda e: e.memset(qTh[1][0:64, :, :], 0.0), writes=["qTz1"])

            def swa_wload(kh):
                w = kh % 2
                if kh > 0:
                    ld("pool", wk_[w][:], wkv[kh * 2], "ld_wk%d" % w, "wk%d" % w)
                    ld("pool", wv_[w][:], wkv[kh * 2 + 1], "ld_wv%d" % w, "wv%d" % w)
                for qt in range(2):
                    ld("pool", wq2[w * 2 + qt][:], wqz[kh * 2 + qt], "ld_wq%d" % (w * 2 + qt), "wq%d" % (w * 2 + qt))
                    ld("pool", wz2[w * 2 + qt][:], wqz[8 + kh * 2 + qt], "ld_wz%d" % (w * 2 + qt), "wz%d" % (w * 2 + qt))

            swa_wload(0)
            for kh in range(4):
                w = kh % 2
                for blk in range(4):
                    b = blk % 2

                    def f(e, b=b, blk=blk, w=w):
                        r = None
                        for kc in range(8):
                            r = e.matmul(ps[b][:, 0:512], lhsT=wk_[w][:, kc, :], rhs=hT[:, kc, XCOL + blk * 512: XCOL + (blk + 1) * 512],
                                         start=(kc == 0), stop=(kc == 7))
                        return r
                    P.op("pe", f, reads=["hT", "wk%d" % w], writes=["ps%d" % b])
                    P.op("act", lambda e, b=b, blk=blk: e.activation(out=kTr[:, blk * 512:(blk + 1) * 512], in_=ps[b][:, 0:512], func=AF.Identity),
                         reads=["ps%d" % b], writes=["kTr"])

                def f(e, w=w):
                    r = None
                    for kc in range(8):
                        r = e.matmul(ps[0][:, 0:16], lhsT=wk_[w][:, kc, :], rhs=hT[:, kc, MCOL:XCOL], start=(kc == 0), stop=(kc == 7))
                    return r
                P.op("pe", f, reads=["hT", "wk%d" % w], writes=["ps0"])
                P.op("act", lambda e: e.activation(out=kTm[:, :], in_=ps[0][:, 0:16], func=AF.Identity), reads=["ps0"], writes=["kTm"])
                for m4 in range(4):
                    b = m4 % 2

                    def f(e, b=b, m4=m4, w=w):
                        r = None
                        for i in range(4):
                            m = m4 * 4 + i
                            for kc in range(8):
                                r = e.matmul(ps[b][:, i * 128:(i + 1) * 128], lhsT=hT[:, kc, XCOL + m * 128: XCOL + (m + 1) * 128],
                                             rhs=wv_[w][:, kc, :], start=(kc == 0), stop=(kc == 7))
                        return r
                    P.op("pe", f, reads=["hT", "wv%d" % w], writes=["ps%d" % b])
                    P.op("act", lambda e, b=b, m4=m4: e.activation(out=vtk[:, m4 * 4:(m4 + 1) * 4, :],
                                                                   in_=ps[b][:, 0:512].rearrange("p (c k) -> p c k", k=128), func=AF.Identity),
                         reads=["ps%d" % b], writes=["vtk"])

                def f(e, w=w):
                    r = None
                    for kc in range(8):
                        r = e.matmul(ps[1][0:16, 0:128], lhsT=hT[:, kc, MCOL:XCOL], rhs=wv_[w][:, kc, :], start=(kc == 0), stop=(kc == 7))
                    return r
                P.op("pe", f, reads=["hT", "wv%d" % w], writes=["ps1"])
                P.op("act", lambda e: e.activation(out=vmt[0:16, :], in_=ps[1][0:16, 0:128], func=AF.Identity), reads=["ps1"], writes=["vmt"])
                for qt in range(2):
                    for blk in range(4):
                        b = blk % 2

                        def f(e, b=b, blk=blk, wi=w * 2 + qt):
                            r = None
                            for kc in range(8):
                                r = e.matmul(ps[b][:, 0:512], lhsT=wq2[wi][:, kc, :], rhs=hT[:, kc, XCOL + blk * 512: XCOL + (blk + 1) * 512],
                                             start=(kc == 0), stop=(kc == 7))
                            return r
                        P.op("pe", f, reads=["hT", "wq%d" % (w * 2 + qt)], writes=["ps%d" % b])
                        P.op("act", lambda e, b=b, blk=blk, qt=qt: e.activation(out=qTh[0][0:64, qt, blk * 512:(blk + 1) * 512],
                                                                              in_=ps[b][0:64, 0:512], func=AF.Identity),
                             reads=["ps%d" % b], writes=["qT"])
                        P.op("dve", lambda e, b=b, blk=blk, qt=qt: e.tensor_copy(out=qTh[1][64:128, qt, blk * 512:(blk + 1) * 512],
                                                                               in_=ps[b][64:128, 0:512]),
                             reads=["ps%d" % b], writes=["qTb"])
                    for blk in range(4):
                        b = blk % 2

                        def f(e, b=b, blk=blk, wi=w * 2 + qt):
                            r = None
                            for kc in range(8):
                                r = e.matmul(ps[b][:, 0:512], lhsT=wz2[wi][:, kc, :], rhs=hT[:, kc, XCOL + blk * 512: XCOL + (blk + 1) * 512],
                                             start=(kc == 0), stop=(kc == 7))
                            return r
                        P.op("pe", f, reads=["hT", "wz%d" % (w * 2 + qt)], writes=["ps%d" % b])
                        P.op("act", lambda e, b=b, blk=blk, qt=qt: e.activation(out=zsb[:, qt, blk * 512:(blk + 1) * 512], in_=ps[b][:, 0:512],
                                                                              func=AF.Silu),
                             reads=["ps%d" % b], writes=["zsb"])
                for pp_ in range(2):
                    P.op("act", lambda e, pp_=pp_: e.activation(out=PTm[pp_][32:33, :], in_=esk[32:33, kh, :], func=AF.Identity),
                         reads=["esk"], writes=["PTm%d" % pp_])

                def stage_s(m):
                    pp = m % 2
                    bP, bO, bM = (0, 2, 4) if pp == 0 else (1, 3, 7)

                    def scores(e, bank, keys, nkeys):
                        r = None
                        for qt in range(2):
                            for half in range(2):
                                r = e.matmul(ps[bank][0:nkeys, (2 * qt + half) * 128:(2 * qt + half + 1) * 128],
                                             lhsT=keys, rhs=qTh[half][:, qt, m * 128:(m + 1) * 128],
                                             start=(qt == 0 and half == 0), stop=False)
                        return r
                    qreads = ["qT", "qTb", "qTz0", "qTz1"]
                    if m >= 1:
                        def f(e):
                            scores(e, bP, kTr[:, (m - 1) * 128: m * 128], 128)
                            return e.matmul(ps[bP][:, 0:512], lhsT=identb[:, :], rhs=biasP[:, kh, :], start=False, stop=True)
                        P.op("pe", f, reads=["kTr", "identb", "biasP"] + qreads, writes=["ps%d" % bP])
                        P.op("act", lambda e: e.activation(out=PTp[pp][:, :], in_=ps[bP][:, 0:512], func=AF.Exp, scale=0.125),
                             reads=["ps%d" % bP], writes=["PTp%d" % pp])

                    def f(e):
                        scores(e, bO, kTr[:, m * 128:(m + 1) * 128], 128)
                        return e.matmul(ps[bO][:, 0:512], lhsT=identb[:, :], rhs=biasO[:, kh, :], start=False, stop=True)
                    P.op("pe", f, reads=["kTr", "identb", "biasO"] + qreads, writes=["ps%d" % bO])
                    P.op("act", lambda e: e.activation(out=PTo[pp][:, :], in_=ps[bO][:, 0:512], func=AF.Exp, scale=0.125),
                         reads=["ps%d" % bO], writes=["PTo%d" % pp])

                    def f(e):
                        scores(e, bM, kTm[:, 0:16], 16)
                        return e.matmul(ps[bM][0:16, 0:512], lhsT=lm[0:3, m, :], rhs=rm[0:3, kh, :], start=False, stop=True)
                    P.op("pe", f, reads=["kTm", "lm", "rm"] + qreads, writes=["ps%d" % bM])
                    P.op("act", lambda e: e.activation(out=PTm[pp][0:16, :], in_=ps[bM][0:16, 0:512], func=AF.Exp, scale=0.125),
                         reads=["ps%d" % bM], writes=["PTm%d" % pp])

                def stage_r(m):
                    pp = m % 2

                    def f(e):
                        first = True
                        if m >= 1:
                            e.matmul(ps[5][:, 0:512], lhsT=vtk[:, m - 1, :], rhs=PTp[pp][:, :], start=True, stop=False)
                            first = False
                        e.matmul(ps[5][:, 0:512], lhsT=vtk[:, m, :], rhs=PTo[pp][:, :], start=first, stop=False)
                        e.matmul(ps[5][:, 0:512], lhsT=vmt[0:33, :], rhs=PTm[pp][0:33, :], start=False, stop=True)
                        first = True
                        if m >= 1:
                            e.matmul(ps[6][:, 0:512], lhsT=onesb[:, :], rhs=PTp[pp][:, :], start=True, stop=False)
                            first = False
                        e.matmul(ps[6][:, 0:512], lhsT=onesb[:, :], rhs=PTo[pp][:, :], start=first, stop=False)
                        return e.matmul(ps[6][:, 0:512], lhsT=onesb[0:33, :], rhs=PTm[pp][0:33, :], start=False, stop=True)
                    P.op("pe", f, reads=["vtk", "vmt", "PTp%d" % pp, "PTo%d" % pp, "PTm%d" % pp, "onesb"], writes=["ps5", "ps6"])
                    P.op("act", lambda e: e.activation(out=rden[:, :], in_=ps[6][:, 0:512], func=AF.Ln), reads=["ps6"], writes=["rden"])
                    P.op("act", lambda e: e.activation(out=rden[:, :], in_=rden[:, :], func=AF.Exp, scale=-1.0), reads=["rden"], writes=["rden"])
                    for half in range(2):
                        rows = slice(half * 64, (half + 1) * 64)
                        cs = slice(half * 128, (half + 1) * 128)

                        def f(e, rows=rows, cs=cs):
                            o3 = ps[5][rows, 0:512].rearrange("p (a b) -> p a b", a=2)[:, :, cs]
                            r3 = rden[rows, :].rearrange("p (a b) -> p a b", a=2)[:, :, cs]
                            return e.tensor_tensor(out=tmpo[rows, :, :], in0=o3, in1=r3, op=ALU.mult)
                        P.op("dve", f, reads=["ps5", "rden"], writes=["tmpo%d" % half])
                        P.op("pool", lambda e, rows=rows: e.tensor_tensor(
                            out=ybT[rows, 2 * kh:2 * kh + 2, m * 128:(m + 1) * 128], in0=tmpo[rows, :, :],
                            in1=zsb[rows, :, m * 128:(m + 1) * 128], op=ALU.mult),
                            reads=["tmpo%d" % half, "zsb"], writes=["ybT"])

                if kh + 1 < 4:
                    swa_wload(kh + 1)
                stage_s(0)
                for m in range(16):
                    if m + 1 < 16:
                        stage_s(m + 1)
                    stage_r(m)
            for i_ in range(4):
                src_ = (wpa, wpb, wgate, wgate)[i_][0 if i_ < 3 else 8]
                ld("pool", wf0_top[i_][:], src_, "ld_wf%d" % i_, "wf%d" % i_)
        marks["S"] = len(P.ops)
        P.fence()

        if debug:
            with contextlib.ExitStack() as d_:
                dtmp = sbuf(d_, "dtmp", [128, 8, SEQ], F32)
                P.op("dve", lambda e: e.tensor_copy(out=dtmp[:], in_=yaT[:]), reads=["yaT"], writes=["dtmp"])
                P.dma("sp", lambda e, s: e.dma_start(out=dbg_ya, in_=dtmp[:]).then_inc(s, 16), "st_dbg", reads=["dtmp"])
                P.op("dve", lambda e: e.tensor_copy(out=dtmp[:], in_=ybT[:]), reads=["ybT"], writes=["dtmp"])
                P.dma("sp", lambda e, s: e.dma_start(out=dbg_yb, in_=dtmp[:]).then_inc(s, 16), "st_dbg", reads=["dtmp"])
            P.fence()

        with contextlib.ExitStack() as f_:
            mixT = sbuf(f_, "mixT", [128, 8, SEQ], BF16)
            wf = wf0_top + [sbuf(f_, "wf%d" % i, [128, 8, 128], BF16) for i in range(4, 8)]
            wo = sbuf(f_, "wo", [128, 8, D], BF16)
            lnw = sbuf(f_, "lnw", [128, D], F32)
            lnb = sbuf(f_, "lnb", [128, D], F32)
            sga = sbuf(f_, "sga", [128, 512], F32)
            sgb = sbuf(f_, "sgb", [128, 512], F32)
            m1 = sbuf(f_, "m1", [128, 512], F32)
            m2 = sbuf(f_, "m2", [128, 512], F32)
            xt = [sbuf(f_, "xt%d" % i, [128, D], F32) for i in range(3)]
            res = [sbuf(f_, "res%d" % i, [128, D], F32) for i in range(3)]
            st3 = [sbuf(f_, "st3_%d" % i, [128, 8], F32) for i in range(3)]

            ld("pool", wo[:], wout, "ld_wo", "wo")
            ld("sp", lnw[:], lnw_d, "ld_c", "lnw")
            ld("sp", lnb[:], lnb_d, "ld_c", "lnb")
            srcs = (wpa, wpb, wgate, wgate)
            def fin_wload(dt_):
                sset = (dt_ % 2) * 4
                for i in range(4):
                    src = srcs[i][dt_ if i < 3 else 8 + dt_]
                    ld("pool", wf[sset + i][:], src, "ld_wf%d" % (sset + i), "wf%d" % (sset + i))

            for dt_ in range(8):
                sset = (dt_ % 2) * 4
                for tb in range(4):
                    if tb == 1 and dt_ + 1 < 8:
                        fin_wload(dt_ + 1)
                    tsl = slice(tb * 512, (tb + 1) * 512)
                    hsl = slice(XCOL + tb * 512, XCOL + (tb + 1) * 512)
                    banks = (0, 1, 2, 3) if tb % 2 == 0 else (4, 5, 6, 7)
                    rhs_l = (lambda kc, tsl=tsl: yaT[:, kc, tsl], lambda kc, tsl=tsl: ybT[:, kc, tsl],
                             lambda kc, hsl=hsl: hT[:, kc, hsl], lambda kc, hsl=hsl: hT[:, kc, hsl])
                    rd = ("yaT", "ybT", "hT", "hT")
                    for i in range(4):
                        def f(e, i=i, bk=banks[i], rf=rhs_l[i], wi=sset + i):
                            r = None
                            for kc in range(8):
                                r = e.matmul(ps[bk][:, 0:512], lhsT=wf[wi][:, kc, :], rhs=rf(kc), start=(kc == 0), stop=(kc == 7))
                            return r
                        P.op("pe", f, reads=[rd[i], "wf%d" % (sset + i)], writes=["ps%d" % banks[i]])
                    P.op("act", lambda e, bk=banks[2], dt_=dt_: e.activation(out=sga[:, :], in_=ps[bk][:, 0:512], func=AF.Sigmoid,
                                                                            bias=bg[:, dt_:dt_ + 1], scale=1.0),
                         reads=["ps%d" % banks[2], "bg"], writes=["sga"])
                    P.op("act", lambda e, bk=banks[3], dt_=dt_: e.activation(out=sgb[:, :], in_=ps[bk][:, 0:512], func=AF.Sigmoid,
                                                                            bias=bg[:, 8 + dt_:9 + dt_], scale=1.0),
                         reads=["ps%d" % banks[3], "bg"], writes=["sgb"])
                    P.op("dve", lambda e, bk=banks[0]: e.tensor_tensor(out=m1[:, :], in0=ps[bk][:, 0:512], in1=sga[:, :], op=ALU.mult),
                         reads=["ps%d" % banks[0], "sga"], writes=["m1"])
                    P.op("dve", lambda e, bk=banks[1]: e.tensor_tensor(out=m2[:, :], in0=ps[bk][:, 0:512], in1=sgb[:, :], op=ALU.mult),
                         reads=["ps%d" % banks[1], "sgb"], writes=["m2"])
                    P.op("pool", lambda e, dt_=dt_, tsl=tsl: e.tensor_tensor(out=mixT[:, dt_, tsl], in0=m1[:, :], in1=m2[:, :], op=ALU.add),
                         reads=["m1", "m2"], writes=["mixT"])
            def lnS1(tt):
                k3 = tt % 3
                pp = tt % 2
                x_, r_ = xt[k3], res[k3]
                ch = Chain()
                ch.dma("sp", lambda e, s: e.dma_start(out=x_[:], in_=xtok[tt * 128:(tt + 1) * 128, :]).then_inc(s, 16),
                       "ld_x%d" % k3, writes=["xt%d" % k3])
                bks = (0, 1) if pp == 0 else (2, 3)
                for hb in range(2):
                    def f(e, hb=hb, bk=bks[hb]):
                        r = None
                        for kc in range(8):
                            r = e.matmul(ps[bk][:, 0:512], lhsT=mixT[:, kc, tt * 128:(tt + 1) * 128], rhs=wo[:, kc, hb * 512:(hb + 1) * 512],
                                         start=(kc == 0), stop=(kc == 7))
                        return r
                    ch.op("pe", f, reads=["mixT", "wo"], writes=["ps%d" % bks[hb]])
                    ch.op("dve", lambda e, hb=hb, bk=bks[hb]: e.scalar_tensor_tensor(
                        out=r_[:, hb * 512:(hb + 1) * 512], in0=x_[:, hb * 512:(hb + 1) * 512], scalar=DEEPNORM_ALPHA,
                        in1=ps[bk][:, 0:512], op0=ALU.mult, op1=ALU.add),
                        reads=["xt%d" % k3, "ps%d" % bks[hb]], writes=["res%d" % k3])
                return ch

            def lnS2(tt):
                k3 = tt % 3
                x_, r_, s_ = xt[k3], res[k3], st3[k3]
                sn = "st%d" % k3
                ch = Chain()
                ch.op("dve", lambda e: e.reduce_sum(out=s_[:, 0:1], in_=r_[:, :], axis=AX.X), reads=["res%d" % k3], writes=[sn + "a"])
                ch.op("act", lambda e: e.activation(out=x_[:, :], in_=r_[:, :], func=AF.Square),
                      reads=["res%d" % k3], writes=["xt%d" % k3])
                ch.op("dve", lambda e: e.reduce_sum(out=s_[:, 1:2], in_=x_[:, :], axis=AX.X), reads=["xt%d" % k3], writes=[sn + "b"])
                ch.op("dve", lambda e: e.tensor_scalar(out=s_[:, 2:3], in0=s_[:, 0:1], scalar1=1.0 / D, scalar2=None, op0=ALU.mult),
                      reads=[sn + "a"], writes=[sn + "c"])
                ch.op("dve", lambda e: e.tensor_tensor(out=s_[:, 3:4], in0=s_[:, 2:3], in1=s_[:, 2:3], op=ALU.mult),
                      reads=[sn + "c"], writes=[sn + "d"])
                ch.op("dve", lambda e: e.scalar_tensor_tensor(out=s_[:, 4:5], in0=s_[:, 1:2], scalar=1.0 / D, in1=s_[:, 3:4],
                                                              op0=ALU.mult, op1=ALU.subtract),
                      reads=[sn + "b", sn + "d"], writes=[sn + "e"])
                ch.op("act", lambda e: e.activation(out=s_[:, 5:6], in_=s_[:, 4:5], func=AF.Sqrt, bias=LN_EPS, scale=1.0),
                      reads=[sn + "e"], writes=[sn + "f"])
                ch.op("dve", lambda e: e.reciprocal(out=s_[:, 6:7], in_=s_[:, 5:6]), reads=[sn + "f"], writes=[sn + "g"])
                return ch

            def lnS3(tt):
                k3 = tt % 3
                r_, s_ = res[k3], st3[k3]
                sn = "st%d" % k3
                ch = Chain()
                ch.op("dve", lambda e: e.tensor_scalar(out=r_[:, :], in0=r_[:, :], scalar1=s_[:, 2:3], scalar2=s_[:, 6:7],
                                                       op0=ALU.subtract, op1=ALU.mult),
                      reads=["res%d" % k3, sn + "c", sn + "g"], writes=["res%d" % k3])
                ch.op("pool", lambda e: e.tensor_tensor(out=r_[:, :], in0=r_[:, :], in1=lnw[:, :], op=ALU.mult),
                      reads=["res%d" % k3, "lnw"], writes=["res%d" % k3])
                ch.op("pool", lambda e: e.tensor_tensor(out=r_[:, :], in0=r_[:, :], in1=lnb[:, :], op=ALU.add),
                      reads=["res%d" % k3, "lnb"], writes=["res%d" % k3])
                ch.dma("sp", lambda e, s: e.dma_start(out=out_d[tt * 128:(tt + 1) * 128, :], in_=r_[:]).then_inc(s, 16),
                       "st_out%d" % k3, reads=["res%d" % k3])
                return ch

            for step in range(16 + 2):
                lists = []
                if step < 16:
                    lists.append(lnS1(step))
                if 0 <= step - 1 < 16:
                    lists.append(lnS2(step - 1))
                if 0 <= step - 2 < 16:
                    lists.append(lnS3(step - 2))
                for it in interleave(lists):
                    P.add_item(it)
        _MARKS.update(marks)
        _MARKS["end"] = len(P.ops)
        lim = None
        if stop is not None:
            lim = marks[stop] if stop in marks else int(stop)
        P.emit(nc, limit=lim)
    return nc


def _tile_w(w):
    n = w.shape[1] // 128
    return np.ascontiguousarray(w.reshape(8, 128, n, 128).transpose(2, 1, 0, 3))


def _const_tables():
    t = np.arange(64)
    c64 = np.zeros((64, 5, 64), np.float32)
    c64[:, 0, :] = (t[:, None] <= t[None, :])
    c64[:, 1, :] = (t[:, None] > t[None, :])
    c64[:, 2, :] = np.eye(64)
    c64[:, 3, :] = np.where(t[None, :] < t[:, None], NEG, 0.0)
    c64[:, 4, :] = np.where(t[None, :] >= t[:, None], NEG, 0.0)
    slopes = np.exp2(-8.0 * np.arange(1, 17, dtype=np.float64) / 16.0)
    k = np.arange(128)[:, None]
    i = np.arange(128)[None, :]
    biasP = np.zeros((128, 4, 512), np.float64)
    biasO = np.zeros((128, 4, 512), np.float64)
    rm = np.zeros((3, 4, 512), np.float64)
    for kh in range(4):
        for qt in range(2):
            for half in range(2):
                h = 4 * kh + 2 * qt + half
                cs = slice((2 * qt + half) * 128, (2 * qt + half + 1) * 128)
                bp = -8.0 * slopes[h] * (128 + i - k)
                bp = np.where((k < 64) & (i >= 64), 8.0 * NEG, bp)
                biasP[:, kh, cs] = bp
                bo = -8.0 * slopes[h] * np.abs(i - k)
                bo = np.where((k >= 64) & (i < 64), 8.0 * NEG, bo)
                biasO[:, kh, cs] = bo
                rm[0, kh, cs] = -8.0 * slopes[h] * (16 + np.arange(128))
                rm[1, kh, cs] = 8.0 * slopes[h]
                rm[2, kh, cs] = -8.0 * slopes[h] * 128.0
    lm = np.zeros((3, 16, 16), np.float32)
    lm[0] = 1.0
    lm[1] = np.arange(16)[None, :]
    lm[2] = np.arange(16)[:, None]
    return (c64, biasP.astype(np.float32), biasO.astype(np.float32), lm, rm.astype(np.float32))


_CACHE = {}


def _get_program(debug=False):
    if debug not in _CACHE:
        _CACHE[debug] = build_program(debug)
    return _CACHE[debug]


def _prepare(x, meta_tokens, w_in, b_gate, conv_w, a_log, dt_bias, gdn_norm_w, sinks,
             w_proj_a, w_proj_b, w_out, ln_w, ln_b):
    f = np.float32
    w = np.asarray(w_in, f)[0]
    o = 0
    seg = {}
    for name, wd in zip(("qa", "ka", "va", "za", "be", "de", "qb", "kb", "vb", "zb", "ga", "gb"),
                        (1024, 1024, 1024, 1024, 8, 8, 1024, 256, 256, 1024, 1024, 1024)):
        seg[name] = w[:, o:o + wd]
        o += wd
    tq, tk, tv, tz = (_tile_w(seg[n]) for n in ("qa", "ka", "va", "za"))
    wg = np.ascontiguousarray(np.stack([tq, tk, tv, tz], axis=1).reshape(32, 128, 8, 128))
    wbd = np.ascontiguousarray(np.concatenate([seg["be"], seg["de"]], axis=1).reshape(8, 128, 16).transpose(1, 0, 2))
    kv = []
    for kh in range(4):
        kc_ = seg["kb"][:, kh * 64:(kh + 1) * 64]
        vc_ = seg["vb"][:, kh * 64:(kh + 1) * 64]
        kv.append(np.concatenate([kc_, kc_], axis=1))
        kv.append(np.concatenate([vc_, vc_], axis=1))
    wkv = _tile_w(np.concatenate(kv, axis=1))
    wqz = np.concatenate([_tile_w(seg["qb"]), _tile_w(seg["zb"])], axis=0)
    wgate = np.concatenate([_tile_w(seg["ga"]), _tile_w(seg["gb"])], axis=0)
    wpa = _tile_w(np.asarray(w_proj_a, f)[0])
    wpb = _tile_w(np.asarray(w_proj_b, f)[0])
    wout = np.ascontiguousarray(np.asarray(w_out, f)[0].reshape(8, 128, D).transpose(1, 0, 2))
    c64, biasP, biasO, lm, rm = _const_tables()
    cw = np.ascontiguousarray(np.asarray(conv_w, f)[0].T.reshape(24, 128, 4).transpose(1, 0, 2))
    sk = np.zeros((1, 4, 512), f)
    sv = np.asarray(sinks, f)[0]
    for kh in range(4):
        for qt in range(2):
            for half in range(2):
                sk[0, kh, (2 * qt + half) * 128:(2 * qt + half + 1) * 128] = sv[4 * kh + 2 * qt + half]
    shared = {
        "metaT": np.ascontiguousarray(np.asarray(meta_tokens, f).T),
        "wg": wg, "wbd": wbd, "wkv": wkv, "wqz": wqz, "wgate": wgate, "wpa": wpa, "wpb": wpb, "wout": wout,
        "c64": c64, "ident": np.eye(128, dtype=f), "cw": cw,
        "gnw": np.ascontiguousarray(np.asarray(gdn_norm_w, f)[0].reshape(128, 1)),
        "bg": np.ascontiguousarray(np.asarray(b_gate, f)[0].reshape(16, 128).T),
        "alog": np.ascontiguousarray(np.broadcast_to(np.asarray(a_log, f)[0][None, :], (64, 8))),
        "dtb": np.ascontiguousarray(np.broadcast_to(np.asarray(dt_bias, f)[0][None, :], (64, 8))),
        "sk": sk,
        "lnw": np.ascontiguousarray(np.broadcast_to(np.asarray(ln_w, f)[0][None, :], (128, D))),
        "lnb": np.ascontiguousarray(np.broadcast_to(np.asarray(ln_b, f)[0][None, :], (128, D))),
        "biasP": biasP, "biasO": biasO, "lm": lm, "rm": rm,
    }
    xs = np.asarray(x, f)
    in_maps = []
    for b in range(xs.shape[0]):
        m = dict(shared)
        m["xT"] = np.ascontiguousarray(xs[b].T)
        m["xtok"] = np.ascontiguousarray(xs[b])
        in_maps.append(m)
    return in_maps


def kernel(x, meta_tokens, w_in, b_gate, conv_w, a_log, dt_bias, gdn_norm_w, sinks,
           w_proj_a, w_proj_b, w_out, ln_w, ln_b, _debug=False):
    in_maps = _prepare(x, meta_tokens, w_in, b_gate, conv_w, a_log, dt_bias, gdn_norm_w, sinks,
                       w_proj_a, w_proj_b, w_out, ln_w, ln_b)
    nc = _get_program(_debug)
    res = run_bass_kernel_spmd(nc, in_maps, core_ids=list(range(len(in_maps))))
    out = np.stack([np.asarray(r["out"], np.float32) for r in res.results], axis=0)
    if _debug:
        return out, res
    return out
```

```python
import contextlib
import numpy as np
import concourse.bass as bass
import concourse.mybir as mybir
from concourse.bass_utils import run_bass_kernel_spmd

F32 = mybir.dt.float32
BF16 = mybir.dt.bfloat16
AF = mybir.ActivationFunctionType
ALU = mybir.AluOpType
AX = mybir.AxisListType

ENGS = ("pe", "act", "dve", "pool", "sp")


class Op:
    __slots__ = ("eng", "fn", "deps", "sig", "dma_sem", "dma_cnt", "idx", "ndma")

    def __init__(self, eng, fn):
        self.eng = eng
        self.fn = fn
        self.deps = set()
        self.sig = None
        self.dma_sem = None
        self.ndma = 0


class _Dummy:
    def then_inc(self, *a, **k):
        return self


class _Rec:
    def __init__(self):
        self.calls = []

    def __getattr__(self, name):
        def m(*a, **k):
            self.calls.append((name, a, k))
            return _Dummy()
        return m


def _free_size(ap):
    n = 1
    for d in tuple(ap.shape)[1:]:
        n *= int(d)
    return n


def _cost_us(eng, calls):
    t = 0.0
    for (nm, a, k) in calls:
        out = k.get("out", a[0] if a else None)
        fs = _free_size(out) if out is not None else 64
        if eng == "pe":
            rhs = k.get("rhs", None)
            f32 = rhs is not None and rhs.dtype == F32
            t += max(0.075, fs * (0.0024 if f32 else 0.00062))
        elif eng == "act":
            t += 0.15 + 0.00095 * fs
        elif eng == "dve":
            t += 0.10 + 0.0011 * fs
        elif eng == "pool":
            t += 0.35 + 0.0016 * fs
        else:
            t += 0.1
    return t


class Chain(list):
    def op(self, eng, fn, reads=(), writes=()):
        rec = _Rec()
        fn(rec)
        self.append(("op", eng, rec.calls, None, tuple(reads), tuple(writes), _cost_us(eng, rec.calls)))

    def dma(self, eng, fn, stream, reads=(), writes=()):
        rec = _Rec()
        fn(rec, None)
        nbytes = 0
        for (nm, a, k) in rec.calls:
            o = k.get("out")
            nbytes += _free_size(o) * 128 * 4
        self.append(("dma", eng, rec.calls, stream, tuple(reads), tuple(writes), 2.0 + nbytes / 150e3))


class Sched:
    def __init__(self):
        self.eng_free = {e: 0.0 for e in ENGS}
        self.w_fin = {}
        self.r_fin = {}
        self.LAT = 0.2
        self.EPS = 0.4

    def run(self, chains, add):
        chains = [c for c in chains if len(c)]
        pos = [0] * len(chains)
        left = sum(len(c) for c in chains)
        while left:
            cands = []
            for k, c in enumerate(chains):
                if pos[k] >= len(c):
                    continue
                it = c[pos[k]]
                eng, reads, writes = it[1], it[4], it[5]
                rdy = 0.0
                for r in reads:
                    rdy = max(rdy, self.w_fin.get(r, 0.0))
                for r in writes:
                    rdy = max(rdy, self.w_fin.get(r, 0.0), self.r_fin.get(r, 0.0))
                st = max(rdy + self.LAT, self.eng_free[eng])
                fr = (pos[k] + 0.5) / len(c)
                cands.append((st, fr, k))
            mn = min(x[0] for x in cands)
            st, fr, k = min((x for x in cands if x[0] <= mn + self.EPS), key=lambda x: x[1])
            it = chains[k][pos[k]]
            pos[k] += 1
            left -= 1
            eng, reads, writes, dur = it[1], it[4], it[5], it[6]
            if it[0] == "dma":
                self.eng_free[eng] = st + 0.1
                fin = st + dur
            else:
                fin = st + dur
                self.eng_free[eng] = fin
            for r in reads:
                self.r_fin[r] = max(self.r_fin.get(r, 0.0), fin)
            for r in writes:
                self.w_fin[r] = fin
                self.r_fin[r] = 0.0
            add(it)


def interleave(lists):
    lists = [l for l in lists if len(l)]
    out = Chain()
    pos = [0] * len(lists)
    total = sum(len(l) for l in lists)
    for _ in range(total):
        best, bf = None, None
        for k, l in enumerate(lists):
            if pos[k] < len(l):
                fr = (pos[k] + 0.5) / len(l)
                if bf is None or fr < bf:
                    best, bf = k, fr
        out.append(lists[best][pos[best]])
        pos[best] += 1
    return out


class Prog:
    def __init__(self):
        self.ops = []
        self.last_w = {}
        self.readers = {}
        self.dma_streams = {}
        self.fence_deps = set()
        self.fenced = set(ENGS)
        self.last_on = {}
        self.dma_since_fence = []

    def fence(self):
        self.fence_deps = set(self.last_on.values()) | set(self.dma_since_fence)
        self.dma_since_fence = []
        self.fenced = set()

    def _add(self, op, reads, writes):
        idx = len(self.ops)
        op.idx = idx
        for r in reads:
            w = self.last_w.get(r)
            if w is not None:
                op.deps.add(w)
        for r in writes:
            w = self.last_w.get(r)
            if w is not None:
                op.deps.add(w)
            for rd in self.readers.get(r, ()):
                op.deps.add(rd)
        for r in reads:
            self.readers.setdefault(r, []).append(idx)
        for r in writes:
            self.last_w[r] = idx
            self.readers[r] = []
        if op.eng not in self.fenced:
            op.deps |= self.fence_deps
            self.fenced.add(op.eng)
        op.deps.discard(idx)
        self.last_on[op.eng] = idx
        if op.dma_sem is not None:
            self.dma_since_fence.append(idx)
        self.ops.append(op)
        return op

    def op(self, eng, fn, reads=(), writes=()):
        rec = _Rec()
        fn(rec)
        return self._add(Op(eng, rec.calls), reads, writes)

    def add_item(self, it):
        kind, eng, calls, stream, reads, writes = it[:6]
        o = Op(eng, calls)
        if kind == "dma":
            o.dma_sem = stream
            o.ndma = len(calls)
            c = self.dma_streams.get(stream, 0) + o.ndma
            self.dma_streams[stream] = c
            o.dma_cnt = c
        return self._add(o, reads, writes)

    def dma(self, eng, fn, stream, ndma=1, reads=(), writes=()):
        rec = _Rec()
        fn(rec, None)
        o = Op(eng, rec.calls)
        ndma = len(rec.calls)
        o.dma_sem = stream
        o.ndma = ndma
        c = self.dma_streams.get(stream, 0) + ndma
        self.dma_streams[stream] = c
        o.dma_cnt = c
        return self._add(o, reads, writes)

    def emit(self, nc, final_wait_eng="sp", limit=None):
        ops = self.ops if limit is None else self.ops[:limit]
        final_cnt = {}
        for o in ops:
            if o.dma_sem is not None:
                final_cnt[o.dma_sem] = max(final_cnt.get(o.dma_sem, 0), o.dma_cnt)
        need = [False] * len(ops)
        for o in ops:
            for d in o.deps:
                p = ops[d]
                if p.dma_sem is None and p.eng == "pe" and o.eng == "pe" and o.dma_sem is None:
                    continue
                need[d] = True
        cnt = {e: 0 for e in ENGS}
        for o in ops:
            if o.dma_sem is not None:
                o.sig = ("dma:" + o.dma_sem, 16 * o.dma_cnt)
            elif need[o.idx]:
                cnt[o.eng] += 1
                o.sig = (o.eng, cnt[o.eng])
        with contextlib.ExitStack() as st:
            sems = {}
            for e in ENGS:
                sems[e] = st.enter_context(nc.semaphore("s_" + e))
            for s in self.dma_streams:
                sems["dma:" + s] = st.enter_context(nc.semaphore("d_" + s))
            block = st.enter_context(nc.Block())

            def stream(eng_name):
                def body(eng):
                    waited = {}
                    for o in ops:
                        if o.eng != eng_name:
                            continue
                        reqs = {}
                        for d in o.deps:
                            p = ops[d]
                            if p.sig is None:
                                continue
                            if p.dma_sem is None and p.eng == "pe" and eng_name == "pe" and o.dma_sem is None:
                                continue
                            k, v = p.sig
                            if reqs.get(k, 0) < v:
                                reqs[k] = v
                        for k, v in reqs.items():
                            if waited.get(k, 0) < v:
                                eng.wait_ge(sems[k], v)
                                waited[k] = v
                        if o.dma_sem is not None:
                            for (nm, a, k) in o.fn:
                                getattr(eng, nm)(*a, **k).then_inc(sems["dma:" + o.dma_sem], 16)
                        else:
                            ins = None
                            for (nm, a, k) in o.fn:
                                ins = getattr(eng, nm)(*a, **k)
                            if o.sig is not None:
                                ins.then_inc(sems[o.sig[0]], 1)
                    if eng_name == final_wait_eng:
                        for s, c in final_cnt.items():
                            eng.wait_ge(sems["dma:" + s], 16 * c)
                return body

            block.tensor(stream("pe"))
            block.scalar(stream("act"))
            block.vector(stream("dve"))
            block.gpsimd(stream("pool"))
            block.sync(stream("sp"))


D = 1024
SEQ = 2048
NMETA = 16
C = 64
NCH = 33
LP = NCH * C
HOFF = 3
HCOLS = HOFF + LP
XCOL = HOFF + C
MCOL = HOFF + 48
NH_A = 8
DEEPNORM_ALPHA = 2.0 ** 0.25
LN_EPS = 1e-5
RMS_EPS = 1e-6
L2_EPS = 1e-6
NEG = -30000.0


def _blocks(n, b=512):
    out = []
    s = 0
    while s < n:
        out.append((s, min(b, n - s)))
        s += b
    return out


_MARKS = {}


def build_program(debug=False, stop=None):
    nc = bass.Bass("TRN2", target_bir_lowering=False)
    marks = {}

    def din(name, shape):
        return nc.dram_tensor(name, list(shape), F32, kind="ExternalInput").ap()

    xT = din("xT", [D, SEQ])
    xtok = din("xtok", [SEQ, D])
    metaT = din("metaT", [D, NMETA])
    wg = din("wg", [32, 128, 8, 128])
    wbd = din("wbd", [128, 8, 16])
    wkv = din("wkv", [8, 128, 8, 128])
    wqz = din("wqz", [16, 128, 8, 128])
    wgate = din("wgate", [16, 128, 8, 128])
    wpa = din("wpa", [8, 128, 8, 128])
    wpb = din("wpb", [8, 128, 8, 128])
    wout = din("wout", [128, 8, D])
    c64_d = din("c64", [64, 5, 64])
    ident_d = din("ident", [128, 128])
    cw_d = din("cw", [128, 24, 4])
    gnw_d = din("gnw", [128, 1])
    bg_d = din("bg", [128, 16])
    alog_d = din("alog", [64, 8])
    dtb_d = din("dtb", [64, 8])
    sk_d = din("sk", [1, 4, 512])
    lnw_d = din("lnw", [128, D])
    lnb_d = din("lnb", [128, D])
    biasP_d = din("biasP", [128, 4, 512])
    biasO_d = din("biasO", [128, 4, 512])
    lm_d = din("lm", [3, 16, 16])
    rm_d = din("rm", [3, 4, 512])
    out_d = nc.dram_tensor("out", [SEQ, D], F32, kind="ExternalOutput").ap()
    if debug:
        dbg_ya = nc.dram_tensor("dbg_ya", [128, 8, SEQ], F32, kind="ExternalOutput").ap()
        dbg_yb = nc.dram_tensor("dbg_yb", [128, 8, SEQ], F32, kind="ExternalOutput").ap()

    P = Prog()
    with contextlib.ExitStack() as top:
        def sbuf(st, name, shape, dt):
            return st.enter_context(nc.sbuf_tensor("sb_" + name, list(shape), dt))

        ps = [top.enter_context(nc.psum_tensor("ps%d" % i, [128, 512], F32)) for i in range(8)]

        hT = sbuf(top, "hT", [128, 8, HCOLS], BF16)
        yaT = sbuf(top, "yaT", [128, 8, SEQ], BF16)
        identb = sbuf(top, "identb", [128, 128], BF16)
        onesb = sbuf(top, "onesb", [128, 128], BF16)
        c64f = sbuf(top, "c64f", [64, 5, 64], F32)
        c64b = sbuf(top, "c64b", [64, 5, 64], BF16)
        gnw = sbuf(top, "gnw", [128, 1], F32)
        bg = sbuf(top, "bg", [128, 16], F32)

        ld_n = [0]

        def ld(eng, dst, src, stream, reg):
            if stream == "ld_c":
                ld_n[0] += 1
                stream = "ldc%d" % ld_n[0]
            P.dma(eng, lambda e, s: e.dma_start(out=dst, in_=src).then_inc(s, 16), stream, writes=[reg])

        P.op("dve", lambda e: e.memset(hT[:, :, 0:MCOL], 0.0), writes=["hT_z"])
        ld("pool", hT[:, :, MCOL:XCOL], metaT.rearrange("(k p) t -> p k t", p=128), "ld_hm", "hT_m")
        for kc in range(8):
            ld("pool", hT[:, kc, XCOL:XCOL + 512], xT[kc * 128:(kc + 1) * 128, 0:512], "ld_ha%d" % kc, "hTa_%d" % kc)
        for kc in range(8):
            ld("pool", hT[:, kc, XCOL + 512:HCOLS], xT[kc * 128:(kc + 1) * 128, 512:SEQ], "ld_h%d" % kc, "hT_%d" % kc)
        ld("pool", identb[:], ident_d, "ld_c", "identb")
        ld("pool", c64b[:], c64_d, "ld_c", "c64b")
        ld("sp", c64f[:], c64_d, "ld_c", "c64f")
        ld("sp", gnw[:], gnw_d, "ld_c", "gnw")
        ld("sp", bg[:], bg_d, "ld_c", "bg")
        P.op("dve", lambda e: e.memset(onesb[:], 1.0), writes=["onesb"])
        hscr = sbuf(top, "hscr", [128, 8], F32)
        P.op("dve", lambda e: e.memset(hscr[:, 0:4], 0.0), reads=["hT_z", "hT_m"] + ["hTa_%d" % k for k in range(8)], writes=["hTa"])
        P.op("dve", lambda e: e.memset(hscr[:, 4:8], 0.0), reads=["hTa"] + ["hT_%d" % k for k in range(8)], writes=["hT"])

        TRI, SLM, I64, NEGT, NEGD = 0, 1, 2, 3, 4
        marks["0"] = len(P.ops)

        wk0_top = sbuf(top, "wk0", [128, 8, 128], BF16)
        wv0_top = sbuf(top, "wv0", [128, 8, 128], BF16)
        with contextlib.ExitStack() as g:
            NTQ = 576
            NCQ = 9
            cw = sbuf(g, "cw", [128, 24, 4], F32)
            alog = sbuf(g, "alog", [64, 8], F32)
            dtb = sbuf(g, "dtb", [64, 8], F32)
            negA = sbuf(g, "negA", [64, 8], F32)
            wbd_sb = sbuf(g, "wbd_sb", [128, 8, 16], BF16)
            negT8 = sbuf(g, "negT8", [64, 8, 64], BF16)
            negD8 = sbuf(g, "negD8", [64, 8, 64], BF16)
            wts = [sbuf(g, "wts%d" % i, [128, 4, 8, 128], BF16) for i in range(3)]
            pre = [sbuf(g, "pre%d" % i, [128, NTQ + 3], F32) for i in range(3)]
            acc = [sbuf(g, "acc%d" % i, [128, NTQ], F32) for i in range(3)]
            sqb = [sbuf(g, "sq%d" % i, [128, NTQ], BF16) for i in range(2)]
            rnb = [sbuf(g, "rn%d" % i, [128, 512], F32) for i in range(2)]
            qn = [sbuf(g, "qn%d" % i, [128, NTQ], BF16) for i in range(2)]
            kn = [sbuf(g, "kn%d" % i, [128, NTQ], BF16) for i in range(2)]
            vT = [sbuf(g, "vT%d" % i, [128, NTQ], BF16) for i in range(2)]
            zs = [sbuf(g, "zs%d" % i, [128, NTQ], BF16) for i in range(3)]
            qd = [sbuf(g, "qd%d" % i, [128, NTQ], BF16) for i in range(3)]
            ke = [sbuf(g, "ke%d" % i, [64, NCQ, 128], BF16) for i in range(2)]
            vtok = [sbuf(g, "vtok%d" % i, [64, NCQ, 128], BF16) for i in range(2)]
            kd = [sbuf(g, "kd%d" % i, [64, NCQ, 128], BF16) for i in range(3)]
            X0s = [sbuf(g, "X0s%d" % i, [64, NCQ * 64], BF16) for i in range(2)]
            XT0s = [sbuf(g, "XT0s%d" % i, [64, NCQ * 64], BF16) for i in range(2)]
            R0s = [sbuf(g, "R0s%d" % i, [64, NCQ * 64], BF16) for i in range(2)]
            gt = {}
            for nm in ("beta", "nbeta", "gg", "gtmp", "gcs", "egc", "ekd"):
                gt[nm] = [sbuf(g, "%s%d" % (nm, i), [64, NCQ, 8], F32) for i in range(2)]
            gt["glast"] = [sbuf(g, "glast%d" % i, [128, NCQ, 8], F32) for i in range(2)]
            gt["ggh"] = [sbuf(g, "ggh%d" % i, [64, NCQ, 8], BF16) for i in range(2)]
            gt["ggl"] = [sbuf(g, "ggl%d" % i, [64, NCQ, 8], BF16) for i in range(2)]
            GMh = sbuf(g, "GMh", [64, 8, 64], BF16)
            GMl = sbuf(g, "GMl", [64, 8, 64], BF16)
            GM2h = sbuf(g, "GM2h", [64, 8, 64], BF16)
            GM2l = sbuf(g, "GM2l", [64, 8, 64], BF16)
            Bd = sbuf(g, "Bd", [64, 8, 64], BF16)
            Dg = sbuf(g, "Dg", [64, 8, 64], BF16)
            decT = sbuf(g, "decT", [64, 512], F32)
            dec = sbuf(g, "dec", [64, 512], F32)
            t1 = sbuf(g, "t1", [64, 512], F32)
            t2 = sbuf(g, "t2", [64, 512], F32)
            Xb = [sbuf(g, "X%d" % i, [64, 512], BF16) for i in range(2)]
            XTb = [sbuf(g, "XT%d" % i, [64, 512], BF16) for i in range(2)]
            Rb = [sbuf(g, "R%d" % i, [64, 512], BF16) for i in range(2)]
            R5 = sbuf(g, "R5", [64, 512], F32)
            Y = sbuf(g, "Y", [64, NCQ * 64], BF16)
            attnT = [sbuf(g, "attnT%d" % i, [64, NCQ * 64], BF16) for i in range(3)]
            WT = [sbuf(g, "WT%d" % i, [128, NCQ * 64], BF16) for i in range(2)]
            U = [sbuf(g, "U%d" % i, [64, NCQ, 128], F32) for i in range(2)]
            S = sbuf(g, "S", [128, 8, 128], F32)
            Sb = sbuf(g, "Sb", [128, 128], BF16)
            vn = [sbuf(g, "vn%d" % i, [64, 128], BF16) for i in range(2)]
            osb = sbuf(g, "osb", [128, 512], F32)
            osq = sbuf(g, "osq", [128, 512], BF16)
            orn = sbuf(g, "orn", [128, 512], F32)

            ld("sp", cw[:], cw_d, "ld_c", "cw")
            ld("sp", alog[:], alog_d, "ld_c", "alog")
            ld("sp", dtb[:], dtb_d, "ld_c", "dtb")
            ld("pool", wbd_sb[:], wbd, "ld_c", "wbd")
            P.op("act", lambda e: e.activation(out=negA[:], in_=alog[:], func=AF.Exp), reads=["alog"], writes=["negA"])
            P.op("dve", lambda e: e.tensor_scalar(out=negA[:], in0=negA[:], scalar1=-1.0, scalar2=None, op0=ALU.mult),
                 reads=["negA"], writes=["negA"])
            P.op("dve", lambda e: e.tensor_copy(out=negT8[:], in_=c64f[:, NEGT, :].unsqueeze(1).to_broadcast([64, 8, 64])),
                 reads=["c64f"], writes=["negT8"])
            P.op("dve", lambda e: e.tensor_copy(out=negD8[:], in_=c64f[:, NEGD, :].unsqueeze(1).to_broadcast([64, 8, 64])),
                 reads=["c64f"], writes=["negD8"])
            P.op("dve", lambda e: e.memset(S[:], 0.0), writes=["S"])

            QUARTERS = ((0, 9), (9, 8), (17, 8), (25, 8))
            items = [(qi, hh) for qi in range(4) for hh in range(NH_A)]

            def mk_rr(banks):
                st_ = [0]

                def nb_():
                    b = banks[st_[0] % len(banks)]
                    st_[0] += 1
                    return b
                return nb_
            bankA = mk_rr((0, 1))
            bankB = mk_rr((2,))
            bankB2 = mk_rr((3, 4))

            def gates(qi):
                c0, NC = QUARTERS[qi]
                qp = qi % 2
                col0 = HOFF + c0 * C
                ch = Chain()
                beta, nbeta, gg, gtmp, gcs, egc, ekd, glast = (gt[n][qp] for n in
                                                               ("beta", "nbeta", "gg", "gtmp", "gcs", "egc", "ekd", "glast"))
                sfx = str(qp)
                pb = 1
                for c in range(NC):
                    def f(e, c=c):
                        r = None
                        for kc in range(8):
                            r = e.matmul(ps[pb][0:64, c * 16:(c + 1) * 16],
                                         lhsT=hT[:, kc, col0 + c * 64: col0 + (c + 1) * 64],
                                         rhs=wbd_sb[:, kc, :], start=(kc == 0), stop=(kc == 7))
                        return r
                    ch.op("pe", f, reads=["hTa" if qi == 0 else "hT", "wbd"], writes=["ps1"])
                bdv = ps[pb][0:64, 0:NC * 16].rearrange("p (c k) -> p c k", k=16)
                ch.op("act", lambda e: e.activation(out=beta[:, 0:NC, :], in_=bdv[:, :, 0:8], func=AF.Sigmoid),
                      reads=["ps1"], writes=["beta" + sfx])
                ch.op("dve", lambda e: e.tensor_tensor(out=gtmp[:, 0:NC, :], in0=bdv[:, :, 8:16],
                                                       in1=dtb[:].unsqueeze(1).to_broadcast([64, NC, 8]), op=ALU.add),
                      reads=["ps1", "dtb"], writes=["gtmp" + sfx])
                ch.op("act", lambda e: e.activation(out=gtmp[:, 0:NC, :], in_=gtmp[:, 0:NC, :], func=AF.Exp),
                      reads=["gtmp" + sfx], writes=["gtmp" + sfx])
                ch.op("act", lambda e: e.activation(out=gtmp[:, 0:NC, :], in_=gtmp[:, 0:NC, :], func=AF.Ln, bias=1.0, scale=1.0),
                      reads=["gtmp" + sfx], writes=["gtmp" + sfx])
                ch.op("dve", lambda e: e.tensor_tensor(out=gg[:, 0:NC, :], in0=gtmp[:, 0:NC, :],
                                                       in1=negA[:].unsqueeze(1).to_broadcast([64, NC, 8]), op=ALU.mult),
                      reads=["gtmp" + sfx, "negA"], writes=["gg" + sfx])
                ch.op("dve", lambda e: e.tensor_scalar(out=nbeta[:, 0:NC, :], in0=beta[:, 0:NC, :], scalar1=-1.0,
                                                       scalar2=None, op0=ALU.mult),
                      reads=["beta" + sfx], writes=["nbeta" + sfx])
                ggh, ggl = gt["ggh"][qp], gt["ggl"][qp]
                ch.op("dve", lambda e: e.tensor_copy(out=ggh[:, 0:NC, :], in_=gg[:, 0:NC, :]), reads=["gg" + sfx], writes=["ggh" + sfx])
                ch.op("dve", lambda e: e.tensor_tensor(out=ggl[:, 0:NC, :], in0=gg[:, 0:NC, :], in1=ggh[:, 0:NC, :], op=ALU.subtract),
                      reads=["gg" + sfx, "ggh" + sfx], writes=["ggl" + sfx])
                gghf = ggh[:, 0:NC, :].rearrange("p c k -> p (c k)")
                gglf = ggl[:, 0:NC, :].rearrange("p c k -> p (c k)")

                def f(e):
                    e.matmul(ps[1][0:64, 0:NC * 8], lhsT=c64b[:, TRI, :], rhs=gghf, start=True, stop=False)
                    return e.matmul(ps[1][0:64, 0:NC * 8], lhsT=c64b[:, TRI, :], rhs=gglf, start=False, stop=True)
                ch.op("pe", f, reads=["ggh" + sfx, "ggl" + sfx, "c64b"], writes=["ps1"])
                gcv = ps[1][0:64, 0:NC * 8].rearrange("p (c k) -> p c k", k=8)
                ch.op("act", lambda e: e.activation(out=gcs[:, 0:NC, :], in_=gcv, func=AF.Identity),
                      reads=["ps1"], writes=["gcs" + sfx])
                ch.op("act", lambda e: e.activation(out=egc[:, 0:NC, :], in_=gcv, func=AF.Exp),
                      reads=["ps1"], writes=["egc" + sfx])

                def f(e):
                    e.matmul(ps[1][:, 0:NC * 8], lhsT=onesb[0:64, :], rhs=gghf, start=True, stop=False)
                    return e.matmul(ps[1][:, 0:NC * 8], lhsT=onesb[0:64, :], rhs=gglf, start=False, stop=True)
                ch.op("pe", f, reads=["ggh" + sfx, "ggl" + sfx, "onesb"], writes=["ps1"])
                totv = ps[1][:, 0:NC * 8].rearrange("p (c k) -> p c k", k=8)
                ch.op("dve", lambda e: e.tensor_tensor(out=ekd[:, 0:NC, :], in0=totv[0:64], in1=gcs[:, 0:NC, :], op=ALU.subtract),
                      reads=["ps1", "gcs" + sfx], writes=["ekd" + sfx])
                ch.op("act", lambda e: e.activation(out=ekd[:, 0:NC, :], in_=ekd[:, 0:NC, :], func=AF.Exp),
                      reads=["ekd" + sfx], writes=["ekd" + sfx])
                ch.op("act", lambda e: e.activation(out=glast[:, 0:NC, :], in_=totv, func=AF.Exp),
                      reads=["ps1"], writes=["glast" + sfx])
                return ch

            def wload(i):
                qi, hh = items[i]
                k3 = i % 3
                ch = Chain()
                ch.dma("pool", lambda e, s: e.dma_start(out=wts[k3][:], in_=wg[hh * 4:(hh + 1) * 4].rearrange("x p k c -> p x k c")).then_inc(s, 16),
                       "ld_ws%d" % k3, writes=["wts%d" % k3])
                return ch

            for it in wload(0):
                P.add_item(it)

            def stageA(i):
                qi, hh = items[i]
                c0, NC = QUARTERS[qi]
                NT = NC * C
                col0 = HOFF + c0 * C
                a_ = i % 2
                w_ = (i % 3) * 4
                head = Chain()
                if i + 1 < len(items):
                    head.extend(wload(i + 1))
                chains = []
                XB = (0, 1, 0)
                for X in range(3):
                    ch = Chain()
                    wX = wts[i % 3][:, X]
                    prb = pre[X]
                    prn = "pre%d" % X
                    a = acc[X]
                    an = "acc%d" % X
                    ti = X * 8 + hh
                    for (s0, n) in _blocks(NT + 3):
                        b = XB[X]

                        def f(e, b=b, s0=s0, n=n):
                            r = None
                            for kc in range(8):
                                r = e.matmul(ps[b][:, 0:n], lhsT=wX[:, kc, :],
                                             rhs=hT[:, kc, col0 - 3 + s0: col0 - 3 + s0 + n],
                                             start=(kc == 0), stop=(kc == 7))
                            return r
                        ch.op("pe", f, reads=["hTa" if qi == 0 else "hT", "wts%d" % (i % 3)], writes=["ps%d" % b])
                        ch.op("act", lambda e, b=b, s0=s0, n=n: e.activation(out=prb[:, s0:s0 + n], in_=ps[b][:, 0:n], func=AF.Identity),
                              reads=["ps%d" % b], writes=[prn])
                    ch.op("act", lambda e: e.activation(out=a[:, 0:NT], in_=prb[:, 3:3 + NT], func=AF.Identity, scale=cw[:, ti, 3:4]),
                          reads=[prn, "cw"], writes=[an])
                    for j in (2, 1, 0):
                        ch.op("dve", lambda e, j=j: e.scalar_tensor_tensor(
                            out=a[:, 0:NT], in0=prb[:, j:j + NT], scalar=cw[:, ti, j:j + 1], in1=a[:, 0:NT],
                            op0=ALU.mult, op1=ALU.add), reads=[prn, "cw", an], writes=[an])
                    if X == 2:
                        ch.op("act", lambda e: e.activation(out=vT[a_][:, 0:NT], in_=a[:, 0:NT], func=AF.Silu),
                              reads=[an], writes=["vT%d" % a_])
                    else:
                        ch.op("act", lambda e: e.activation(out=a[:, 0:NT], in_=a[:, 0:NT], func=AF.Silu), reads=[an], writes=[an])
                        sq_ = sqb[X]
                        rn_ = rnb[X]
                        ch.op("pool", lambda e: e.tensor_tensor(out=sq_[:, 0:NT], in0=a[:, 0:NT], in1=a[:, 0:NT], op=ALU.mult),
                              reads=[an], writes=["sq%d" % X])
                        for (s0, n) in _blocks(NT):
                            ch.op("pe", lambda e, s0=s0, n=n: e.matmul(ps[XB[X]][:, 0:n], lhsT=onesb[:, :], rhs=sq_[:, s0:s0 + n],
                                                                      start=True, stop=True),
                                  reads=["sq%d" % X, "onesb"], writes=["ps%d" % XB[X]])
                            ch.op("act", lambda e, n=n: e.activation(out=rn_[:, 0:n], in_=ps[XB[X]][:, 0:n], func=AF.Ln, bias=L2_EPS, scale=1.0),
                                  reads=["ps%d" % XB[X]], writes=["rn%d" % X])
                            ch.op("act", lambda e, n=n: e.activation(out=rn_[:, 0:n], in_=rn_[:, 0:n], func=AF.Exp, scale=-0.5),
                                  reads=["rn%d" % X], writes=["rn%d" % X])
                            if X == 0:
                                ch.op("dve", lambda e, s0=s0, n=n: e.scalar_tensor_tensor(
                                    out=qn[a_][:, s0:s0 + n], in0=a[:, s0:s0 + n], scalar=128.0 ** -0.5, in1=rn_[:, 0:n],
                                    op0=ALU.mult, op1=ALU.mult), reads=[an, "rn0"], writes=["qn%d" % a_])
                            else:
                                ch.op("dve", lambda e, s0=s0, n=n: e.tensor_tensor(
                                    out=kn[a_][:, s0:s0 + n], in0=a[:, s0:s0 + n], in1=rn_[:, 0:n], op=ALU.mult),
                                    reads=[an, "rn1"], writes=["kn%d" % a_])
                    chains.append(ch)
                kch = chains[1]
                if hh == 0:
                    kch = Chain(list(chains[1]) + list(gates(qi)))
                return [head, Chain(list(chains[0]) + list(chains[2])), kch]

            def stageB1(i):
                qi, hh = items[i]
                c0, NC = QUARTERS[qi]
                NT = NC * C
                col0 = HOFF + c0 * C
                a_ = i % 2
                t_ = i % 3
                qp = qi % 2
                sfx = str(qp)
                beta, nbeta, gg, egc, ekd = (gt[n][qp] for n in ("beta", "nbeta", "gg", "egc", "ekd"))
                qn_, kn_, vT_, zs_, qd_, kd_ = qn[a_], kn[a_], vT[a_], zs[t_], qd[t_], kd[t_]
                attnT_ = attnT[t_]
                ke_, vtok_ = ke[a_], vtok[a_]
                sa = str(a_)
                st = str(t_)
                ch = Chain()
                wz_ = wts[i % 3][:, 3]
                for (s0, n) in _blocks(NT):
                    b = bankB()

                    def f(e, b=b, s0=s0, n=n):
                        r = None
                        for kc in range(8):
                            r = e.matmul(ps[b][:, 0:n], lhsT=wz_[:, kc, :], rhs=hT[:, kc, col0 + s0: col0 + s0 + n],
                                         start=(kc == 0), stop=(kc == 7))
                        return r
                    ch.op("pe", f, reads=["hTa" if qi == 0 else "hT", "wts%d" % (i % 3)], writes=["ps%d" % b])
                    ch.op("act", lambda e, b=b, s0=s0, n=n: e.activation(out=zs_[:, s0:s0 + n], in_=ps[b][:, 0:n], func=AF.Silu),
                          reads=["ps%d" % b], writes=["zs" + st])
                for cb in range(0, NC, 4):
                    ncb = min(4, NC - cb)
                    b = bankB()

                    def f(e, b=b, cb=cb, ncb=ncb):
                        r = None
                        for j in range(ncb):
                            c = cb + j
                            r = e.matmul(ps[b][0:64, j * 128:(j + 1) * 128], lhsT=kn_[:, c * 64:(c + 1) * 64], rhs=identb[:, :],
                                         start=True, stop=True)
                        return r
                    ch.op("pe", f, reads=["kn" + sa, "identb"], writes=["ps%d" % b])
                    pv = ps[b][0:64, 0:ncb * 128].rearrange("p (c k) -> p c k", k=128)
                    ch.op("dve", lambda e, pv=pv, cb=cb, ncb=ncb: e.tensor_tensor(
                        out=ke_[:, cb:cb + ncb, :], in0=pv, in1=egc[:, cb:cb + ncb, hh:hh + 1].to_broadcast([64, ncb, 128]),
                        op=ALU.mult), reads=["ps%d" % b, "egc" + sfx], writes=["ke" + sa])
                    ch.op("dve", lambda e, pv=pv, cb=cb, ncb=ncb: e.tensor_tensor(
                        out=kd_[:, cb:cb + ncb, :], in0=pv, in1=ekd[:, cb:cb + ncb, hh:hh + 1].to_broadcast([64, ncb, 128]),
                        op=ALU.mult), reads=["ps%d" % b, "ekd" + sfx], writes=["kd" + st])
                    b = bankB()

                    def f(e, b=b, cb=cb, ncb=ncb):
                        r = None
                        for j in range(ncb):
                            c = cb + j
                            r = e.matmul(ps[b][0:64, j * 128:(j + 1) * 128], lhsT=vT_[:, c * 64:(c + 1) * 64], rhs=identb[:, :],
                                         start=True, stop=True)
                        return r
                    ch.op("pe", f, reads=["vT" + sa, "identb"], writes=["ps%d" % b])
                    pv2 = ps[b][0:64, 0:ncb * 128].rearrange("p (c k) -> p c k", k=128)
                    ch.op("act", lambda e, pv2=pv2, cb=cb, ncb=ncb: e.activation(out=vtok_[:, cb:cb + ncb, :], in_=pv2, func=AF.Identity),
                          reads=["ps%d" % b], writes=["vtok" + sa])
                for cb in range(0, NC, 8):
                    nb = min(8, NC - cb)
                    W = nb * 64
                    ggh, ggl = gt["ggh"][qp], gt["ggl"][qp]
                    for (dst, dn, tab, src, sn_) in ((GMh, "GMh", TRI, ggh, "ggh"), (GMl, "GMl", TRI, ggl, "ggl")):
                        ch.op("pool", lambda e, dst=dst, tab=tab, src=src: e.tensor_tensor(
                            out=dst[:, 0:nb, :], in0=c64f[:, tab, :].unsqueeze(1).to_broadcast([64, nb, 64]),
                            in1=src[:, cb:cb + nb, hh:hh + 1].to_broadcast([64, nb, 64]), op=ALU.mult),
                            reads=["c64f", sn_ + sfx], writes=[dn])
                    ch.op("pool", lambda e, nb=nb, cb=cb: e.tensor_tensor(
                        out=Bd[:, 0:nb, :], in0=c64f[:, I64, :].unsqueeze(1).to_broadcast([64, nb, 64]),
                        in1=beta[:, cb:cb + nb, hh:hh + 1].to_broadcast([64, nb, 64]), op=ALU.mult),
                        reads=["c64f", "beta" + sfx], writes=["Bd"])
                    ch.op("pool", lambda e, nb=nb, cb=cb: e.tensor_tensor(
                        out=Dg[:, 0:nb, :], in0=c64f[:, I64, :].unsqueeze(1).to_broadcast([64, nb, 64]),
                        in1=egc[:, cb:cb + nb, hh:hh + 1].to_broadcast([64, nb, 64]), op=ALU.mult),
                        reads=["c64f", "egc" + sfx], writes=["Dg"])
                    GMhf = GMh[:, 0:nb, :].rearrange("p c k -> p (c k)")
                    GMlf = GMl[:, 0:nb, :].rearrange("p c k -> p (c k)")
                    Bdf = Bd[:, 0:nb, :].rearrange("p c k -> p (c k)")
                    Dgf = Dg[:, 0:nb, :].rearrange("p c k -> p (c k)")
                    nT8 = negT8[:, 0:nb, :].rearrange("p c k -> p (c k)")
                    bDT = bankB()

                    def f(e, b=bDT, W=W, nT8=nT8):
                        e.matmul(ps[b][0:64, 0:W], lhsT=c64b[:, SLM, :], rhs=GMhf, start=True, stop=False)
                        e.matmul(ps[b][0:64, 0:W], lhsT=c64b[:, SLM, :], rhs=GMlf, start=False, stop=False)
                        return e.matmul(ps[b][0:64, 0:W], lhsT=c64b[:, I64, :], rhs=nT8, start=False, stop=True)
                    ch.op("pe", f, reads=["c64b", "GMh", "GMl", "negT8"], writes=["ps%d" % bDT])
                    ch.op("act", lambda e, b=bDT, W=W: e.activation(out=decT[:, 0:W], in_=ps[b][0:64, 0:W], func=AF.Exp),
                          reads=["ps%d" % bDT], writes=["decT"])
                    bKK = bankB()

                    def f(e, b=bKK, cb=cb, nb=nb):
                        r = None
                        for j in range(nb):
                            c = cb + j
                            r = e.matmul(ps[b][0:64, j * 64:(j + 1) * 64], lhsT=kn_[:, c * 64:(c + 1) * 64],
                                         rhs=kn_[:, c * 64:(c + 1) * 64], start=True, stop=True)
                        return r
                    ch.op("pe", f, reads=["kn" + sa], writes=["ps%d" % bKK])
                    ch.op("dve", lambda e, b=bKK, W=W: e.tensor_tensor(out=t1[:, 0:W], in0=ps[b][0:64, 0:W], in1=decT[:, 0:W], op=ALU.mult),
                          reads=["ps%d" % bKK, "decT"], writes=["t1"])
                    bBR = bankB()
                    ch.op("pe", lambda e, b=bBR, W=W, Bdf=Bdf: e.matmul(ps[b][0:64, 0:W], lhsT=c64b[:, SLM, :], rhs=Bdf, start=True, stop=True),
                          reads=["c64b", "Bd"], writes=["ps%d" % bBR])
                    X0, XT0, R0 = X0s[a_], XT0s[a_], R0s[a_]
                    o0 = cb * 64
                    ch.op("dve", lambda e, b=bBR, W=W: e.scalar_tensor_tensor(
                        out=X0[:, o0:o0 + W], in0=t1[:, 0:W], scalar=-1.0, in1=ps[b][0:64, 0:W], op0=ALU.mult, op1=ALU.mult),
                        reads=["t1", "ps%d" % bBR], writes=["X0s" + sa])
                    bXT = bankB()

                    def f(e, b=bXT, nb=nb, o0=o0):
                        r = None
                        for j in range(nb):
                            r = e.matmul(ps[b][0:64, j * 64:(j + 1) * 64], lhsT=X0[:, o0 + j * 64: o0 + (j + 1) * 64],
                                         rhs=c64b[:, I64, :], start=True, stop=True)
                        return r
                    ch.op("pe", f, reads=["X0s" + sa, "c64b"], writes=["ps%d" % bXT])
                    ch.op("act", lambda e, b=bXT, W=W, o0=o0: e.activation(out=XT0[:, o0:o0 + W], in_=ps[b][0:64, 0:W], func=AF.Identity),
                          reads=["ps%d" % bXT], writes=["XT0s" + sa])
                    ch.op("dve", lambda e, nb=nb: e.tensor_tensor(
                        out=R0[:, o0:o0 + nb * 64].rearrange("p (c k) -> p c k", k=64),
                        in0=X0[:, o0:o0 + nb * 64].rearrange("p (c k) -> p c k", k=64),
                        in1=c64f[:, I64, :].unsqueeze(1).to_broadcast([64, nb, 64]), op=ALU.add),
                        reads=["X0s" + sa, "c64f"], writes=["R0s" + sa])
                    bQK = bankB()

                    def f(e, b=bQK, cb=cb, nb=nb):
                        r = None
                        for j in range(nb):
                            c = cb + j
                            r = e.matmul(ps[b][0:64, j * 64:(j + 1) * 64], lhsT=kn_[:, c * 64:(c + 1) * 64],
                                         rhs=qn_[:, c * 64:(c + 1) * 64], start=True, stop=True)
                        return r
                    ch.op("pe", f, reads=["kn" + sa, "qn" + sa], writes=["ps%d" % bQK])
                    ch.op("dve", lambda e, b=bQK, W=W, cb=cb: e.tensor_tensor(
                        out=attnT_[:, cb * 64: cb * 64 + W], in0=ps[b][0:64, 0:W], in1=decT[:, 0:W], op=ALU.mult),
                        reads=["ps%d" % bQK, "decT"], writes=["attnT" + st])
                    bEG = bankB()
                    ch.op("pe", lambda e, b=bEG, W=W, Dgf=Dgf: e.matmul(ps[b][:, 0:W], lhsT=onesb[0:64, :], rhs=Dgf, start=True, stop=True),
                          reads=["onesb", "Dg"], writes=["ps%d" % bEG])
                    ch.op("dve", lambda e, b=bEG, W=W, cb=cb: e.tensor_tensor(
                        out=qd_[:, cb * 64: cb * 64 + W], in0=ps[b][:, 0:W], in1=qn_[:, cb * 64: cb * 64 + W], op=ALU.mult),
                        reads=["ps%d" % bEG, "qn" + sa], writes=["qd" + st])
                return ch

            def stageB2(i):
                qi, hh = items[i]
                c0, NC = QUARTERS[qi]
                a_ = i % 2
                qp = qi % 2
                sfx = str(qp)
                beta = gt["beta"][qp]
                ke_, vtok_ = ke[a_], vtok[a_]
                WT_, U_ = WT[a_], U[a_]
                sa = str(a_)
                ch = Chain()
                for cb in range(0, NC, 8):
                    nb = min(8, NC - cb)
                    W = nb * 64
                    o0 = cb * 64
                    cur = 0
                    for lvl in range(1, 6):
                        if lvl == 1:
                            Xp, XTp, Rp = X0s[a_][:, o0:o0 + W], XT0s[a_][:, o0:o0 + W], R0s[a_][:, o0:o0 + W]
                            rXp, rXTp, rRp = "X0s" + sa, "XT0s" + sa, "R0s" + sa
                        else:
                            Xp, XTp, Rp = Xb[cur], XTb[cur], Rb[cur]
                            rXp, rXTp, rRp = "X%d" % cur, "XT%d" % cur, "R%d" % cur
                        Xn, XTn, Rn = Xb[1 - cur], XTb[1 - cur], Rb[1 - cur]
                        nn = str(1 - cur)
                        if lvl <= 4:
                            b = bankB2()

                            def f(e, b=b, nb=nb, Xp=Xp, XTp=XTp):
                                r = None
                                for j in range(nb):
                                    r = e.matmul(ps[b][0:64, j * 64:(j + 1) * 64], lhsT=XTp[:, j * 64:(j + 1) * 64],
                                                 rhs=Xp[:, j * 64:(j + 1) * 64], start=True, stop=True)
                                return r
                            ch.op("pe", f, reads=[rXp, rXTp], writes=["ps%d" % b])
                            ch.op("act", lambda e, b=b, W=W, Xn=Xn: e.activation(out=Xn[:, 0:W], in_=ps[b][0:64, 0:W], func=AF.Identity),
                                  reads=["ps%d" % b], writes=["X" + nn])
                        b = bankB2()

                        def f(e, b=b, nb=nb, Xp=Xp, XTp=XTp):
                            r = None
                            for j in range(nb):
                                r = e.matmul(ps[b][0:64, j * 64:(j + 1) * 64], lhsT=Xp[:, j * 64:(j + 1) * 64],
                                             rhs=XTp[:, j * 64:(j + 1) * 64], start=True, stop=True)
                            return r
                        ch.op("pe", f, reads=[rXp, rXTp], writes=["ps%d" % b])
                        ch.op("act", lambda e, b=b, W=W, XTn=XTn: e.activation(out=XTn[:, 0:W], in_=ps[b][0:64, 0:W], func=AF.Identity),
                              reads=["ps%d" % b], writes=["XT" + nn])
                        b = bankB2()

                        def f(e, b=b, nb=nb, XTn=XTn, Rp=Rp):
                            r = None
                            for j in range(nb):
                                r = e.matmul(ps[b][0:64, j * 64:(j + 1) * 64], lhsT=XTn[:, j * 64:(j + 1) * 64],
                                             rhs=Rp[:, j * 64:(j + 1) * 64], start=True, stop=True)
                            return r
                        ch.op("pe", f, reads=["XT" + nn, rRp], writes=["ps%d" % b])
                        if lvl < 5:
                            ch.op("dve", lambda e, b=b, W=W, Rn=Rn, Rp=Rp: e.tensor_tensor(
                                out=Rn[:, 0:W], in0=ps[b][0:64, 0:W], in1=Rp[:, 0:W], op=ALU.add),
                                reads=["ps%d" % b, rRp], writes=["R" + nn])
                        else:
                            ch.op("dve", lambda e, b=b, W=W, Rp=Rp: e.tensor_tensor(
                                out=R5[:, 0:W], in0=ps[b][0:64, 0:W], in1=Rp[:, 0:W], op=ALU.add),
                                reads=["ps%d" % b, rRp], writes=["R5"])
                            ch.op("dve", lambda e, nb=nb, cb=cb: e.tensor_tensor(
                                out=Y[:, cb * 64:(cb + nb) * 64].rearrange("p (c k) -> p c k", k=64),
                                in0=R5[:, 0:nb * 64].rearrange("p (c k) -> p c k", k=64),
                                in1=beta[:, cb:cb + nb, hh:hh + 1].to_broadcast([64, nb, 64]), op=ALU.mult),
                                reads=["R5", "beta" + sfx], writes=["Y"])
                        cur = 1 - cur
                    b = bankB2()

                    def f(e, b=b, cb=cb, nb=nb):
                        r = None
                        for j in range(nb):
                            c = cb + j
                            r = e.matmul(ps[b][:, j * 64:(j + 1) * 64], lhsT=ke_[:, c, :], rhs=Y[:, c * 64:(c + 1) * 64],
                                         start=True, stop=True)
                        return r
                    ch.op("pe", f, reads=["ke" + sa, "Y"], writes=["ps%d" % b])
                    ch.op("act", lambda e, b=b, W=W, cb=cb: e.activation(out=WT_[:, cb * 64: cb * 64 + W], in_=ps[b][:, 0:W], func=AF.Identity),
                          reads=["ps%d" % b], writes=["WT" + sa])
                    for c4 in range(0, nb, 4):
                        n4 = min(4, nb - c4)
                        b = bankB2()

                        def f(e, b=b, cb=cb, c4=c4, n4=n4):
                            r = None
                            for j in range(n4):
                                c = cb + c4 + j
                                r = e.matmul(ps[b][0:64, j * 128:(j + 1) * 128], lhsT=Y[:, c * 64:(c + 1) * 64], rhs=vtok_[:, c, :],
                                             start=True, stop=True)
                            return r
                        ch.op("pe", f, reads=["Y", "vtok" + sa], writes=["ps%d" % b])
                        ch.op("act", lambda e, b=b, cb=cb, c4=c4, n4=n4: e.activation(
                            out=U_[:, cb + c4: cb + c4 + n4, :], in_=ps[b][0:64, 0:n4 * 128].rearrange("p (c k) -> p c k", k=128),
                            func=AF.Identity), reads=["ps%d" % b], writes=["U" + sa])
                return ch

            def stageC(i):
                qi, hh = items[i]
                c0, NC = QUARTERS[qi]
                a_ = i % 2
                qp = qi % 2
                sa = str(a_)
                glast = gt["glast"][qp]
                t_ = i % 3
                st = str(t_)
                zs_, qd_, kd_, attnT_, WT_, U_ = zs[t_], qd[t_], kd[t_], attnT[t_], WT[a_], U[a_]
                ch = Chain()
                Sh = S[:, hh, :]
                Sn = "S%d" % hh
                ch.op("act", lambda e: e.activation(out=Sb[:, :], in_=Sh, func=AF.Identity), reads=["S", Sn], writes=["Sb"])
                for c in range(NC):
                    cg = c0 + c
                    v = vn[c % 2]
                    vname = "vn%d" % (c % 2)
                    ch.op("pe", lambda e, c=c: e.matmul(ps[6][0:64, 0:128], lhsT=WT_[:, c * 64:(c + 1) * 64], rhs=Sb[:, :], start=True, stop=True),
                          reads=["WT" + sa, "Sb"], writes=["ps6"])
                    ch.op("dve", lambda e, c=c, v=v: e.tensor_tensor(out=v[:, :], in0=U_[:, c, :], in1=ps[6][0:64, 0:128], op=ALU.subtract),
                          reads=["U" + sa, "ps6"], writes=[vname])
                    if cg >= 1:
                        oc = (cg - 1) % 8

                        def f(e, c=c, v=v, oc=oc):
                            e.matmul(ps[7][:, oc * 64:(oc + 1) * 64], lhsT=Sb[:, :], rhs=qd_[:, c * 64:(c + 1) * 64], start=True, stop=False)
                            return e.matmul(ps[7][:, oc * 64:(oc + 1) * 64], lhsT=v[:, :], rhs=attnT_[:, c * 64:(c + 1) * 64],
                                            start=False, stop=True)
                        ch.op("pe", f, reads=["Sb", "qd" + st, vname, "attnT" + st], writes=["ps7"])
                    ch.op("pe", lambda e, c=c, v=v: e.matmul(ps[5][:, 0:128], lhsT=kd_[:, c, :], rhs=v[:, :], start=True, stop=True),
                          reads=["kd" + st, vname], writes=["ps5"])
                    ch.op("dve", lambda e, c=c: e.scalar_tensor_tensor(
                        out=Sb[:, :], in0=Sh, scalar=glast[:, c, hh:hh + 1], in1=ps[5][:, 0:128], op0=ALU.mult, op1=ALU.add),
                        reads=["S", Sn, "glast%d" % qp, "ps5"], writes=["Sb"])
                    ch.op("dve", lambda e, c=c: e.scalar_tensor_tensor(
                        out=Sh, in0=Sh, scalar=glast[:, c, hh:hh + 1], in1=ps[5][:, 0:128], op0=ALU.mult, op1=ALU.add),
                        reads=["S", Sn, "glast%d" % qp, "ps5"], writes=[Sn])
                    if cg >= 1 and ((cg - 1) % 8 == 7 or c == NC - 1):
                        ng = (cg - 1) % 8 + 1
                        Wg = ng * 64
                        r0 = (cg - ng) * 64
                        z0 = (c - ng + 1) * 64
                        ch.op("act", lambda e, Wg=Wg: e.activation(out=osb[:, 0:Wg], in_=ps[7][:, 0:Wg], func=AF.Identity),
                              reads=["ps7"], writes=["osb"])
                        ch.op("pool", lambda e, Wg=Wg: e.tensor_tensor(out=osq[:, 0:Wg], in0=osb[:, 0:Wg], in1=osb[:, 0:Wg], op=ALU.mult),
                              reads=["osb"], writes=["osq"])
                        ch.op("pe", lambda e, Wg=Wg: e.matmul(ps[6][:, 0:Wg], lhsT=onesb[:, :], rhs=osq[:, 0:Wg], start=True, stop=True),
                              reads=["osq", "onesb"], writes=["ps6"])
                        ch.op("act", lambda e, Wg=Wg: e.activation(out=orn[:, 0:Wg], in_=ps[6][:, 0:Wg], func=AF.Ln,
                                                                   bias=RMS_EPS, scale=1.0 / 128.0),
                              reads=["ps6"], writes=["orn"])
                        ch.op("act", lambda e, Wg=Wg: e.activation(out=orn[:, 0:Wg], in_=orn[:, 0:Wg], func=AF.Exp, scale=-0.5),
                              reads=["orn"], writes=["orn"])
                        ch.op("dve", lambda e, Wg=Wg: e.scalar_tensor_tensor(
                            out=osb[:, 0:Wg], in0=osb[:, 0:Wg], scalar=gnw[:, 0:1], in1=orn[:, 0:Wg], op0=ALU.mult, op1=ALU.mult),
                            reads=["osb", "gnw", "orn"], writes=["osb"])
                        ch.op("dve", lambda e, Wg=Wg, r0=r0, z0=z0: e.tensor_tensor(
                            out=yaT[:, hh, r0:r0 + Wg], in0=osb[:, 0:Wg], in1=zs_[:, z0:z0 + Wg], op=ALU.mult),
                            reads=["osb", "zs" + st], writes=["yaT"])
                return ch

            nI = len(items)
            sched = Sched()
            for step in range(nI + 3):
                lists = []
                if step < nI:
                    la = stageA(step)
                    sched.run([la[0]], P.add_item)
                    lists.extend(la[1:])
                if 0 <= step - 1 < nI:
                    lists.append(stageB1(step - 1))
                if 0 <= step - 2 < nI:
                    lists.append(stageB2(step - 2))
                if 0 <= step - 3 < nI:
                    lists.append(stageC(step - 3))
                sched.run(lists, P.add_item)
            ld("pool", wk0_top[:], wkv[0], "ld_wk0", "wk0")
            ld("pool", wv0_top[:], wkv[1], "ld_wv0", "wv0")
        marks["G"] = len(P.ops)
        P.fence()
        ybT = sbuf(top, "ybT", [128, 8, SEQ], BF16)
        wf0_top = [sbuf(top, "wf%d" % i, [128, 8, 128], BF16) for i in range(4)]

        with contextlib.ExitStack() as s_:
            biasP = sbuf(s_, "biasP", [128, 4, 512], BF16)
            biasO = sbuf(s_, "biasO", [128, 4, 512], BF16)
            lm = sbuf(s_, "lm", [3, 16, 16], BF16)
            rm = sbuf(s_, "rm", [3, 4, 512], BF16)
            skf = sbuf(s_, "skf", [33, 4, 512], F32)
            esk = sbuf(s_, "esk", [33, 4, 512], BF16)
            wk_ = [wk0_top, sbuf(s_, "wk1", [128, 8, 128], BF16)]
            wv_ = [wv0_top, sbuf(s_, "wv1", [128, 8, 128], BF16)]
            wq2 = [sbuf(s_, "wq2_%d" % i, [128, 8, 128], BF16) for i in range(4)]
            wz2 = [sbuf(s_, "wz2_%d" % i, [128, 8, 128], BF16) for i in range(4)]
            kTr = sbuf(s_, "kTr", [128, SEQ], BF16)
            kTm = sbuf(s_, "kTm", [128, 16], BF16)
            vtk = sbuf(s_, "vtk", [128, 16, 128], BF16)
            vmt = sbuf(s_, "vmt", [33, 128], BF16)
            qTh = [sbuf(s_, "qTh%d" % i, [128, 2, SEQ], BF16) for i in range(2)]
            zsb = sbuf(s_, "zsb", [128, 2, SEQ], BF16)
            PTp = [sbuf(s_, "PTp%d" % i, [128, 512], BF16) for i in range(2)]
            PTo = [sbuf(s_, "PTo%d" % i, [128, 512], BF16) for i in range(2)]
            PTm = [sbuf(s_, "PTm%d" % i, [33, 512], BF16) for i in range(2)]
            rden = sbuf(s_, "rden", [128, 512], F32)
            tmpo = sbuf(s_, "tmpo", [128, 2, 128], F32)

            ld("pool", biasP[:], biasP_d, "ld_c", "biasP")
            ld("pool", biasO[:], biasO_d, "ld_c", "biasO")
            ld("pool", lm[:], lm_d, "ld_c", "lm")
            ld("pool", rm[:], rm_d, "ld_c", "rm")
            ld("sp", skf[32:33, :, :], sk_d, "ld_c", "skf")
            P.op("act", lambda e: e.activation(out=esk[32:33, :, :], in_=skf[32:33, :, :], func=AF.Exp), reads=["skf"], writes=["esk"])
            P.op("dve", lambda e: e.memset(vmt[:, :], 0.0), writes=["vmt"])
            for pp_ in range(2):
                P.op("dve", lambda e, pp_=pp_: e.memset(PTm[pp_][:, :], 0.0), writes=["PTm%d" % pp_])
            P.op("pool", lambda e: e.memset(qTh[0][64:128, :, :], 0.0), writes=["qTz0"])
            P.op("pool", lambda e: e.memset(qTh[1][0:64, :, :], 0.0), writes=["qTz1"])

            def swa_wload(kh):
                w = kh % 2
                if kh > 0:
                    ld("pool", wk_[w][:], wkv[kh * 2], "ld_wk%d" % w, "wk%d" % w)
                    ld("pool", wv_[w][:], wkv[kh * 2 + 1], "ld_wv%d" % w, "wv%d" % w)
                for qt in range(2):
                    ld("pool", wq2[w * 2 + qt][:], wqz[kh * 2 + qt], "ld_wq%d" % (w * 2 + qt), "wq%d" % (w * 2 + qt))
                    ld("pool", wz2[w * 2 + qt][:], wqz[8 + kh * 2 + qt], "ld_wz%d" % (w * 2 + qt), "wz%d" % (w * 2 + qt))

            swa_wload(0)
            for kh in range(4):
                w = kh % 2
                for blk in range(4):
                    b = blk % 2

                    def f(e, b=b, blk=blk, w=w):
                        r = None
                        for kc in range(8):
                            r = e.matmul(ps[b][:, 0:512], lhsT=wk_[w][:, kc, :], rhs=hT[:, kc, XCOL + blk * 512: XCOL + (blk + 1) * 512],
                                         start=(kc == 0), stop=(kc == 7))
                        return r
                    P.op("pe", f, reads=["hT", "wk%d" % w], writes=["ps%d" % b])
                    P.op("act", lambda e, b=b, blk=blk: e.activation(out=kTr[:, blk * 512:(blk + 1) * 512], in_=ps[b][:, 0:512], func=AF.Identity),
                         reads=["ps%d" % b], writes=["kTr"])

                def f(e, w=w):
                    r = None
                    for kc in range(8):
                        r = e.matmul(ps[0][:, 0:16], lhsT=wk_[w][:, kc, :], rhs=hT[:, kc, MCOL:XCOL], start=(kc == 0), stop=(kc == 7))
                    return r
                P.op("pe", f, reads=["hT", "wk%d" % w], writes=["ps0"])
                P.op("act", lambda e: e.activation(out=kTm[:, :], in_=ps[0][:, 0:16], func=AF.Identity), reads=["ps0"], writes=["kTm"])
                for m4 in range(4):
                    b = m4 % 2

                    def f(e, b=b, m4=m4, w=w):
                        r = None
                        for i in range(4):
                            m = m4 * 4 + i
                            for kc in range(8):
                                r = e.matmul(ps[b][:, i * 128:(i + 1) * 128], lhsT=hT[:, kc, XCOL + m * 128: XCOL + (m + 1) * 128],
                                             rhs=wv_[w][:, kc, :], start=(kc == 0), stop=(kc == 7))
                        return r
                    P.op("pe", f, reads=["hT", "wv%d" % w], writes=["ps%d" % b])
                    P.op("act", lambda e, b=b, m4=m4: e.activation(out=vtk[:, m4 * 4:(m4 + 1) * 4, :],
                                                                   in_=ps[b][:, 0:512].rearrange("p (c k) -> p c k", k=128), func=AF.Identity),
                         reads=["ps%d" % b], writes=["vtk"])

                def f(e, w=w):
                    r = None
                    for kc in range(8):
                        r = e.matmul(ps[1][0:16, 0:128], lhsT=hT[:, kc, MCOL:XCOL], rhs=wv_[w][:, kc, :], start=(kc == 0), stop=(kc == 7))
                    return r
                P.op("pe", f, reads=["hT", "wv%d" % w], writes=["ps1"])
                P.op("act", lambda e: e.activation(out=vmt[0:16, :], in_=ps[1][0:16, 0:128], func=AF.Identity), reads=["ps1"], writes=["vmt"])
                for qt in range(2):
                    for blk in range(4):
                        b = blk % 2

                        def f(e, b=b, blk=blk, wi=w * 2 + qt):
                            r = None
                            for kc in range(8):
                                r = e.matmul(ps[b][:, 0:512], lhsT=wq2[wi][:, kc, :], rhs=hT[:, kc, XCOL + blk * 512: XCOL + (blk + 1) * 512],
                                             start=(kc == 0), stop=(kc == 7))
                            return r
                        P.op("pe", f, reads=["hT", "wq%d" % (w * 2 + qt)], writes=["ps%d" % b])
                        P.op("act", lambda e, b=b, blk=blk, qt=qt: e.activation(out=qTh[0][0:64, qt, blk * 512:(blk + 1) * 512],
                                                                              in_=ps[b][0:64, 0:512], func=AF.Identity),
                             reads=["ps%d" % b], writes=["qT"])
                        P.op("dve", lambda e, b=b, blk=blk, qt=qt: e.tensor_copy(out=qTh[1][64:128, qt, blk * 512:(blk + 1) * 512],
                                                                               in_=ps[b][64:128, 0:512]),
                             reads=["ps%d" % b], writes=["qTb"])
                    for blk in range(4):
                        b = blk % 2

                        def f(e, b=b, blk=blk, wi=w * 2 + qt):
                            r = None
                            for kc in range(8):
                                r = e.matmul(ps[b][:, 0:512], lhsT=wz2[wi][:, kc, :], rhs=hT[:, kc, XCOL + blk * 512: XCOL + (blk + 1) * 512],
                                             start=(kc == 0), stop=(kc == 7))
                            return r
                        P.op("pe", f, reads=["hT", "wz%d" % (w * 2 + qt)], writes=["ps%d" % b])
                        P.op("act", lambda e, b=b, blk=blk, qt=qt: e.activation(out=zsb[:, qt, blk * 512:(blk + 1) * 512], in_=ps[b][:, 0:512],
                                                                              func=AF.Silu),
                             reads=["ps%d" % b], writes=["zsb"])
                for pp_ in range(2):
                    P.op("act", lambda e, pp_=pp_: e.activation(out=PTm[pp_][32:33, :], in_=esk[32:33, kh, :], func=AF.Identity),
                         reads=["esk"], writes=["PTm%d" % pp_])

                def stage_s(m):
                    pp = m % 2
                    bP, bO, bM = (0, 2, 4) if pp == 0 else (1, 3, 7)

                    def scores(e, bank, keys, nkeys):
                        r = None
                        for qt in range(2):
                            for half in range(2):
                                r = e.matmul(ps[bank][0:nkeys, (2 * qt + half) * 128:(2 * qt + half + 1) * 128],
                                             lhsT=keys, rhs=qTh[half][:, qt, m * 128:(m + 1) * 128],
                                             start=(qt == 0 and half == 0), stop=False)
                        return r
                    qreads = ["qT", "qTb", "qTz0", "qTz1"]
                    if m >= 1:
                        def f(e):
                            scores(e, bP, kTr[:, (m - 1) * 128: m * 128], 128)
                            return e.matmul(ps[bP][:, 0:512], lhsT=identb[:, :], rhs=biasP[:, kh, :], start=False, stop=True)
                        P.op("pe", f, reads=["kTr", "identb", "biasP"] + qreads, writes=["ps%d" % bP])
                        P.op("act", lambda e: e.activation(out=PTp[pp][:, :], in_=ps[bP][:, 0:512], func=AF.Exp, scale=0.125),
                             reads=["ps%d" % bP], writes=["PTp%d" % pp])

                    def f(e):
                        scores(e, bO, kTr[:, m * 128:(m + 1) * 128], 128)
                        return e.matmul(ps[bO][:, 0:512], lhsT=identb[:, :], rhs=biasO[:, kh, :], start=False, stop=True)
                    P.op("pe", f, reads=["kTr", "identb", "biasO"] + qreads, writes=["ps%d" % bO])
                    P.op("act", lambda e: e.activation(out=PTo[pp][:, :], in_=ps[bO][:, 0:512], func=AF.Exp, scale=0.125),
                         reads=["ps%d" % bO], writes=["PTo%d" % pp])

                    def f(e):
                        scores(e, bM, kTm[:, 0:16], 16)
                        return e.matmul(ps[bM][0:16, 0:512], lhsT=lm[0:3, m, :], rhs=rm[0:3, kh, :], start=False, stop=True)
                    P.op("pe", f, reads=["kTm", "lm", "rm"] + qreads, writes=["ps%d" % bM])
                    P.op("act", lambda e: e.activation(out=PTm[pp][0:16, :], in_=ps[bM][0:16, 0:512], func=AF.Exp, scale=0.125),
                         reads=["ps%d" % bM], writes=["PTm%d" % pp])

                def stage_r(m):
                    pp = m % 2

                    def f(e):
                        first = True
                        if m >= 1:
                            e.matmul(ps[5][:, 0:512], lhsT=vtk[:, m - 1, :], rhs=PTp[pp][:, :], start=True, stop=False)
                            first = False
                        e.matmul(ps[5][:, 0:512], lhsT=vtk[:, m, :], rhs=PTo[pp][:, :], start=first, stop=False)
                        e.matmul(ps[5][:, 0:512], lhsT=vmt[0:33, :], rhs=PTm[pp][0:33, :], start=False, stop=True)
                        first = True
                        if m >= 1:
                            e.matmul(ps[6][:, 0:512], lhsT=onesb[:, :], rhs=PTp[pp][:, :], start=True, stop=False)
                            first = False
                        e.matmul(ps[6][:, 0:512], lhsT=onesb[:, :], rhs=PTo[pp][:, :], start=first, stop=False)
                        return e.matmul(ps[6][:, 0:512], lhsT=onesb[0:33, :], rhs=PTm[pp][0:33, :], start=False, stop=True)
                    P.op("pe", f, reads=["vtk", "vmt", "PTp%d" % pp, "PTo%d" % pp, "PTm%d" % pp, "onesb"], writes=["ps5", "ps6"])
                    P.op("act", lambda e: e.activation(out=rden[:, :], in_=ps[6][:, 0:512], func=AF.Ln), reads=["ps6"], writes=["rden"])
                    P.op("act", lambda e: e.activation(out=rden[:, :], in_=rden[:, :], func=AF.Exp, scale=-1.0), reads=["rden"], writes=["rden"])
                    for half in range(2):
                        rows = slice(half * 64, (half + 1) * 64)
                        cs = slice(half * 128, (half + 1) * 128)

                        def f(e, rows=rows, cs=cs):
                            o3 = ps[5][rows, 0:512].rearrange("p (a b) -> p a b", a=2)[:, :, cs]
                            r3 = rden[rows, :].rearrange("p (a b) -> p a b", a=2)[:, :, cs]
                            return e.tensor_tensor(out=tmpo[rows, :, :], in0=o3, in1=r3, op=ALU.mult)
                        P.op("dve", f, reads=["ps5", "rden"], writes=["tmpo%d" % half])
                        P.op("pool", lambda e, rows=rows: e.tensor_tensor(
                            out=ybT[rows, 2 * kh:2 * kh + 2, m * 128:(m + 1) * 128], in0=tmpo[rows, :, :],
                            in1=zsb[rows, :, m * 128:(m + 1) * 128], op=ALU.mult),
                            reads=["tmpo%d" % half, "zsb"], writes=["ybT"])

                if kh + 1 < 4:
                    swa_wload(kh + 1)
                stage_s(0)
                for m in range(16):
                    if m + 1 < 16:
                        stage_s(m + 1)
                    stage_r(m)
            for i_ in range(4):
                src_ = (wpa, wpb, wgate, wgate)[i_][0 if i_ < 3 else 8]
                ld("pool", wf0_top[i_][:], src_, "ld_wf%d" % i_, "wf%d" % i_)
        marks["S"] = len(P.ops)
        P.fence()

        if debug:
            with contextlib.ExitStack() as d_:
                dtmp = sbuf(d_, "dtmp", [128, 8, SEQ], F32)
                P.op("dve", lambda e: e.tensor_copy(out=dtmp[:], in_=yaT[:]), reads=["yaT"], writes=["dtmp"])
                P.dma("sp", lambda e, s: e.dma_start(out=dbg_ya, in_=dtmp[:]).then_inc(s, 16), "st_dbg", reads=["dtmp"])
                P.op("dve", lambda e: e.tensor_copy(out=dtmp[:], in_=ybT[:]), reads=["ybT"], writes=["dtmp"])
                P.dma("sp", lambda e, s: e.dma_start(out=dbg_yb, in_=dtmp[:]).then_inc(s, 16), "st_dbg", reads=["dtmp"])
            P.fence()

        with contextlib.ExitStack() as f_:
            mixT = sbuf(f_, "mixT", [128, 8, SEQ], BF16)
            wf = wf0_top + [sbuf(f_, "wf%d" % i, [128, 8, 128], BF16) for i in range(4, 8)]
            wo = sbuf(f_, "wo", [128, 8, D], BF16)
            lnw = sbuf(f_, "lnw", [128, D], F32)
            lnb = sbuf(f_, "lnb", [128, D], F32)
            sga = sbuf(f_, "sga", [128, 512], F32)
            sgb = sbuf(f_, "sgb", [128, 512], F32)
            m1 = sbuf(f_, "m1", [128, 512], F32)
            m2 = sbuf(f_, "m2", [128, 512], F32)
            xt = [sbuf(f_, "xt%d" % i, [128, D], F32) for i in range(3)]
            res = [sbuf(f_, "res%d" % i, [128, D], F32) for i in range(3)]
            st3 = [sbuf(f_, "st3_%d" % i, [128, 8], F32) for i in range(3)]

            ld("pool", wo[:], wout, "ld_wo", "wo")
            ld("sp", lnw[:], lnw_d, "ld_c", "lnw")
            ld("sp", lnb[:], lnb_d, "ld_c", "lnb")
            srcs = (wpa, wpb, wgate, wgate)
            def fin_wload(dt_):
                sset = (dt_ % 2) * 4
                for i in range(4):
                    src = srcs[i][dt_ if i < 3 else 8 + dt_]
                    ld("pool", wf[sset + i][:], src, "ld_wf%d" % (sset + i), "wf%d" % (sset + i))

            for dt_ in range(8):
                sset = (dt_ % 2) * 4
                for tb in range(4):
                    if tb == 1 and dt_ + 1 < 8:
                        fin_wload(dt_ + 1)
                    tsl = slice(tb * 512, (tb + 1) * 512)
                    hsl = slice(XCOL + tb * 512, XCOL + (tb + 1) * 512)
                    banks = (0, 1, 2, 3) if tb % 2 == 0 else (4, 5, 6, 7)
                    rhs_l = (lambda kc, tsl=tsl: yaT[:, kc, tsl], lambda kc, tsl=tsl: ybT[:, kc, tsl],
                             lambda kc, hsl=hsl: hT[:, kc, hsl], lambda kc, hsl=hsl: hT[:, kc, hsl])
                    rd = ("yaT", "ybT", "hT", "hT")
                    for i in range(4):
                        def f(e, i=i, bk=banks[i], rf=rhs_l[i], wi=sset + i):
                            r = None
                            for kc in range(8):
                                r = e.matmul(ps[bk][:, 0:512], lhsT=wf[wi][:, kc, :], rhs=rf(kc), start=(kc == 0), stop=(kc == 7))
                            return r
                        P.op("pe", f, reads=[rd[i], "wf%d" % (sset + i)], writes=["ps%d" % banks[i]])
                    P.op("act", lambda e, bk=banks[2], dt_=dt_: e.activation(out=sga[:, :], in_=ps[bk][:, 0:512], func=AF.Sigmoid,
                                                                            bias=bg[:, dt_:dt_ + 1], scale=1.0),
                         reads=["ps%d" % banks[2], "bg"], writes=["sga"])
                    P.op("act", lambda e, bk=banks[3], dt_=dt_: e.activation(out=sgb[:, :], in_=ps[bk][:, 0:512], func=AF.Sigmoid,
                                                                            bias=bg[:, 8 + dt_:9 + dt_], scale=1.0),
                         reads=["ps%d" % banks[3], "bg"], writes=["sgb"])
                    P.op("dve", lambda e, bk=banks[0]: e.tensor_tensor(out=m1[:, :], in0=ps[bk][:, 0:512], in1=sga[:, :], op=ALU.mult),
                         reads=["ps%d" % banks[0], "sga"], writes=["m1"])
                    P.op("dve", lambda e, bk=banks[1]: e.tensor_tensor(out=m2[:, :], in0=ps[bk][:, 0:512], in1=sgb[:, :], op=ALU.mult),
                         reads=["ps%d" % banks[1], "sgb"], writes=["m2"])
                    P.op("pool", lambda e, dt_=dt_, tsl=tsl: e.tensor_tensor(out=mixT[:, dt_, tsl], in0=m1[:, :], in1=m2[:, :], op=ALU.add),
                         reads=["m1", "m2"], writes=["mixT"])
            def lnS1(tt):
                k3 = tt % 3
                pp = tt % 2
                x_, r_ = xt[k3], res[k3]
                ch = Chain()
                ch.dma("sp", lambda e, s: e.dma_start(out=x_[:], in_=xtok[tt * 128:(tt + 1) * 128, :]).then_inc(s, 16),
                       "ld_x%d" % k3, writes=["xt%d" % k3])
                bks = (0, 1) if pp == 0 else (2, 3)
                for hb in range(2):
                    def f(e, hb=hb, bk=bks[hb]):
                        r = None
                        for kc in range(8):
                            r = e.matmul(ps[bk][:, 0:512], lhsT=mixT[:, kc, tt * 128:(tt + 1) * 128], rhs=wo[:, kc, hb * 512:(hb + 1) * 512],
                                         start=(kc == 0), stop=(kc == 7))
                        return r
                    ch.op("pe", f, reads=["mixT", "wo"], writes=["ps%d" % bks[hb]])
                    ch.op("dve", lambda e, hb=hb, bk=bks[hb]: e.scalar_tensor_tensor(
                        out=r_[:, hb * 512:(hb + 1) * 512], in0=x_[:, hb * 512:(hb + 1) * 512], scalar=DEEPNORM_ALPHA,
                        in1=ps[bk][:, 0:512], op0=ALU.mult, op1=ALU.add),
                        reads=["xt%d" % k3, "ps%d" % bks[hb]], writes=["res%d" % k3])
                return ch

            def lnS2(tt):
                k3 = tt % 3
                x_, r_, s_ = xt[k3], res[k3], st3[k3]
                sn = "st%d" % k3
                ch = Chain()
                ch.op("dve", lambda e: e.reduce_sum(out=s_[:, 0:1], in_=r_[:, :], axis=AX.X), reads=["res%d" % k3], writes=[sn + "a"])
                ch.op("act", lambda e: e.activation(out=x_[:, :], in_=r_[:, :], func=AF.Square),
                      reads=["res%d" % k3], writes=["xt%d" % k3])
                ch.op("dve", lambda e: e.reduce_sum(out=s_[:, 1:2], in_=x_[:, :], axis=AX.X), reads=["xt%d" % k3], writes=[sn + "b"])
                ch.op("dve", lambda e: e.tensor_scalar(out=s_[:, 2:3], in0=s_[:, 0:1], scalar1=1.0 / D, scalar2=None, op0=ALU.mult),
                      reads=[sn + "a"], writes=[sn + "c"])
                ch.op("dve", lambda e: e.tensor_tensor(out=s_[:, 3:4], in0=s_[:, 2:3], in1=s_[:, 2:3], op=ALU.mult),
                      reads=[sn + "c"], writes=[sn + "d"])
                ch.op("dve", lambda e: e.scalar_tensor_tensor(out=s_[:, 4:5], in0=s_[:, 1:2], scalar=1.0 / D, in1=s_[:, 3:4],
                                                              op0=ALU.mult, op1=ALU.subtract),
                      reads=[sn + "b", sn + "d"], writes=[sn + "e"])
                ch.op("act", lambda e: e.activation(out=s_[:, 5:6], in_=s_[:, 4:5], func=AF.Sqrt, bias=LN_EPS, scale=1.0),
                      reads=[sn + "e"], writes=[sn + "f"])
                ch.op("dve", lambda e: e.reciprocal(out=s_[:, 6:7], in_=s_[:, 5:6]), reads=[sn + "f"], writes=[sn + "g"])
                return ch

            def lnS3(tt):
                k3 = tt % 3
                r_, s_ = res[k3], st3[k3]
                sn = "st%d" % k3
                ch = Chain()
                ch.op("dve", lambda e: e.tensor_scalar(out=r_[:, :], in0=r_[:, :], scalar1=s_[:, 2:3], scalar2=s_[:, 6:7],
                                                       op0=ALU.subtract, op1=ALU.mult),
                      reads=["res%d" % k3, sn + "c", sn + "g"], writes=["res%d" % k3])
                ch.op("pool", lambda e: e.tensor_tensor(out=r_[:, :], in0=r_[:, :], in1=lnw[:, :], op=ALU.mult),
                      reads=["res%d" % k3, "lnw"], writes=["res%d" % k3])
                ch.op("pool", lambda e: e.tensor_tensor(out=r_[:, :], in0=r_[:, :], in1=lnb[:, :], op=ALU.add),
                      reads=["res%d" % k3, "lnb"], writes=["res%d" % k3])
                ch.dma("sp", lambda e, s: e.dma_start(out=out_d[tt * 128:(tt + 1) * 128, :], in_=r_[:]).then_inc(s, 16),
                       "st_out%d" % k3, reads=["res%d" % k3])
                return ch

            for step in range(16 + 2):
                lists = []
                if step < 16:
                    lists.append(lnS1(step))
                if 0 <= step - 1 < 16:
                    lists.append(lnS2(step - 1))
                if 0 <= step - 2 < 16:
                    lists.append(lnS3(step - 2))
                for it in interleave(lists):
                    P.add_item(it)
        _MARKS.update(marks)
        _MARKS["end"] = len(P.ops)
        lim = None
        if stop is not None:
            lim = marks[stop] if stop in marks else int(stop)
        P.emit(nc, limit=lim)
    return nc


def _tile_w(w):
    n = w.shape[1] // 128
    return np.ascontiguousarray(w.reshape(8, 128, n, 128).transpose(2, 1, 0, 3))


def _const_tables():
    t = np.arange(64)
    c64 = np.zeros((64, 5, 64), np.float32)
    c64[:, 0, :] = (t[:, None] <= t[None, :])
    c64[:, 1, :] = (t[:, None] > t[None, :])
    c64[:, 2, :] = np.eye(64)
    c64[:, 3, :] = np.where(t[None, :] < t[:, None], NEG, 0.0)
    c64[:, 4, :] = np.where(t[None, :] >= t[:, None], NEG, 0.0)
    slopes = np.exp2(-8.0 * np.arange(1, 17, dtype=np.float64) / 16.0)
    k = np.arange(128)[:, None]
    i = np.arange(128)[None, :]
    biasP = np.zeros((128, 4, 512), np.float64)
    biasO = np.zeros((128, 4, 512), np.float64)
    rm = np.zeros((3, 4, 512), np.float64)
    for kh in range(4):
        for qt in range(2):
            for half in range(2):
                h = 4 * kh + 2 * qt + half
                cs = slice((2 * qt + half) * 128, (2 * qt + half + 1) * 128)
                bp = -8.0 * slopes[h] * (128 + i - k)
                bp = np.where((k < 64) & (i >= 64), 8.0 * NEG, bp)
                biasP[:, kh, cs] = bp
                bo = -8.0 * slopes[h] * np.abs(i - k)
                bo = np.where((k >= 64) & (i < 64), 8.0 * NEG, bo)
                biasO[:, kh, cs] = bo
                rm[0, kh, cs] = -8.0 * slopes[h] * (16 + np.arange(128))
                rm[1, kh, cs] = 8.0 * slopes[h]
                rm[2, kh, cs] = -8.0 * slopes[h] * 128.0
    lm = np.zeros((3, 16, 16), np.float32)
    lm[0] = 1.0
    lm[1] = np.arange(16)[None, :]
    lm[2] = np.arange(16)[:, None]
    return (c64, biasP.astype(np.float32), biasO.astype(np.float32), lm, rm.astype(np.float32))


_CACHE = {}


def _get_program(debug=False):
    if debug not in _CACHE:
        _CACHE[debug] = build_program(debug)
    return _CACHE[debug]


def _prepare(x, meta_tokens, w_in, b_gate, conv_w, a_log, dt_bias, gdn_norm_w, sinks,
             w_proj_a, w_proj_b, w_out, ln_w, ln_b):
    f = np.float32
    w = np.asarray(w_in, f)[0]
    o = 0
    seg = {}
    for name, wd in zip(("qa", "ka", "va", "za", "be", "de", "qb", "kb", "vb", "zb", "ga", "gb"),
                        (1024, 1024, 1024, 1024, 8, 8, 1024, 256, 256, 1024, 1024, 1024)):
        seg[name] = w[:, o:o + wd]
        o += wd
    tq, tk, tv, tz = (_tile_w(seg[n]) for n in ("qa", "ka", "va", "za"))
    wg = np.ascontiguousarray(np.stack([tq, tk, tv, tz], axis=1).reshape(32, 128, 8, 128))
    wbd = np.ascontiguousarray(np.concatenate([seg["be"], seg["de"]], axis=1).reshape(8, 128, 16).transpose(1, 0, 2))
    kv = []
    for kh in range(4):
        kc_ = seg["kb"][:, kh * 64:(kh + 1) * 64]
        vc_ = seg["vb"][:, kh * 64:(kh + 1) * 64]
        kv.append(np.concatenate([kc_, kc_], axis=1))
        kv.append(np.concatenate([vc_, vc_], axis=1))
    wkv = _tile_w(np.concatenate(kv, axis=1))
    wqz = np.concatenate([_tile_w(seg["qb"]), _tile_w(seg["zb"])], axis=0)
    wgate = np.concatenate([_tile_w(seg["ga"]), _tile_w(seg["gb"])], axis=0)
    wpa = _tile_w(np.asarray(w_proj_a, f)[0])
    wpb = _tile_w(np.asarray(w_proj_b, f)[0])
    wout = np.ascontiguousarray(np.asarray(w_out, f)[0].reshape(8, 128, D).transpose(1, 0, 2))
    c64, biasP, biasO, lm, rm = _const_tables()
    cw = np.ascontiguousarray(np.asarray(conv_w, f)[0].T.reshape(24, 128, 4).transpose(1, 0, 2))
    sk = np.zeros((1, 4, 512), f)
    sv = np.asarray(sinks, f)[0]
    for kh in range(4):
        for qt in range(2):
            for half in range(2):
                sk[0, kh, (2 * qt + half) * 128:(2 * qt + half + 1) * 128] = sv[4 * kh + 2 * qt + half]
    shared = {
        "metaT": np.ascontiguousarray(np.asarray(meta_tokens, f).T),
        "wg": wg, "wbd": wbd, "wkv": wkv, "wqz": wqz, "wgate": wgate, "wpa": wpa, "wpb": wpb, "wout": wout,
        "c64": c64, "ident": np.eye(128, dtype=f), "cw": cw,
        "gnw": np.ascontiguousarray(np.asarray(gdn_norm_w, f)[0].reshape(128, 1)),
        "bg": np.ascontiguousarray(np.asarray(b_gate, f)[0].reshape(16, 128).T),
        "alog": np.ascontiguousarray(np.broadcast_to(np.asarray(a_log, f)[0][None, :], (64, 8))),
        "dtb": np.ascontiguousarray(np.broadcast_to(np.asarray(dt_bias, f)[0][None, :], (64, 8))),
        "sk": sk,
        "lnw": np.ascontiguousarray(np.broadcast_to(np.asarray(ln_w, f)[0][None, :], (128, D))),
        "lnb": np.ascontiguousarray(np.broadcast_to(np.asarray(ln_b, f)[0][None, :], (128, D))),
        "biasP": biasP, "biasO": biasO, "lm": lm, "rm": rm,
    }
    xs = np.asarray(x, f)
    in_maps = []
    for b in range(xs.shape[0]):
        m = dict(shared)
        m["xT"] = np.ascontiguousarray(xs[b].T)
        m["xtok"] = np.ascontiguousarray(xs[b])
        in_maps.append(m)
    return in_maps


def kernel(x, meta_tokens, w_in, b_gate, conv_w, a_log, dt_bias, gdn_norm_w, sinks,
           w_proj_a, w_proj_b, w_out, ln_w, ln_b, _debug=False):
    in_maps = _prepare(x, meta_tokens, w_in, b_gate, conv_w, a_log, dt_bias, gdn_norm_w, sinks,
                       w_proj_a, w_proj_b, w_out, ln_w, ln_b)
    nc = _get_program(_debug)
    res = run_bass_kernel_spmd(nc, in_maps, core_ids=list(range(len(in_maps))))
    out = np.stack([np.asarray(r["out"], np.float32) for r in res.results], axis=0)
    if _debug:
        return out, res
    return out
```

```python
import contextlib
import numpy as np
import concourse.bass as bass
import concourse.mybir as mybir
from concourse.bass_utils import run_bass_kernel_spmd

F32 = mybir.dt.float32
BF16 = mybir.dt.bfloat16
AF = mybir.ActivationFunctionType
ALU = mybir.AluOpType
AX = mybir.AxisListType

ENGS = ("pe", "act", "dve", "pool", "sp")


class Op:
    __slots__ = ("eng", "fn", "deps", "sig", "dma_sem", "dma_cnt", "idx", "ndma")

    def __init__(self, eng, fn):
        self.eng = eng
        self.fn = fn
        self.deps = set()
        self.sig = None
        self.dma_sem = None
        self.ndma = 0


class _Dummy:
    def then_inc(self, *a, **k):
        return self


class _Rec:
    def __init__(self):
        self.calls = []

    def __getattr__(self, name):
        def m(*a, **k):
            self.calls.append((name, a, k))
            return _Dummy()
        return m


def _free_size(ap):
    n = 1
    for d in tuple(ap.shape)[1:]:
        n *= int(d)
    return n


def _cost_us(eng, calls):
    t = 0.0
    for (nm, a, k) in calls:
        out = k.get("out", a[0] if a else None)
        fs = _free_size(out) if out is not None else 64
        if eng == "pe":
            rhs = k.get("rhs", None)
            f32 = rhs is not None and rhs.dtype == F32
            t += max(0.075, fs * (0.0024 if f32 else 0.00062))
        elif eng == "act":
            t += 0.15 + 0.00095 * fs
        elif eng == "dve":
            t += 0.10 + 0.0011 * fs
        elif eng == "pool":
            t += 0.35 + 0.0016 * fs
        else:
            t += 0.1
    return t


class Chain(list):
    def op(self, eng, fn, reads=(), writes=()):
        rec = _Rec()
        fn(rec)
        self.append(("op", eng, rec.calls, None, tuple(reads), tuple(writes), _cost_us(eng, rec.calls)))

    def dma(self, eng, fn, stream, reads=(), writes=()):
        rec = _Rec()
        fn(rec, None)
        nbytes = 0
        for (nm, a, k) in rec.calls:
            o = k.get("out")
            nbytes += _free_size(o) * 128 * 4
        self.append(("dma", eng, rec.calls, stream, tuple(reads), tuple(writes), 2.0 + nbytes / 150e3))


class Sched:
    def __init__(self):
        self.eng_free = {e: 0.0 for e in ENGS}
        self.w_fin = {}
        self.r_fin = {}
        self.LAT = 0.2
        self.EPS = 0.4

    def run(self, chains, add):
        chains = [c for c in chains if len(c)]
        pos = [0] * len(chains)
        left = sum(len(c) for c in chains)
        while left:
            cands = []
            for k, c in enumerate(chains):
                if pos[k] >= len(c):
                    continue
                it = c[pos[k]]
                eng, reads, writes = it[1], it[4], it[5]
                rdy = 0.0
                for r in reads:
                    rdy = max(rdy, self.w_fin.get(r, 0.0))
                for r in writes:
                    rdy = max(rdy, self.w_fin.get(r, 0.0), self.r_fin.get(r, 0.0))
                st = max(rdy + self.LAT, self.eng_free[eng])
                fr = (pos[k] + 0.5) / len(c)
                cands.append((st, fr, k))
            mn = min(x[0] for x in cands)
            st, fr, k = min((x for x in cands if x[0] <= mn + self.EPS), key=lambda x: x[1])
            it = chains[k][pos[k]]
            pos[k] += 1
            left -= 1
            eng, reads, writes, dur = it[1], it[4], it[5], it[6]
            if it[0] == "dma":
                self.eng_free[eng] = st + 0.1
                fin = st + dur
            else:
                fin = st + dur
                self.eng_free[eng] = fin
            for r in reads:
                self.r_fin[r] = max(self.r_fin.get(r, 0.0), fin)
            for r in writes:
                self.w_fin[r] = fin
                self.r_fin[r] = 0.0
            add(it)


def interleave(lists):
    lists = [l for l in lists if len(l)]
    out = Chain()
    pos = [0] * len(lists)
    total = sum(len(l) for l in lists)
    for _ in range(total):
        best, bf = None, None
        for k, l in enumerate(lists):
            if pos[k] < len(l):
                fr = (pos[k] + 0.5) / len(l)
                if bf is None or fr < bf:
                    best, bf = k, fr
        out.append(lists[best][pos[best]])
        pos[best] += 1
    return out


class Prog:
    def __init__(self):
        self.ops = []
        self.last_w = {}
        self.readers = {}
        self.dma_streams = {}
        self.fence_deps = set()
        self.fenced = set(ENGS)
        self.last_on = {}
        self.dma_since_fence = []

    def fence(self):
        self.fence_deps = set(self.last_on.values()) | set(self.dma_since_fence)
        self.dma_since_fence = []
        self.fenced = set()

    def _add(self, op, reads, writes):
        idx = len(self.ops)
        op.idx = idx
        for r in reads:
            w = self.last_w.get(r)
            if w is not None:
                op.deps.add(w)
        for r in writes:
            w = self.last_w.get(r)
            if w is not None:
                op.deps.add(w)
            for rd in self.readers.get(r, ()):
                op.deps.add(rd)
        for r in reads:
            self.readers.setdefault(r, []).append(idx)
        for r in writes:
            self.last_w[r] = idx
            self.readers[r] = []
        if op.eng not in self.fenced:
            op.deps |= self.fence_deps
            self.fenced.add(op.eng)
        op.deps.discard(idx)
        self.last_on[op.eng] = idx
        if op.dma_sem is not None:
            self.dma_since_fence.append(idx)
        self.ops.append(op)
        return op

    def op(self, eng, fn, reads=(), writes=()):
        rec = _Rec()
        fn(rec)
        return self._add(Op(eng, rec.calls), reads, writes)

    def add_item(self, it):
        kind, eng, calls, stream, reads, writes = it[:6]
        o = Op(eng, calls)
        if kind == "dma":
            o.dma_sem = stream
            o.ndma = len(calls)
            c = self.dma_streams.get(stream, 0) + o.ndma
            self.dma_streams[stream] = c
            o.dma_cnt = c
        return self._add(o, reads, writes)

    def dma(self, eng, fn, stream, ndma=1, reads=(), writes=()):
        rec = _Rec()
        fn(rec, None)
        o = Op(eng, rec.calls)
        ndma = len(rec.calls)
        o.dma_sem = stream
        o.ndma = ndma
        c = self.dma_streams.get(stream, 0) + ndma
        self.dma_streams[stream] = c
        o.dma_cnt = c
        return self._add(o, reads, writes)

    def emit(self, nc, final_wait_eng="sp", limit=None):
        ops = self.ops if limit is None else self.ops[:limit]
        final_cnt = {}
        for o in ops:
            if o.dma_sem is not None:
                final_cnt[o.dma_sem] = max(final_cnt.get(o.dma_sem, 0), o.dma_cnt)
        need = [False] * len(ops)
        for o in ops:
            for d in o.deps:
                p = ops[d]
                if p.dma_sem is None and p.eng == "pe" and o.eng == "pe" and o.dma_sem is None:
                    continue
                need[d] = True
        cnt = {e: 0 for e in ENGS}
        for o in ops:
            if o.dma_sem is not None:
                o.sig = ("dma:" + o.dma_sem, 16 * o.dma_cnt)
            elif need[o.idx]:
                cnt[o.eng] += 1
                o.sig = (o.eng, cnt[o.eng])
        with contextlib.ExitStack() as st:
            sems = {}
            for e in ENGS:
                sems[e] = st.enter_context(nc.semaphore("s_" + e))
            for s in self.dma_streams:
                sems["dma:" + s] = st.enter_context(nc.semaphore("d_" + s))
            block = st.enter_context(nc.Block())

            def stream(eng_name):
                def body(eng):
                    waited = {}
                    for o in ops:
                        if o.eng != eng_name:
                            continue
                        reqs = {}
                        for d in o.deps:
                            p = ops[d]
                            if p.sig is None:
                                continue
                            if p.dma_sem is None and p.eng == "pe" and eng_name == "pe" and o.dma_sem is None:
                                continue
                            k, v = p.sig
                            if reqs.get(k, 0) < v:
                                reqs[k] = v
                        for k, v in reqs.items():
                            if waited.get(k, 0) < v:
                                eng.wait_ge(sems[k], v)
                                waited[k] = v
                        if o.dma_sem is not None:
                            for (nm, a, k) in o.fn:
                                getattr(eng, nm)(*a, **k).then_inc(sems["dma:" + o.dma_sem], 16)
                        else:
                            ins = None
                            for (nm, a, k) in o.fn:
                                ins = getattr(eng, nm)(*a, **k)
                            if o.sig is not None:
                                ins.then_inc(sems[o.sig[0]], 1)
                    if eng_name == final_wait_eng:
                        for s, c in final_cnt.items():
                            eng.wait_ge(sems["dma:" + s], 16 * c)
                return body

            block.tensor(stream("pe"))
            block.scalar(stream("act"))
            block.vector(stream("dve"))
            block.gpsimd(stream("pool"))
            block.sync(stream("sp"))


D = 1024
SEQ = 2048
NMETA = 16
C = 64
NCH = 33
LP = NCH * C
HOFF = 3
HCOLS = HOFF + LP
XCOL = HOFF + C
MCOL = HOFF + 48
NH_A = 8
DEEPNORM_ALPHA = 2.0 ** 0.25
LN_EPS = 1e-5
RMS_EPS = 1e-6
L2_EPS = 1e-6
NEG = -30000.0


def _blocks(n, b=512):
    out = []
    s = 0
    while s < n:
        out.append((s, min(b, n - s)))
        s += b
    return out


_MARKS = {}


def build_program(debug=False, stop=None):
    nc = bass.Bass("TRN2", target_bir_lowering=False)
    marks = {}

    def din(name, shape):
        return nc.dram_tensor(name, list(shape), F32, kind="ExternalInput").ap()

    xT = din("xT", [D, SEQ])
    xtok = din("xtok", [SEQ, D])
    metaT = din("metaT", [D, NMETA])
    wg = din("wg", [32, 128, 8, 128])
    wbd = din("wbd", [128, 8, 16])
    wkv = din("wkv", [8, 128, 8, 128])
    wqz = din("wqz", [16, 128, 8, 128])
    wgate = din("wgate", [16, 128, 8, 128])
    wpa = din("wpa", [8, 128, 8, 128])
    wpb = din("wpb", [8, 128, 8, 128])
    wout = din("wout", [128, 8, D])
    c64_d = din("c64", [64, 5, 64])
    ident_d = din("ident", [128, 128])
    cw_d = din("cw", [128, 24, 4])
    gnw_d = din("gnw", [128, 1])
    bg_d = din("bg", [128, 16])
    alog_d = din("alog", [64, 8])
    dtb_d = din("dtb", [64, 8])
    sk_d = din("sk", [1, 4, 512])
    lnw_d = din("lnw", [128, D])
    lnb_d = din("lnb", [128, D])
    biasP_d = din("biasP", [128, 4, 512])
    biasO_d = din("biasO", [128, 4, 512])
    lm_d = din("lm", [3, 16, 16])
    rm_d = din("rm", [3, 4, 512])
    out_d = nc.dram_tensor("out", [SEQ, D], F32, kind="ExternalOutput").ap()
    if debug:
        dbg_ya = nc.dram_tensor("dbg_ya", [128, 8, SEQ], F32, kind="ExternalOutput").ap()
        dbg_yb = nc.dram_tensor("dbg_yb", [128, 8, SEQ], F32, kind="ExternalOutput").ap()

    P = Prog()
    with contextlib.ExitStack() as top:
        def sbuf(st, name, shape, dt):
            return st.enter_context(nc.sbuf_tensor("sb_" + name, list(shape), dt))

        ps = [top.enter_context(nc.psum_tensor("ps%d" % i, [128, 512], F32)) for i in range(8)]

        hT = sbuf(top, "hT", [128, 8, HCOLS], BF16)
        yaT = sbuf(top, "yaT", [128, 8, SEQ], BF16)
        identb = sbuf(top, "identb", [128, 128], BF16)
        onesb = sbuf(top, "onesb", [128, 128], BF16)
        c64f = sbuf(top, "c64f", [64, 5, 64], F32)
        c64b = sbuf(top, "c64b", [64, 5, 64], BF16)
        gnw = sbuf(top, "gnw", [128, 1], F32)
        bg = sbuf(top, "bg", [128, 16], F32)

        ld_n = [0]

        def ld(eng, dst, src, stream, reg):
            if stream == "ld_c":
                ld_n[0] += 1
                stream = "ldc%d" % ld_n[0]
            P.dma(eng, lambda e, s: e.dma_start(out=dst, in_=src).then_inc(s, 16), stream, writes=[reg])

        P.op("dve", lambda e: e.memset(hT[:, :, 0:MCOL], 0.0), writes=["hT_z"])
        ld("pool", hT[:, :, MCOL:XCOL], metaT.rearrange("(k p) t -> p k t", p=128), "ld_hm", "hT_m")
        xT3 = xT.rearrange("(k p) t -> p k t", p=128)
        ld("pool", hT[:, :, XCOL:XCOL + 512], xT3[:, :, 0:512], "ld_ha", "hTa_x")
        ld("pool", hT[:, :, XCOL + 512:HCOLS], xT3[:, :, 512:SEQ], "ld_hb", "hT_x")
        ld("pool", identb[:], ident_d, "ld_c", "identb")
        ld("pool", c64b[:], c64_d, "ld_c", "c64b")
        ld("sp", c64f[:], c64_d, "ld_c", "c64f")
        ld("sp", gnw[:], gnw_d, "ld_c", "gnw")
        ld("sp", bg[:], bg_d, "ld_c", "bg")
        P.op("dve", lambda e: e.memset(onesb[:], 1.0), writes=["onesb"])
        hscr = sbuf(top, "hscr", [128, 8], F32)
        P.op("dve", lambda e: e.memset(hscr[:, 0:4], 0.0), reads=["hT_z", "hT_m", "hTa_x"], writes=["hTa"])
        P.op("dve", lambda e: e.memset(hscr[:, 4:8], 0.0), reads=["hTa", "hT_x"], writes=["hT"])

        TRI, SLM, I64, NEGT, NEGD = 0, 1, 2, 3, 4
        marks["0"] = len(P.ops)

        wk0_top = sbuf(top, "wk0", [128, 8, 128], BF16)
        wv0_top = sbuf(top, "wv0", [128, 8, 128], BF16)
        with contextlib.ExitStack() as g:
            NTQ = 576
            NCQ = 9
            cw = sbuf(g, "cw", [128, 24, 4], F32)
            alog = sbuf(g, "alog", [64, 8], F32)
            dtb = sbuf(g, "dtb", [64, 8], F32)
            negA = sbuf(g, "negA", [64, 8], F32)
            wbd_sb = sbuf(g, "wbd_sb", [128, 8, 16], BF16)
            negT8 = sbuf(g, "negT8", [64, 8, 64], BF16)
            negD8 = sbuf(g, "negD8", [64, 8, 64], BF16)
            wts = [sbuf(g, "wts%d" % i, [128, 4, 8, 128], BF16) for i in range(3)]
            pre = [sbuf(g, "pre%d" % i, [128, NTQ + 3], F32) for i in range(3)]
            acc = [sbuf(g, "acc%d" % i, [128, NTQ], F32) for i in range(3)]
            sqb = [sbuf(g, "sq%d" % i, [128, NTQ], BF16) for i in range(2)]
            rnb = [sbuf(g, "rn%d" % i, [128, 512], F32) for i in range(2)]
            qn = [sbuf(g, "qn%d" % i, [128, NTQ], BF16) for i in range(2)]
            kn = [sbuf(g, "kn%d" % i, [128, NTQ], BF16) for i in range(2)]
            vT = [sbuf(g, "vT%d" % i, [128, NTQ], BF16) for i in range(2)]
            zs = [sbuf(g, "zs%d" % i, [128, NTQ], BF16) for i in range(3)]
            qd = [sbuf(g, "qd%d" % i, [128, NTQ], BF16) for i in range(3)]
            ke = [sbuf(g, "ke%d" % i, [64, NCQ, 128], BF16) for i in range(2)]
            vtok = [sbuf(g, "vtok%d" % i, [64, NCQ, 128], BF16) for i in range(2)]
            kd = [sbuf(g, "kd%d" % i, [64, NCQ, 128], BF16) for i in range(3)]
            X0s = [sbuf(g, "X0s%d" % i, [64, NCQ * 64], BF16) for i in range(2)]
            XT0s = [sbuf(g, "XT0s%d" % i, [64, NCQ * 64], BF16) for i in range(2)]
            R0s = [sbuf(g, "R0s%d" % i, [64, NCQ * 64], BF16) for i in range(2)]
            gt = {}
            for nm in ("beta", "nbeta", "gg", "gtmp", "gcs", "egc", "ekd"):
                gt[nm] = [sbuf(g, "%s%d" % (nm, i), [64, NCQ, 8], F32) for i in range(2)]
            gt["glast"] = [sbuf(g, "glast%d" % i, [128, NCQ, 8], F32) for i in range(2)]
            gt["ggh"] = [sbuf(g, "ggh%d" % i, [64, NCQ, 8], BF16) for i in range(2)]
            gt["ggl"] = [sbuf(g, "ggl%d" % i, [64, NCQ, 8], BF16) for i in range(2)]
            GMh = sbuf(g, "GMh", [64, 8, 64], BF16)
            GMl = sbuf(g, "GMl", [64, 8, 64], BF16)
            GM2h = sbuf(g, "GM2h", [64, 8, 64], BF16)
            GM2l = sbuf(g, "GM2l", [64, 8, 64], BF16)
            Bd = sbuf(g, "Bd", [64, 8, 64], BF16)
            Dg = sbuf(g, "Dg", [64, 8, 64], BF16)
            decT = sbuf(g, "decT", [64, 512], F32)
            dec = sbuf(g, "dec", [64, 512], F32)
            t1 = sbuf(g, "t1", [64, 512], F32)
            t2 = sbuf(g, "t2", [64, 512], F32)
            Xb = [sbuf(g, "X%d" % i, [64, 512], BF16) for i in range(2)]
            XTb = [sbuf(g, "XT%d" % i, [64, 512], BF16) for i in range(2)]
            Rb = [sbuf(g, "R%d" % i, [64, 512], BF16) for i in range(2)]
            R5 = sbuf(g, "R5", [64, 512], F32)
            Y = sbuf(g, "Y", [64, NCQ * 64], BF16)
            attnT = [sbuf(g, "attnT%d" % i, [64, NCQ * 64], BF16) for i in range(3)]
            WT = [sbuf(g, "WT%d" % i, [128, NCQ * 64], BF16) for i in range(2)]
            U = [sbuf(g, "U%d" % i, [64, NCQ, 128], F32) for i in range(2)]
            S = sbuf(g, "S", [128, 8, 128], F32)
            Sb = sbuf(g, "Sb", [128, 128], BF16)
            vn = [sbuf(g, "vn%d" % i, [64, 128], BF16) for i in range(2)]
            osb = sbuf(g, "osb", [128, 512], F32)
            osq = sbuf(g, "osq", [128, 512], BF16)
            orn = sbuf(g, "orn", [128, 512], F32)

            ld("sp", cw[:], cw_d, "ld_c", "cw")
            ld("sp", alog[:], alog_d, "ld_c", "alog")
            ld("sp", dtb[:], dtb_d, "ld_c", "dtb")
            ld("pool", wbd_sb[:], wbd, "ld_c", "wbd")
            P.op("act", lambda e: e.activation(out=negA[:], in_=alog[:], func=AF.Exp), reads=["alog"], writes=["negA"])
            P.op("dve", lambda e: e.tensor_scalar(out=negA[:], in0=negA[:], scalar1=-1.0, scalar2=None, op0=ALU.mult),
                 reads=["negA"], writes=["negA"])
            P.op("dve", lambda e: e.tensor_copy(out=negT8[:], in_=c64f[:, NEGT, :].unsqueeze(1).to_broadcast([64, 8, 64])),
                 reads=["c64f"], writes=["negT8"])
            P.op("dve", lambda e: e.tensor_copy(out=negD8[:], in_=c64f[:, NEGD, :].unsqueeze(1).to_broadcast([64, 8, 64])),
                 reads=["c64f"], writes=["negD8"])
            P.op("dve", lambda e: e.memset(S[:], 0.0), writes=["S"])

            QUARTERS = ((0, 9), (9, 8), (17, 8), (25, 8))
            items = [(qi, hh) for qi in range(4) for hh in range(NH_A)]

            def mk_rr(banks):
                st_ = [0]

                def nb_():
                    b = banks[st_[0] % len(banks)]
                    st_[0] += 1
                    return b
                return nb_
            bankA = mk_rr((0, 1))
            bankB = mk_rr((2,))
            bankB2 = mk_rr((3, 4))

            def gates(qi):
                c0, NC = QUARTERS[qi]
                qp = qi % 2
                col0 = HOFF + c0 * C
                ch = Chain()
                beta, nbeta, gg, gtmp, gcs, egc, ekd, glast = (gt[n][qp] for n in
                                                               ("beta", "nbeta", "gg", "gtmp", "gcs", "egc", "ekd", "glast"))
                sfx = str(qp)
                pb = 1
                for c in range(NC):
                    def f(e, c=c):
                        r = None
                        for kc in range(8):
                            r = e.matmul(ps[pb][0:64, c * 16:(c + 1) * 16],
                                         lhsT=hT[:, kc, col0 + c * 64: col0 + (c + 1) * 64],
                                         rhs=wbd_sb[:, kc, :], start=(kc == 0), stop=(kc == 7))
                        return r
                    ch.op("pe", f, reads=["hTa" if qi == 0 else "hT", "wbd"], writes=["ps1"])
                bdv = ps[pb][0:64, 0:NC * 16].rearrange("p (c k) -> p c k", k=16)
                ch.op("act", lambda e: e.activation(out=beta[:, 0:NC, :], in_=bdv[:, :, 0:8], func=AF.Sigmoid),
                      reads=["ps1"], writes=["beta" + sfx])
                ch.op("dve", lambda e: e.tensor_tensor(out=gtmp[:, 0:NC, :], in0=bdv[:, :, 8:16],
                                                       in1=dtb[:].unsqueeze(1).to_broadcast([64, NC, 8]), op=ALU.add),
                      reads=["ps1", "dtb"], writes=["gtmp" + sfx])
                ch.op("act", lambda e: e.activation(out=gtmp[:, 0:NC, :], in_=gtmp[:, 0:NC, :], func=AF.Exp),
                      reads=["gtmp" + sfx], writes=["gtmp" + sfx])
                ch.op("act", lambda e: e.activation(out=gtmp[:, 0:NC, :], in_=gtmp[:, 0:NC, :], func=AF.Ln, bias=1.0, scale=1.0),
                      reads=["gtmp" + sfx], writes=["gtmp" + sfx])
                ch.op("dve", lambda e: e.tensor_tensor(out=gg[:, 0:NC, :], in0=gtmp[:, 0:NC, :],
                                                       in1=negA[:].unsqueeze(1).to_broadcast([64, NC, 8]), op=ALU.mult),
                      reads=["gtmp" + sfx, "negA"], writes=["gg" + sfx])
                ch.op("dve", lambda e: e.tensor_scalar(out=nbeta[:, 0:NC, :], in0=beta[:, 0:NC, :], scalar1=-1.0,
                                                       scalar2=None, op0=ALU.mult),
                      reads=["beta" + sfx], writes=["nbeta" + sfx])
                ggh, ggl = gt["ggh"][qp], gt["ggl"][qp]
                ch.op("dve", lambda e: e.tensor_copy(out=ggh[:, 0:NC, :], in_=gg[:, 0:NC, :]), reads=["gg" + sfx], writes=["ggh" + sfx])
                ch.op("dve", lambda e: e.tensor_tensor(out=ggl[:, 0:NC, :], in0=gg[:, 0:NC, :], in1=ggh[:, 0:NC, :], op=ALU.subtract),
                      reads=["gg" + sfx, "ggh" + sfx], writes=["ggl" + sfx])
                gghf = ggh[:, 0:NC, :].rearrange("p c k -> p (c k)")
                gglf = ggl[:, 0:NC, :].rearrange("p c k -> p (c k)")

                def f(e):
                    e.matmul(ps[1][0:64, 0:NC * 8], lhsT=c64b[:, TRI, :], rhs=gghf, start=True, stop=False)
                    return e.matmul(ps[1][0:64, 0:NC * 8], lhsT=c64b[:, TRI, :], rhs=gglf, start=False, stop=True)
                ch.op("pe", f, reads=["ggh" + sfx, "ggl" + sfx, "c64b"], writes=["ps1"])
                gcv = ps[1][0:64, 0:NC * 8].rearrange("p (c k) -> p c k", k=8)
                ch.op("act", lambda e: e.activation(out=gcs[:, 0:NC, :], in_=gcv, func=AF.Identity),
                      reads=["ps1"], writes=["gcs" + sfx])
                ch.op("act", lambda e: e.activation(out=egc[:, 0:NC, :], in_=gcv, func=AF.Exp),
                      reads=["ps1"], writes=["egc" + sfx])

                def f(e):
                    e.matmul(ps[1][:, 0:NC * 8], lhsT=onesb[0:64, :], rhs=gghf, start=True, stop=False)
                    return e.matmul(ps[1][:, 0:NC * 8], lhsT=onesb[0:64, :], rhs=gglf, start=False, stop=True)
                ch.op("pe", f, reads=["ggh" + sfx, "ggl" + sfx, "onesb"], writes=["ps1"])
                totv = ps[1][:, 0:NC * 8].rearrange("p (c k) -> p c k", k=8)
                ch.op("dve", lambda e: e.tensor_tensor(out=ekd[:, 0:NC, :], in0=totv[0:64], in1=gcs[:, 0:NC, :], op=ALU.subtract),
                      reads=["ps1", "gcs" + sfx], writes=["ekd" + sfx])
                ch.op("act", lambda e: e.activation(out=ekd[:, 0:NC, :], in_=ekd[:, 0:NC, :], func=AF.Exp),
                      reads=["ekd" + sfx], writes=["ekd" + sfx])
                ch.op("act", lambda e: e.activation(out=glast[:, 0:NC, :], in_=totv, func=AF.Exp),
                      reads=["ps1"], writes=["glast" + sfx])
                return ch

            def wload(i):
                qi, hh = items[i]
                k3 = i % 3
                ch = Chain()
                ch.dma("pool", lambda e, s: e.dma_start(out=wts[k3][:], in_=wg[hh * 4:(hh + 1) * 4].rearrange("x p k c -> p x k c")).then_inc(s, 16),
                       "ld_ws%d" % k3, writes=["wts%d" % k3])
                return ch

            for it in wload(0):
                P.add_item(it)

            def stageA(i):
                qi, hh = items[i]
                c0, NC = QUARTERS[qi]
                NT = NC * C
                col0 = HOFF + c0 * C
                a_ = i % 2
                w_ = (i % 3) * 4
                head = Chain()
                if i + 1 < len(items):
                    head.extend(wload(i + 1))
                chains = []
                XB = (0, 1, 0)
                for X in range(3):
                    ch = Chain()
                    wX = wts[i % 3][:, X]
                    prb = pre[X]
                    prn = "pre%d" % X
                    a = acc[X]
                    an = "acc%d" % X
                    ti = X * 8 + hh
                    for (s0, n) in _blocks(NT + 3):
                        b = XB[X]

                        def f(e, b=b, s0=s0, n=n):
                            r = None
                            for kc in range(8):
                                r = e.matmul(ps[b][:, 0:n], lhsT=wX[:, kc, :],
                                             rhs=hT[:, kc, col0 - 3 + s0: col0 - 3 + s0 + n],
                                             start=(kc == 0), stop=(kc == 7))
                            return r
                        ch.op("pe", f, reads=["hTa" if qi == 0 else "hT", "wts%d" % (i % 3)], writes=["ps%d" % b])
                        ch.op("act", lambda e, b=b, s0=s0, n=n: e.activation(out=prb[:, s0:s0 + n], in_=ps[b][:, 0:n], func=AF.Identity),
                              reads=["ps%d" % b], writes=[prn])
                    ch.op("act", lambda e: e.activation(out=a[:, 0:NT], in_=prb[:, 3:3 + NT], func=AF.Identity, scale=cw[:, ti, 3:4]),
                          reads=[prn, "cw"], writes=[an])
                    for j in (2, 1, 0):
                        ch.op("dve", lambda e, j=j: e.scalar_tensor_tensor(
                            out=a[:, 0:NT], in0=prb[:, j:j + NT], scalar=cw[:, ti, j:j + 1], in1=a[:, 0:NT],
                            op0=ALU.mult, op1=ALU.add), reads=[prn, "cw", an], writes=[an])
                    if X == 2:
                        ch.op("act", lambda e: e.activation(out=vT[a_][:, 0:NT], in_=a[:, 0:NT], func=AF.Silu),
                              reads=[an], writes=["vT%d" % a_])
                    else:
                        ch.op("act", lambda e: e.activation(out=a[:, 0:NT], in_=a[:, 0:NT], func=AF.Silu), reads=[an], writes=[an])
                        sq_ = sqb[X]
                        rn_ = rnb[X]
                        ch.op("pool", lambda e: e.tensor_tensor(out=sq_[:, 0:NT], in0=a[:, 0:NT], in1=a[:, 0:NT], op=ALU.mult),
                              reads=[an], writes=["sq%d" % X])
                        for (s0, n) in _blocks(NT):
                            ch.op("pe", lambda e, s0=s0, n=n: e.matmul(ps[XB[X]][:, 0:n], lhsT=onesb[:, :], rhs=sq_[:, s0:s0 + n],
                                                                      start=True, stop=True),
                                  reads=["sq%d" % X, "onesb"], writes=["ps%d" % XB[X]])
                            ch.op("act", lambda e, n=n: e.activation(out=rn_[:, 0:n], in_=ps[XB[X]][:, 0:n], func=AF.Ln, bias=L2_EPS, scale=1.0),
                                  reads=["ps%d" % XB[X]], writes=["rn%d" % X])
                            ch.op("act", lambda e, n=n: e.activation(out=rn_[:, 0:n], in_=rn_[:, 0:n], func=AF.Exp, scale=-0.5),
                                  reads=["rn%d" % X], writes=["rn%d" % X])
                            if X == 0:
                                ch.op("dve", lambda e, s0=s0, n=n: e.scalar_tensor_tensor(
                                    out=qn[a_][:, s0:s0 + n], in0=a[:, s0:s0 + n], scalar=128.0 ** -0.5, in1=rn_[:, 0:n],
                                    op0=ALU.mult, op1=ALU.mult), reads=[an, "rn0"], writes=["qn%d" % a_])
                            else:
                                ch.op("dve", lambda e, s0=s0, n=n: e.tensor_tensor(
                                    out=kn[a_][:, s0:s0 + n], in0=a[:, s0:s0 + n], in1=rn_[:, 0:n], op=ALU.mult),
                                    reads=[an, "rn1"], writes=["kn%d" % a_])
                    chains.append(ch)
                kch = chains[1]
                if hh == 0:
                    kch = Chain(list(chains[1]) + list(gates(qi)))
                return [head, Chain(list(chains[0]) + list(chains[2])), kch]

            def stageB1(i):
                qi, hh = items[i]
                c0, NC = QUARTERS[qi]
                NT = NC * C
                col0 = HOFF + c0 * C
                a_ = i % 2
                t_ = i % 3
                qp = qi % 2
                sfx = str(qp)
                beta, nbeta, gg, egc, ekd = (gt[n][qp] for n in ("beta", "nbeta", "gg", "egc", "ekd"))
                qn_, kn_, vT_, zs_, qd_, kd_ = qn[a_], kn[a_], vT[a_], zs[t_], qd[t_], kd[t_]
                attnT_ = attnT[t_]
                ke_, vtok_ = ke[a_], vtok[a_]
                sa = str(a_)
                st = str(t_)
                ch = Chain()
                wz_ = wts[i % 3][:, 3]
                for (s0, n) in _blocks(NT):
                    b = bankB()

                    def f(e, b=b, s0=s0, n=n):
                        r = None
                        for kc in range(8):
                            r = e.matmul(ps[b][:, 0:n], lhsT=wz_[:, kc, :], rhs=hT[:, kc, col0 + s0: col0 + s0 + n],
                                         start=(kc == 0), stop=(kc == 7))
                        return r
                    ch.op("pe", f, reads=["hTa" if qi == 0 else "hT", "wts%d" % (i % 3)], writes=["ps%d" % b])
                    ch.op("act", lambda e, b=b, s0=s0, n=n: e.activation(out=zs_[:, s0:s0 + n], in_=ps[b][:, 0:n], func=AF.Silu),
                          reads=["ps%d" % b], writes=["zs" + st])
                for cb in range(0, NC, 4):
                    ncb = min(4, NC - cb)
                    b = bankB()

                    def f(e, b=b, cb=cb, ncb=ncb):
                        r = None
                        for j in range(ncb):
                            c = cb + j
                            r = e.matmul(ps[b][0:64, j * 128:(j + 1) * 128], lhsT=kn_[:, c * 64:(c + 1) * 64], rhs=identb[:, :],
                                         start=True, stop=True)
                        return r
                    ch.op("pe", f, reads=["kn" + sa, "identb"], writes=["ps%d" % b])
                    pv = ps[b][0:64, 0:ncb * 128].rearrange("p (c k) -> p c k", k=128)
                    ch.op("dve", lambda e, pv=pv, cb=cb, ncb=ncb: e.tensor_tensor(
                        out=ke_[:, cb:cb + ncb, :], in0=pv, in1=egc[:, cb:cb + ncb, hh:hh + 1].to_broadcast([64, ncb, 128]),
                        op=ALU.mult), reads=["ps%d" % b, "egc" + sfx], writes=["ke" + sa])
                    ch.op("dve", lambda e, pv=pv, cb=cb, ncb=ncb: e.tensor_tensor(
                        out=kd_[:, cb:cb + ncb, :], in0=pv, in1=ekd[:, cb:cb + ncb, hh:hh + 1].to_broadcast([64, ncb, 128]),
                        op=ALU.mult), reads=["ps%d" % b, "ekd" + sfx], writes=["kd" + st])
                    b = bankB()

                    def f(e, b=b, cb=cb, ncb=ncb):
                        r = None
                        for j in range(ncb):
                            c = cb + j
                            r = e.matmul(ps[b][0:64, j * 128:(j + 1) * 128], lhsT=vT_[:, c * 64:(c + 1) * 64], rhs=identb[:, :],
                                         start=True, stop=True)
                        return r
                    ch.op("pe", f, reads=["vT" + sa, "identb"], writes=["ps%d" % b])
                    pv2 = ps[b][0:64, 0:ncb * 128].rearrange("p (c k) -> p c k", k=128)
                    ch.op("act", lambda e, pv2=pv2, cb=cb, ncb=ncb: e.activation(out=vtok_[:, cb:cb + ncb, :], in_=pv2, func=AF.Identity),
                          reads=["ps%d" % b], writes=["vtok" + sa])
                for cb in range(0, NC, 8):
                    nb = min(8, NC - cb)
                    W = nb * 64
                    ggh, ggl = gt["ggh"][qp], gt["ggl"][qp]
                    for (dst, dn, tab, src, sn_) in ((GMh, "GMh", TRI, ggh, "ggh"), (GMl, "GMl", TRI, ggl, "ggl")):
                        ch.op("pool", lambda e, dst=dst, tab=tab, src=src: e.tensor_tensor(
                            out=dst[:, 0:nb, :], in0=c64f[:, tab, :].unsqueeze(1).to_broadcast([64, nb, 64]),
                            in1=src[:, cb:cb + nb, hh:hh + 1].to_broadcast([64, nb, 64]), op=ALU.mult),
                            reads=["c64f", sn_ + sfx], writes=[dn])
                    ch.op("pool", lambda e, nb=nb, cb=cb: e.tensor_tensor(
                        out=Bd[:, 0:nb, :], in0=c64f[:, I64, :].unsqueeze(1).to_broadcast([64, nb, 64]),
                        in1=beta[:, cb:cb + nb, hh:hh + 1].to_broadcast([64, nb, 64]), op=ALU.mult),
                        reads=["c64f", "beta" + sfx], writes=["Bd"])
                    ch.op("pool", lambda e, nb=nb, cb=cb: e.tensor_tensor(
                        out=Dg[:, 0:nb, :], in0=c64f[:, I64, :].unsqueeze(1).to_broadcast([64, nb, 64]),
                        in1=egc[:, cb:cb + nb, hh:hh + 1].to_broadcast([64, nb, 64]), op=ALU.mult),
                        reads=["c64f", "egc" + sfx], writes=["Dg"])
                    GMhf = GMh[:, 0:nb, :].rearrange("p c k -> p (c k)")
                    GMlf = GMl[:, 0:nb, :].rearrange("p c k -> p (c k)")
                    Bdf = Bd[:, 0:nb, :].rearrange("p c k -> p (c k)")
                    Dgf = Dg[:, 0:nb, :].rearrange("p c k -> p (c k)")
                    nT8 = negT8[:, 0:nb, :].rearrange("p c k -> p (c k)")
                    bDT = bankB()

                    def f(e, b=bDT, W=W, nT8=nT8):
                        e.matmul(ps[b][0:64, 0:W], lhsT=c64b[:, SLM, :], rhs=GMhf, start=True, stop=False)
                        e.matmul(ps[b][0:64, 0:W], lhsT=c64b[:, SLM, :], rhs=GMlf, start=False, stop=False)
                        return e.matmul(ps[b][0:64, 0:W], lhsT=c64b[:, I64, :], rhs=nT8, start=False, stop=True)
                    ch.op("pe", f, reads=["c64b", "GMh", "GMl", "negT8"], writes=["ps%d" % bDT])
                    ch.op("act", lambda e, b=bDT, W=W: e.activation(out=decT[:, 0:W], in_=ps[b][0:64, 0:W], func=AF.Exp),
                          reads=["ps%d" % bDT], writes=["decT"])
                    bKK = bankB()

                    def f(e, b=bKK, cb=cb, nb=nb):
                        r = None
                        for j in range(nb):
                            c = cb + j
                            r = e.matmul(ps[b][0:64, j * 64:(j + 1) * 64], lhsT=kn_[:, c * 64:(c + 1) * 64],
                                         rhs=kn_[:, c * 64:(c + 1) * 64], start=True, stop=True)
                        return r
                    ch.op("pe", f, reads=["kn" + sa], writes=["ps%d" % bKK])
                    ch.op("dve", lambda e, b=bKK, W=W: e.tensor_tensor(out=t1[:, 0:W], in0=ps[b][0:64, 0:W], in1=decT[:, 0:W], op=ALU.mult),
                          reads=["ps%d" % bKK, "decT"], writes=["t1"])
                    bBR = bankB()
                    ch.op("pe", lambda e, b=bBR, W=W, Bdf=Bdf: e.matmul(ps[b][0:64, 0:W], lhsT=c64b[:, SLM, :], rhs=Bdf, start=True, stop=True),
                          reads=["c64b", "Bd"], writes=["ps%d" % bBR])
                    X0, XT0, R0 = X0s[a_], XT0s[a_], R0s[a_]
                    o0 = cb * 64
                    ch.op("dve", lambda e, b=bBR, W=W: e.scalar_tensor_tensor(
                        out=X0[:, o0:o0 + W], in0=t1[:, 0:W], scalar=-1.0, in1=ps[b][0:64, 0:W], op0=ALU.mult, op1=ALU.mult),
                        reads=["t1", "ps%d" % bBR], writes=["X0s" + sa])
                    bXT = bankB()

                    def f(e, b=bXT, nb=nb, o0=o0):
                        r = None
                        for j in range(nb):
                            r = e.matmul(ps[b][0:64, j * 64:(j + 1) * 64], lhsT=X0[:, o0 + j * 64: o0 + (j + 1) * 64],
                                         rhs=c64b[:, I64, :], start=True, stop=True)
                        return r
                    ch.op("pe", f, reads=["X0s" + sa, "c64b"], writes=["ps%d" % bXT])
                    ch.op("act", lambda e, b=bXT, W=W, o0=o0: e.activation(out=XT0[:, o0:o0 + W], in_=ps[b][0:64, 0:W], func=AF.Identity),
                          reads=["ps%d" % bXT], writes=["XT0s" + sa])
                    ch.op("pool", lambda e, nb=nb: e.tensor_tensor(
                        out=R0[:, o0:o0 + nb * 64].rearrange("p (c k) -> p c k", k=64),
                        in0=X0[:, o0:o0 + nb * 64].rearrange("p (c k) -> p c k", k=64),
                        in1=c64f[:, I64, :].unsqueeze(1).to_broadcast([64, nb, 64]), op=ALU.add),
                        reads=["X0s" + sa, "c64f"], writes=["R0s" + sa])
                    bQK = bankB()

                    def f(e, b=bQK, cb=cb, nb=nb):
                        r = None
                        for j in range(nb):
                            c = cb + j
                            r = e.matmul(ps[b][0:64, j * 64:(j + 1) * 64], lhsT=kn_[:, c * 64:(c + 1) * 64],
                                         rhs=qn_[:, c * 64:(c + 1) * 64], start=True, stop=True)
                        return r
                    ch.op("pe", f, reads=["kn" + sa, "qn" + sa], writes=["ps%d" % bQK])
                    ch.op("dve", lambda e, b=bQK, W=W, cb=cb: e.tensor_tensor(
                        out=attnT_[:, cb * 64: cb * 64 + W], in0=ps[b][0:64, 0:W], in1=decT[:, 0:W], op=ALU.mult),
                        reads=["ps%d" % bQK, "decT"], writes=["attnT" + st])
                    bEG = bankB()
                    ch.op("pe", lambda e, b=bEG, W=W, Dgf=Dgf: e.matmul(ps[b][:, 0:W], lhsT=onesb[0:64, :], rhs=Dgf, start=True, stop=True),
                          reads=["onesb", "Dg"], writes=["ps%d" % bEG])
                    ch.op("dve", lambda e, b=bEG, W=W, cb=cb: e.tensor_tensor(
                        out=qd_[:, cb * 64: cb * 64 + W], in0=ps[b][:, 0:W], in1=qn_[:, cb * 64: cb * 64 + W], op=ALU.mult),
                        reads=["ps%d" % bEG, "qn" + sa], writes=["qd" + st])
                return ch

            def stageB2(i):
                qi, hh = items[i]
                c0, NC = QUARTERS[qi]
                a_ = i % 2
                qp = qi % 2
                sfx = str(qp)
                beta = gt["beta"][qp]
                ke_, vtok_ = ke[a_], vtok[a_]
                WT_, U_ = WT[a_], U[a_]
                sa = str(a_)
                ch = Chain()
                for cb in range(0, NC, 8):
                    nb = min(8, NC - cb)
                    W = nb * 64
                    o0 = cb * 64
                    cur = 0
                    for lvl in range(1, 6):
                        if lvl == 1:
                            Xp, XTp, Rp = X0s[a_][:, o0:o0 + W], XT0s[a_][:, o0:o0 + W], R0s[a_][:, o0:o0 + W]
                            rXp, rXTp, rRp = "X0s" + sa, "XT0s" + sa, "R0s" + sa
                        else:
                            Xp, XTp, Rp = Xb[cur], XTb[cur], Rb[cur]
                            rXp, rXTp, rRp = "X%d" % cur, "XT%d" % cur, "R%d" % cur
                        Xn, XTn, Rn = Xb[1 - cur], XTb[1 - cur], Rb[1 - cur]
                        nn = str(1 - cur)
                        if lvl <= 4:
                            b = bankB2()

                            def f(e, b=b, nb=nb, Xp=Xp, XTp=XTp):
                                r = None
                                for j in range(nb):
                                    r = e.matmul(ps[b][0:64, j * 64:(j + 1) * 64], lhsT=XTp[:, j * 64:(j + 1) * 64],
                                                 rhs=Xp[:, j * 64:(j + 1) * 64], start=True, stop=True)
                                return r
                            ch.op("pe", f, reads=[rXp, rXTp], writes=["ps%d" % b])
                            ch.op("act", lambda e, b=b, W=W, Xn=Xn: e.activation(out=Xn[:, 0:W], in_=ps[b][0:64, 0:W], func=AF.Identity),
                                  reads=["ps%d" % b], writes=["X" + nn])
                        b = bankB2()

                        def f(e, b=b, nb=nb, Xp=Xp, XTp=XTp):
                            r = None
                            for j in range(nb):
                                r = e.matmul(ps[b][0:64, j * 64:(j + 1) * 64], lhsT=Xp[:, j * 64:(j + 1) * 64],
                                             rhs=XTp[:, j * 64:(j + 1) * 64], start=True, stop=True)
                            return r
                        ch.op("pe", f, reads=[rXp, rXTp], writes=["ps%d" % b])
                        ch.op("act", lambda e, b=b, W=W, XTn=XTn: e.activation(out=XTn[:, 0:W], in_=ps[b][0:64, 0:W], func=AF.Identity),
                              reads=["ps%d" % b], writes=["XT" + nn])
                        b = bankB2()

                        def f(e, b=b, nb=nb, XTn=XTn, Rp=Rp):
                            r = None
                            for j in range(nb):
                                r = e.matmul(ps[b][0:64, j * 64:(j + 1) * 64], lhsT=XTn[:, j * 64:(j + 1) * 64],
                                             rhs=Rp[:, j * 64:(j + 1) * 64], start=True, stop=True)
                            return r
                        ch.op("pe", f, reads=["XT" + nn, rRp], writes=["ps%d" % b])
                        if lvl < 5:
                            ch.op("dve", lambda e, b=b, W=W, Rn=Rn, Rp=Rp: e.tensor_tensor(
                                out=Rn[:, 0:W], in0=ps[b][0:64, 0:W], in1=Rp[:, 0:W], op=ALU.add),
                                reads=["ps%d" % b, rRp], writes=["R" + nn])
                        else:
                            ch.op("dve", lambda e, b=b, W=W, Rp=Rp: e.tensor_tensor(
                                out=R5[:, 0:W], in0=ps[b][0:64, 0:W], in1=Rp[:, 0:W], op=ALU.add),
                                reads=["ps%d" % b, rRp], writes=["R5"])
                            ch.op("dve", lambda e, nb=nb, cb=cb: e.tensor_tensor(
                                out=Y[:, cb * 64:(cb + nb) * 64].rearrange("p (c k) -> p c k", k=64),
                                in0=R5[:, 0:nb * 64].rearrange("p (c k) -> p c k", k=64),
                                in1=beta[:, cb:cb + nb, hh:hh + 1].to_broadcast([64, nb, 64]), op=ALU.mult),
                                reads=["R5", "beta" + sfx], writes=["Y"])
                        cur = 1 - cur
                    b = bankB2()

                    def f(e, b=b, cb=cb, nb=nb):
                        r = None
                        for j in range(nb):
                            c = cb + j
                            r = e.matmul(ps[b][:, j * 64:(j + 1) * 64], lhsT=ke_[:, c, :], rhs=Y[:, c * 64:(c + 1) * 64],
                                         start=True, stop=True)
                        return r
                    ch.op("pe", f, reads=["ke" + sa, "Y"], writes=["ps%d" % b])
                    ch.op("act", lambda e, b=b, W=W, cb=cb: e.activation(out=WT_[:, cb * 64: cb * 64 + W], in_=ps[b][:, 0:W], func=AF.Identity),
                          reads=["ps%d" % b], writes=["WT" + sa])
                    for c4 in range(0, nb, 4):
                        n4 = min(4, nb - c4)
                        b = bankB2()

                        def f(e, b=b, cb=cb, c4=c4, n4=n4):
                            r = None
                            for j in range(n4):
                                c = cb + c4 + j
                                r = e.matmul(ps[b][0:64, j * 128:(j + 1) * 128], lhsT=Y[:, c * 64:(c + 1) * 64], rhs=vtok_[:, c, :],
                                             start=True, stop=True)
                            return r
                        ch.op("pe", f, reads=["Y", "vtok" + sa], writes=["ps%d" % b])
                        ch.op("act", lambda e, b=b, cb=cb, c4=c4, n4=n4: e.activation(
                            out=U_[:, cb + c4: cb + c4 + n4, :], in_=ps[b][0:64, 0:n4 * 128].rearrange("p (c k) -> p c k", k=128),
                            func=AF.Identity), reads=["ps%d" % b], writes=["U" + sa])
                return ch

            def stageC(i):
                qi, hh = items[i]
                c0, NC = QUARTERS[qi]
                a_ = i % 2
                qp = qi % 2
                sa = str(a_)
                glast = gt["glast"][qp]
                t_ = i % 3
                st = str(t_)
                zs_, qd_, kd_, attnT_, WT_, U_ = zs[t_], qd[t_], kd[t_], attnT[t_], WT[a_], U[a_]
                ch = Chain()
                Sh = S[:, hh, :]
                Sn = "S%d" % hh
                ch.op("act", lambda e: e.activation(out=Sb[:, :], in_=Sh, func=AF.Identity), reads=["S", Sn], writes=["Sb"])
                for c in range(NC):
                    cg = c0 + c
                    v = vn[c % 2]
                    vname = "vn%d" % (c % 2)
                    ch.op("pe", lambda e, c=c: e.matmul(ps[6][0:64, 0:128], lhsT=WT_[:, c * 64:(c + 1) * 64], rhs=Sb[:, :], start=True, stop=True),
                          reads=["WT" + sa, "Sb"], writes=["ps6"])
                    ch.op("dve", lambda e, c=c, v=v: e.tensor_tensor(out=v[:, :], in0=U_[:, c, :], in1=ps[6][0:64, 0:128], op=ALU.subtract),
                          reads=["U" + sa, "ps6"], writes=[vname])
                    if cg >= 1:
                        oc = (cg - 1) % 8

                        def f(e, c=c, v=v, oc=oc):
                            e.matmul(ps[7][:, oc * 64:(oc + 1) * 64], lhsT=Sb[:, :], rhs=qd_[:, c * 64:(c + 1) * 64], start=True, stop=False)
                            return e.matmul(ps[7][:, oc * 64:(oc + 1) * 64], lhsT=v[:, :], rhs=attnT_[:, c * 64:(c + 1) * 64],
                                            start=False, stop=True)
                        ch.op("pe", f, reads=["Sb", "qd" + st, vname, "attnT" + st], writes=["ps7"])
                    ch.op("pe", lambda e, c=c, v=v: e.matmul(ps[5][:, 0:128], lhsT=kd_[:, c, :], rhs=v[:, :], start=True, stop=True),
                          reads=["kd" + st, vname], writes=["ps5"])
                    ch.op("dve", lambda e, c=c: e.scalar_tensor_tensor(
                        out=Sb[:, :], in0=Sh, scalar=glast[:, c, hh:hh + 1], in1=ps[5][:, 0:128], op0=ALU.mult, op1=ALU.add),
                        reads=["S", Sn, "glast%d" % qp, "ps5"], writes=["Sb"])
                    ch.op("dve", lambda e, c=c: e.scalar_tensor_tensor(
                        out=Sh, in0=Sh, scalar=glast[:, c, hh:hh + 1], in1=ps[5][:, 0:128], op0=ALU.mult, op1=ALU.add),
                        reads=["S", Sn, "glast%d" % qp, "ps5"], writes=[Sn])
                    if cg >= 1 and ((cg - 1) % 8 == 7 or c == NC - 1):
                        ng = (cg - 1) % 8 + 1
                        Wg = ng * 64
                        r0 = (cg - ng) * 64
                        z0 = (c - ng + 1) * 64
                        ch.op("act", lambda e, Wg=Wg: e.activation(out=osb[:, 0:Wg], in_=ps[7][:, 0:Wg], func=AF.Identity),
                              reads=["ps7"], writes=["osb"])
                        ch.op("pool", lambda e, Wg=Wg: e.tensor_tensor(out=osq[:, 0:Wg], in0=osb[:, 0:Wg], in1=osb[:, 0:Wg], op=ALU.mult),
                              reads=["osb"], writes=["osq"])
                        ch.op("pe", lambda e, Wg=Wg: e.matmul(ps[6][:, 0:Wg], lhsT=onesb[:, :], rhs=osq[:, 0:Wg], start=True, stop=True),
                              reads=["osq", "onesb"], writes=["ps6"])
                        ch.op("act", lambda e, Wg=Wg: e.activation(out=orn[:, 0:Wg], in_=ps[6][:, 0:Wg], func=AF.Ln,
                                                                   bias=RMS_EPS, scale=1.0 / 128.0),
                              reads=["ps6"], writes=["orn"])
                        ch.op("act", lambda e, Wg=Wg: e.activation(out=orn[:, 0:Wg], in_=orn[:, 0:Wg], func=AF.Exp, scale=-0.5),
                              reads=["orn"], writes=["orn"])
                        ch.op("dve", lambda e, Wg=Wg: e.scalar_tensor_tensor(
                            out=osb[:, 0:Wg], in0=osb[:, 0:Wg], scalar=gnw[:, 0:1], in1=orn[:, 0:Wg], op0=ALU.mult, op1=ALU.mult),
                            reads=["osb", "gnw", "orn"], writes=["osb"])
                        ch.op("dve", lambda e, Wg=Wg, r0=r0, z0=z0: e.tensor_tensor(
                            out=yaT[:, hh, r0:r0 + Wg], in0=osb[:, 0:Wg], in1=zs_[:, z0:z0 + Wg], op=ALU.mult),
                            reads=["osb", "zs" + st], writes=["yaT"])
                return ch

            nI = len(items)
            sched = Sched()
            for step in range(nI + 3):
                lists = []
                if step < nI:
                    la = stageA(step)
                    sched.run([la[0]], P.add_item)
                    lists.extend(la[1:])
                if 0 <= step - 1 < nI:
                    lists.append(stageB1(step - 1))
                if 0 <= step - 2 < nI:
                    lists.append(stageB2(step - 2))
                if 0 <= step - 3 < nI:
                    lists.append(stageC(step - 3))
                sched.run(lists, P.add_item)
            ld("pool", wk0_top[:], wkv[0], "ld_wk0", "wk0")
            ld("pool", wv0_top[:], wkv[1], "ld_wv0", "wv0")
        marks["G"] = len(P.ops)
        P.fence()
        ybT = sbuf(top, "ybT", [128, 8, SEQ], BF16)
        wf0_top = [sbuf(top, "wf%d" % i, [128, 8, 128], BF16) for i in range(4)]

        with contextlib.ExitStack() as s_:
            biasP = sbuf(s_, "biasP", [128, 4, 512], BF16)
            biasO = sbuf(s_, "biasO", [128, 4, 512], BF16)
            lm = sbuf(s_, "lm", [3, 16, 16], BF16)
            rm = sbuf(s_, "rm", [3, 4, 512], BF16)
            skf = sbuf(s_, "skf", [33, 4, 512], F32)
            esk = sbuf(s_, "esk", [33, 4, 512], BF16)
            wk_ = [wk0_top, sbuf(s_, "wk1", [128, 8, 128], BF16)]
            wv_ = [wv0_top, sbuf(s_, "wv1", [128, 8, 128], BF16)]
            wq2 = [sbuf(s_, "wq2_%d" % i, [128, 8, 128], BF16) for i in range(4)]
            wz2 = [sbuf(s_, "wz2_%d" % i, [128, 8, 128], BF16) for i in range(4)]
            kTr = sbuf(s_, "kTr", [128, SEQ], BF16)
            kTm = sbuf(s_, "kTm", [128, 16], BF16)
            vtk = sbuf(s_, "vtk", [128, 16, 128], BF16)
            vmt = sbuf(s_, "vmt", [33, 128], BF16)
            qTh = [sbuf(s_, "qTh%d" % i, [128, 2, SEQ], BF16) for i in range(2)]
            zsb = sbuf(s_, "zsb", [128, 2, SEQ], BF16)
            PTp = [sbuf(s_, "PTp%d" % i, [128, 512], BF16) for i in range(2)]
            PTo = [sbuf(s_, "PTo%d" % i, [128, 512], BF16) for i in range(2)]
            PTm = [sbuf(s_, "PTm%d" % i, [33, 512], BF16) for i in range(2)]
            rden = sbuf(s_, "rden", [128, 512], F32)
            tmpo = sbuf(s_, "tmpo", [128, 2, 128], F32)

            ld("pool", biasP[:], biasP_d, "ld_c", "biasP")
            ld("pool", biasO[:], biasO_d, "ld_c", "biasO")
            ld("pool", lm[:], lm_d, "ld_c", "lm")
            ld("pool", rm[:], rm_d, "ld_c", "rm")
            ld("sp", skf[32:33, :, :], sk_d, "ld_c", "skf")
            P.op("act", lambda e: e.activation(out=esk[32:33, :, :], in_=skf[32:33, :, :], func=AF.Exp), reads=["skf"], writes=["esk"])
            P.op("dve", lambda e: e.memset(vmt[:, :], 0.0), writes=["vmt"])
            for pp_ in range(2):
                P.op("dve", lambda e, pp_=pp_: e.memset(PTm[pp_][:, :], 0.0), writes=["PTm%d" % pp_])
            P.op("pool", lambda e: e.memset(qTh[0][64:128, :, :], 0.0), writes=["qTz0"])
            P.op("pool", lambda e: e.memset(qTh[1][0:64, :, :], 0.0), writes=["qTz1"])

            def swa_wload(kh):
                w = kh % 2
                if kh > 0:
                    ld("pool", wk_[w][:], wkv[kh * 2], "ld_wk%d" % w, "wk%d" % w)
                    ld("pool", wv_[w][:], wkv[kh * 2 + 1], "ld_wv%d" % w, "wv%d" % w)
                for qt in range(2):
                    ld("pool", wq2[w * 2 + qt][:], wqz[kh * 2 + qt], "ld_wq%d" % (w * 2 + qt), "wq%d" % (w * 2 + qt))
                    ld("pool", wz2[w * 2 + qt][:], wqz[8 + kh * 2 + qt], "ld_wz%d" % (w * 2 + qt), "wz%d" % (w * 2 + qt))

            swa_wload(0)
            for kh in range(4):
                w = kh % 2
                for blk in range(4):
                    b = blk % 2

                    def f(e, b=b, blk=blk, w=w):
                        r = None
                        for kc in range(8):
                            r = e.matmul(ps[b][:, 0:512], lhsT=wk_[w][:, kc, :], rhs=hT[:, kc, XCOL + blk * 512: XCOL + (blk + 1) * 512],
                                         start=(kc == 0), stop=(kc == 7))
                        return r
                    P.op("pe", f, reads=["hT", "wk%d" % w], writes=["ps%d" % b])
                    P.op("act", lambda e, b=b, blk=blk: e.activation(out=kTr[:, blk * 512:(blk + 1) * 512], in_=ps[b][:, 0:512], func=AF.Identity),
                         reads=["ps%d" % b], writes=["kTr"])

                def f(e, w=w):
                    r = None
                    for kc in range(8):
                        r = e.matmul(ps[0][:, 0:16], lhsT=wk_[w][:, kc, :], rhs=hT[:, kc, MCOL:XCOL], start=(kc == 0), stop=(kc == 7))
                    return r
                P.op("pe", f, reads=["hT", "wk%d" % w], writes=["ps0"])
                P.op("act", lambda e: e.activation(out=kTm[:, :], in_=ps[0][:, 0:16], func=AF.Identity), reads=["ps0"], writes=["kTm"])
                for m4 in range(4):
                    b = m4 % 2

                    def f(e, b=b, m4=m4, w=w):
                        r = None
                        for i in range(4):
                            m = m4 * 4 + i
                            for kc in range(8):
                                r = e.matmul(ps[b][:, i * 128:(i + 1) * 128], lhsT=hT[:, kc, XCOL + m * 128: XCOL + (m + 1) * 128],
                                             rhs=wv_[w][:, kc, :], start=(kc == 0), stop=(kc == 7))
                        return r
                    P.op("pe", f, reads=["hT", "wv%d" % w], writes=["ps%d" % b])
                    P.op("act", lambda e, b=b, m4=m4: e.activation(out=vtk[:, m4 * 4:(m4 + 1) * 4, :],
                                                                   in_=ps[b][:, 0:512].rearrange("p (c k) -> p c k", k=128), func=AF.Identity),
                         reads=["ps%d" % b], writes=["vtk"])

                def f(e, w=w):
                    r = None
                    for kc in range(8):
                        r = e.matmul(ps[1][0:16, 0:128], lhsT=hT[:, kc, MCOL:XCOL], rhs=wv_[w][:, kc, :], start=(kc == 0), stop=(kc == 7))
                    return r
                P.op("pe", f, reads=["hT", "wv%d" % w], writes=["ps1"])
                P.op("act", lambda e: e.activation(out=vmt[0:16, :], in_=ps[1][0:16, 0:128], func=AF.Identity), reads=["ps1"], writes=["vmt"])
                for qt in range(2):
                    for blk in range(4):
                        b = blk % 2

                        def f(e, b=b, blk=blk, wi=w * 2 + qt):
                            r = None
                            for kc in range(8):
                                r = e.matmul(ps[b][:, 0:512], lhsT=wq2[wi][:, kc, :], rhs=hT[:, kc, XCOL + blk * 512: XCOL + (blk + 1) * 512],
                                             start=(kc == 0), stop=(kc == 7))
                            return r
                        P.op("pe", f, reads=["hT", "wq%d" % (w * 2 + qt)], writes=["ps%d" % b])
                        P.op("act", lambda e, b=b, blk=blk, qt=qt: e.activation(out=qTh[0][0:64, qt, blk * 512:(blk + 1) * 512],
                                                                              in_=ps[b][0:64, 0:512], func=AF.Identity),
                             reads=["ps%d" % b], writes=["qT"])
                        P.op("dve", lambda e, b=b, blk=blk, qt=qt: e.tensor_copy(out=qTh[1][64:128, qt, blk * 512:(blk + 1) * 512],
                                                                               in_=ps[b][64:128, 0:512]),
                             reads=["ps%d" % b], writes=["qTb"])
                    for blk in range(4):
                        b = blk % 2

                        def f(e, b=b, blk=blk, wi=w * 2 + qt):
                            r = None
                            for kc in range(8):
                                r = e.matmul(ps[b][:, 0:512], lhsT=wz2[wi][:, kc, :], rhs=hT[:, kc, XCOL + blk * 512: XCOL + (blk + 1) * 512],
                                             start=(kc == 0), stop=(kc == 7))
                            return r
                        P.op("pe", f, reads=["hT", "wz%d" % (w * 2 + qt)], writes=["ps%d" % b])
                        P.op("act", lambda e, b=b, blk=blk, qt=qt: e.activation(out=zsb[:, qt, blk * 512:(blk + 1) * 512], in_=ps[b][:, 0:512],
                                                                              func=AF.Silu),
                             reads=["ps%d" % b], writes=["zsb"])
                for pp_ in range(2):
                    P.op("act", lambda e, pp_=pp_: e.activation(out=PTm[pp_][32:33, :], in_=esk[32:33, kh, :], func=AF.Identity),
                         reads=["esk"], writes=["PTm%d" % pp_])

                def stage_s(m):
                    pp = m % 2
                    bP, bO, bM = (0, 2, 4) if pp == 0 else (1, 3, 7)

                    def scores(e, bank, keys, nkeys):
                        r = None
                        for qt in range(2):
                            for half in range(2):
                                r = e.matmul(ps[bank][0:nkeys, (2 * qt + half) * 128:(2 * qt + half + 1) * 128],
                                             lhsT=keys, rhs=qTh[half][:, qt, m * 128:(m + 1) * 128],
                                             start=(qt == 0 and half == 0), stop=False)
                        return r
                    qreads = ["qT", "qTb", "qTz0", "qTz1"]
                    if m >= 1:
                        def f(e):
                            scores(e, bP, kTr[:, (m - 1) * 128: m * 128], 128)
                            return e.matmul(ps[bP][:, 0:512], lhsT=identb[:, :], rhs=biasP[:, kh, :], start=False, stop=True)
                        P.op("pe", f, reads=["kTr", "identb", "biasP"] + qreads, writes=["ps%d" % bP])
                        P.op("act", lambda e: e.activation(out=PTp[pp][:, :], in_=ps[bP][:, 0:512], func=AF.Exp, scale=0.125),
                             reads=["ps%d" % bP], writes=["PTp%d" % pp])

                    def f(e):
                        scores(e, bO, kTr[:, m * 128:(m + 1) * 128], 128)
                        return e.matmul(ps[bO][:, 0:512], lhsT=identb[:, :], rhs=biasO[:, kh, :], start=False, stop=True)
                    P.op("pe", f, reads=["kTr", "identb", "biasO"] + qreads, writes=["ps%d" % bO])
                    P.op("act", lambda e: e.activation(out=PTo[pp][:, :], in_=ps[bO][:, 0:512], func=AF.Exp, scale=0.125),
                         reads=["ps%d" % bO], writes=["PTo%d" % pp])

                    def f(e):
                        scores(e, bM, kTm[:, 0:16], 16)
                        return e.matmul(ps[bM][0:16, 0:512], lhsT=lm[0:3, m, :], rhs=rm[0:3, kh, :], start=False, stop=True)
                    P.op("pe", f, reads=["kTm", "lm", "rm"] + qreads, writes=["ps%d" % bM])
                    P.op("act", lambda e: e.activation(out=PTm[pp][0:16, :], in_=ps[bM][0:16, 0:512], func=AF.Exp, scale=0.125),
                         reads=["ps%d" % bM], writes=["PTm%d" % pp])

                def stage_r(m):
                    pp = m % 2

                    def f(e):
                        first = True
                        if m >= 1:
                            e.matmul(ps[5][:, 0:512], lhsT=vtk[:, m - 1, :], rhs=PTp[pp][:, :], start=True, stop=False)
                            first = False
                        e.matmul(ps[5][:, 0:512], lhsT=vtk[:, m, :], rhs=PTo[pp][:, :], start=first, stop=False)
                        e.matmul(ps[5][:, 0:512], lhsT=vmt[0:33, :], rhs=PTm[pp][0:33, :], start=False, stop=True)
                        first = True
                        if m >= 1:
                            e.matmul(ps[6][:, 0:512], lhsT=onesb[:, :], rhs=PTp[pp][:, :], start=True, stop=False)
                            first = False
                        e.matmul(ps[6][:, 0:512], lhsT=onesb[:, :], rhs=PTo[pp][:, :], start=first, stop=False)
                        return e.matmul(ps[6][:, 0:512], lhsT=onesb[0:33, :], rhs=PTm[pp][0:33, :], start=False, stop=True)
                    P.op("pe", f, reads=["vtk", "vmt", "PTp%d" % pp, "PTo%d" % pp, "PTm%d" % pp, "onesb"], writes=["ps5", "ps6"])
                    P.op("act", lambda e: e.activation(out=rden[:, :], in_=ps[6][:, 0:512], func=AF.Ln), reads=["ps6"], writes=["rden"])
                    P.op("act", lambda e: e.activation(out=rden[:, :], in_=rden[:, :], func=AF.Exp, scale=-1.0), reads=["rden"], writes=["rden"])
                    for half in range(2):
                        rows = slice(half * 64, (half + 1) * 64)
                        cs = slice(half * 128, (half + 1) * 128)

                        def f(e, rows=rows, cs=cs):
                            o3 = ps[5][rows, 0:512].rearrange("p (a b) -> p a b", a=2)[:, :, cs]
                            r3 = rden[rows, :].rearrange("p (a b) -> p a b", a=2)[:, :, cs]
                            return e.tensor_tensor(out=tmpo[rows, :, :], in0=o3, in1=r3, op=ALU.mult)
                        P.op("dve", f, reads=["ps5", "rden"], writes=["tmpo%d" % half])
                        P.op("pool", lambda e, rows=rows: e.tensor_tensor(
                            out=ybT[rows, 2 * kh:2 * kh + 2, m * 128:(m + 1) * 128], in0=tmpo[rows, :, :],
                            in1=zsb[rows, :, m * 128:(m + 1) * 128], op=ALU.mult),
                            reads=["tmpo%d" % half, "zsb"], writes=["ybT"])

                if kh + 1 < 4:
                    swa_wload(kh + 1)
                stage_s(0)
                for m in range(16):
                    if m + 1 < 16:
                        stage_s(m + 1)
                    stage_r(m)
            for i_ in range(4):
                src_ = (wpa, wpb, wgate, wgate)[i_][0 if i_ < 3 else 8]
                ld("pool", wf0_top[i_][:], src_, "ld_wf%d" % i_, "wf%d" % i_)
        marks["S"] = len(P.ops)
        P.fence()

        if debug:
            with contextlib.ExitStack() as d_:
                dtmp = sbuf(d_, "dtmp", [128, 8, SEQ], F32)
                P.op("dve", lambda e: e.tensor_copy(out=dtmp[:], in_=yaT[:]), reads=["yaT"], writes=["dtmp"])
                P.dma("sp", lambda e, s: e.dma_start(out=dbg_ya, in_=dtmp[:]).then_inc(s, 16), "st_dbg", reads=["dtmp"])
                P.op("dve", lambda e: e.tensor_copy(out=dtmp[:], in_=ybT[:]), reads=["ybT"], writes=["dtmp"])
                P.dma("sp", lambda e, s: e.dma_start(out=dbg_yb, in_=dtmp[:]).then_inc(s, 16), "st_dbg", reads=["dtmp"])
            P.fence()

        with contextlib.ExitStack() as f_:
            mixT = sbuf(f_, "mixT", [128, 8, SEQ], BF16)
            wf = wf0_top + [sbuf(f_, "wf%d" % i, [128, 8, 128], BF16) for i in range(4, 8)]
            wo = sbuf(f_, "wo", [128, 8, D], BF16)
            lnw = sbuf(f_, "lnw", [128, D], F32)
            lnb = sbuf(f_, "lnb", [128, D], F32)
            sga = sbuf(f_, "sga", [128, 512], F32)
            sgb = sbuf(f_, "sgb", [128, 512], F32)
            m1 = sbuf(f_, "m1", [128, 512], F32)
            m2 = sbuf(f_, "m2", [128, 512], F32)
            xt = [sbuf(f_, "xt%d" % i, [128, D], F32) for i in range(3)]
            res = [sbuf(f_, "res%d" % i, [128, D], F32) for i in range(3)]
            st3 = [sbuf(f_, "st3_%d" % i, [128, 8], F32) for i in range(3)]

            ld("pool", wo[:], wout, "ld_wo", "wo")
            ld("sp", lnw[:], lnw_d, "ld_c", "lnw")
            ld("sp", lnb[:], lnb_d, "ld_c", "lnb")
            srcs = (wpa, wpb, wgate, wgate)
            def fin_wload(dt_):
                sset = (dt_ % 2) * 4
                for i in range(4):
                    src = srcs[i][dt_ if i < 3 else 8 + dt_]
                    ld("pool", wf[sset + i][:], src, "ld_wf%d" % (sset + i), "wf%d" % (sset + i))

            for dt_ in range(8):
                sset = (dt_ % 2) * 4
                for tb in range(4):
                    if tb == 1 and dt_ + 1 < 8:
                        fin_wload(dt_ + 1)
                    tsl = slice(tb * 512, (tb + 1) * 512)
                    hsl = slice(XCOL + tb * 512, XCOL + (tb + 1) * 512)
                    banks = (0, 1, 2, 3) if tb % 2 == 0 else (4, 5, 6, 7)
                    rhs_l = (lambda kc, tsl=tsl: yaT[:, kc, tsl], lambda kc, tsl=tsl: ybT[:, kc, tsl],
                             lambda kc, hsl=hsl: hT[:, kc, hsl], lambda kc, hsl=hsl: hT[:, kc, hsl])
                    rd = ("yaT", "ybT", "hT", "hT")
                    for i in range(4):
                        def f(e, i=i, bk=banks[i], rf=rhs_l[i], wi=sset + i):
                            r = None
                            for kc in range(8):
                                r = e.matmul(ps[bk][:, 0:512], lhsT=wf[wi][:, kc, :], rhs=rf(kc), start=(kc == 0), stop=(kc == 7))
                            return r
                        P.op("pe", f, reads=[rd[i], "wf%d" % (sset + i)], writes=["ps%d" % banks[i]])
                    P.op("act", lambda e, bk=banks[2], dt_=dt_: e.activation(out=sga[:, :], in_=ps[bk][:, 0:512], func=AF.Sigmoid,
                                                                            bias=bg[:, dt_:dt_ + 1], scale=1.0),
                         reads=["ps%d" % banks[2], "bg"], writes=["sga"])
                    P.op("act", lambda e, bk=banks[3], dt_=dt_: e.activation(out=sgb[:, :], in_=ps[bk][:, 0:512], func=AF.Sigmoid,
                                                                            bias=bg[:, 8 + dt_:9 + dt_], scale=1.0),
                         reads=["ps%d" % banks[3], "bg"], writes=["sgb"])
                    P.op("dve", lambda e, bk=banks[0]: e.tensor_tensor(out=m1[:, :], in0=ps[bk][:, 0:512], in1=sga[:, :], op=ALU.mult),
                         reads=["ps%d" % banks[0], "sga"], writes=["m1"])
                    P.op("dve", lambda e, bk=banks[1]: e.tensor_tensor(out=m2[:, :], in0=ps[bk][:, 0:512], in1=sgb[:, :], op=ALU.mult),
                         reads=["ps%d" % banks[1], "sgb"], writes=["m2"])
                    P.op("pool", lambda e, dt_=dt_, tsl=tsl: e.tensor_tensor(out=mixT[:, dt_, tsl], in0=m1[:, :], in1=m2[:, :], op=ALU.add),
                         reads=["m1", "m2"], writes=["mixT"])
            def lnS1(tt):
                k3 = tt % 3
                pp = tt % 2
                x_, r_ = xt[k3], res[k3]
                ch = Chain()
                ch.dma("sp", lambda e, s: e.dma_start(out=x_[:], in_=xtok[tt * 128:(tt + 1) * 128, :]).then_inc(s, 16),
                       "ld_x%d" % k3, writes=["xt%d" % k3])
                bks = (0, 1) if pp == 0 else (2, 3)
                for hb in range(2):
                    def f(e, hb=hb, bk=bks[hb]):
                        r = None
                        for kc in range(8):
                            r = e.matmul(ps[bk][:, 0:512], lhsT=mixT[:, kc, tt * 128:(tt + 1) * 128], rhs=wo[:, kc, hb * 512:(hb + 1) * 512],
                                         start=(kc == 0), stop=(kc == 7))
                        return r
                    ch.op("pe", f, reads=["mixT", "wo"], writes=["ps%d" % bks[hb]])
                    ch.op("dve", lambda e, hb=hb, bk=bks[hb]: e.scalar_tensor_tensor(
                        out=r_[:, hb * 512:(hb + 1) * 512], in0=x_[:, hb * 512:(hb + 1) * 512], scalar=DEEPNORM_ALPHA,
                        in1=ps[bk][:, 0:512], op0=ALU.mult, op1=ALU.add),
                        reads=["xt%d" % k3, "ps%d" % bks[hb]], writes=["res%d" % k3])
                return ch

            def lnS2(tt):
                k3 = tt % 3
                x_, r_, s_ = xt[k3], res[k3], st3[k3]
                sn = "st%d" % k3
                ch = Chain()
                ch.op("dve", lambda e: e.reduce_sum(out=s_[:, 0:1], in_=r_[:, :], axis=AX.X), reads=["res%d" % k3], writes=[sn + "a"])
                ch.op("act", lambda e: e.activation(out=x_[:, :], in_=r_[:, :], func=AF.Square),
                      reads=["res%d" % k3], writes=["xt%d" % k3])
                ch.op("dve", lambda e: e.reduce_sum(out=s_[:, 1:2], in_=x_[:, :], axis=AX.X), reads=["xt%d" % k3], writes=[sn + "b"])
                ch.op("dve", lambda e: e.tensor_scalar(out=s_[:, 2:3], in0=s_[:, 0:1], scalar1=1.0 / D, scalar2=None, op0=ALU.mult),
                      reads=[sn + "a"], writes=[sn + "c"])
                ch.op("dve", lambda e: e.tensor_tensor(out=s_[:, 3:4], in0=s_[:, 2:3], in1=s_[:, 2:3], op=ALU.mult),
                      reads=[sn + "c"], writes=[sn + "d"])
                ch.op("dve", lambda e: e.scalar_tensor_tensor(out=s_[:, 4:5], in0=s_[:, 1:2], scalar=1.0 / D, in1=s_[:, 3:4],
                                                              op0=ALU.mult, op1=ALU.subtract),
                      reads=[sn + "b", sn + "d"], writes=[sn + "e"])
                ch.op("act", lambda e: e.activation(out=s_[:, 5:6], in_=s_[:, 4:5], func=AF.Sqrt, bias=LN_EPS, scale=1.0),
                      reads=[sn + "e"], writes=[sn + "f"])
                ch.op("dve", lambda e: e.reciprocal(out=s_[:, 6:7], in_=s_[:, 5:6]), reads=[sn + "f"], writes=[sn + "g"])
                return ch

            def lnS3(tt):
                k3 = tt % 3
                r_, s_ = res[k3], st3[k3]
                sn = "st%d" % k3
                ch = Chain()
                ch.op("dve", lambda e: e.tensor_scalar(out=r_[:, :], in0=r_[:, :], scalar1=s_[:, 2:3], scalar2=s_[:, 6:7],
                                                       op0=ALU.subtract, op1=ALU.mult),
                      reads=["res%d" % k3, sn + "c", sn + "g"], writes=["res%d" % k3])
                ch.op("pool", lambda e: e.tensor_tensor(out=r_[:, :], in0=r_[:, :], in1=lnw[:, :], op=ALU.mult),
                      reads=["res%d" % k3, "lnw"], writes=["res%d" % k3])
                ch.op("pool", lambda e: e.tensor_tensor(out=r_[:, :], in0=r_[:, :], in1=lnb[:, :], op=ALU.add),
                      reads=["res%d" % k3, "lnb"], writes=["res%d" % k3])
                ch.dma("sp", lambda e, s: e.dma_start(out=out_d[tt * 128:(tt + 1) * 128, :], in_=r_[:]).then_inc(s, 16),
                       "st_out%d" % k3, reads=["res%d" % k3])
                return ch

            for step in range(16 + 2):
                lists = []
                if step < 16:
                    lists.append(lnS1(step))
                if 0 <= step - 1 < 16:
                    lists.append(lnS2(step - 1))
                if 0 <= step - 2 < 16:
                    lists.append(lnS3(step - 2))
                for it in interleave(lists):
                    P.add_item(it)
        _MARKS.update(marks)
        _MARKS["end"] = len(P.ops)
        lim = None
        if stop is not None:
            lim = marks[stop] if stop in marks else int(stop)
        P.emit(nc, limit=lim)
    return nc


def _tile_w(w):
    n = w.shape[1] // 128
    return np.ascontiguousarray(w.reshape(8, 128, n, 128).transpose(2, 1, 0, 3))


def _const_tables():
    t = np.arange(64)
    c64 = np.zeros((64, 5, 64), np.float32)
    c64[:, 0, :] = (t[:, None] <= t[None, :])
    c64[:, 1, :] = (t[:, None] > t[None, :])
    c64[:, 2, :] = np.eye(64)
    c64[:, 3, :] = np.where(t[None, :] < t[:, None], NEG, 0.0)
    c64[:, 4, :] = np.where(t[None, :] >= t[:, None], NEG, 0.0)
    slopes = np.exp2(-8.0 * np.arange(1, 17, dtype=np.float64) / 16.0)
    k = np.arange(128)[:, None]
    i = np.arange(128)[None, :]
    biasP = np.zeros((128, 4, 512), np.float64)
    biasO = np.zeros((128, 4, 512), np.float64)
    rm = np.zeros((3, 4, 512), np.float64)
    for kh in range(4):
        for qt in range(2):
            for half in range(2):
                h = 4 * kh + 2 * qt + half
                cs = slice((2 * qt + half) * 128, (2 * qt + half + 1) * 128)
                bp = -8.0 * slopes[h] * (128 + i - k)
                bp = np.where((k < 64) & (i >= 64), 8.0 * NEG, bp)
                biasP[:, kh, cs] = bp
                bo = -8.0 * slopes[h] * np.abs(i - k)
                bo = np.where((k >= 64) & (i < 64), 8.0 * NEG, bo)
                biasO[:, kh, cs] = bo
                rm[0, kh, cs] = -8.0 * slopes[h] * (16 + np.arange(128))
                rm[1, kh, cs] = 8.0 * slopes[h]
                rm[2, kh, cs] = -8.0 * slopes[h] * 128.0
    lm = np.zeros((3, 16, 16), np.float32)
    lm[0] = 1.0
    lm[1] = np.arange(16)[None, :]
    lm[2] = np.arange(16)[:, None]
    return (c64, biasP.astype(np.float32), biasO.astype(np.float32), lm, rm.astype(np.float32))


_CACHE = {}


def _get_program(debug=False):
    if debug not in _CACHE:
        _CACHE[debug] = build_program(debug)
    return _CACHE[debug]


def _prepare(x, meta_tokens, w_in, b_gate, conv_w, a_log, dt_bias, gdn_norm_w, sinks,
             w_proj_a, w_proj_b, w_out, ln_w, ln_b):
    f = np.float32
    w = np.asarray(w_in, f)[0]
    o = 0
    seg = {}
    for name, wd in zip(("qa", "ka", "va", "za", "be", "de", "qb", "kb", "vb", "zb", "ga", "gb"),
                        (1024, 1024, 1024, 1024, 8, 8, 1024, 256, 256, 1024, 1024, 1024)):
        seg[name] = w[:, o:o + wd]
        o += wd
    tq, tk, tv, tz = (_tile_w(seg[n]) for n in ("qa", "ka", "va", "za"))
    wg = np.ascontiguousarray(np.stack([tq, tk, tv, tz], axis=1).reshape(32, 128, 8, 128))
    wbd = np.ascontiguousarray(np.concatenate([seg["be"], seg["de"]], axis=1).reshape(8, 128, 16).transpose(1, 0, 2))
    kv = []
    for kh in range(4):
        kc_ = seg["kb"][:, kh * 64:(kh + 1) * 64]
        vc_ = seg["vb"][:, kh * 64:(kh + 1) * 64]
        kv.append(np.concatenate([kc_, kc_], axis=1))
        kv.append(np.concatenate([vc_, vc_], axis=1))
    wkv = _tile_w(np.concatenate(kv, axis=1))
    wqz = np.concatenate([_tile_w(seg["qb"]), _tile_w(seg["zb"])], axis=0)
    wgate = np.concatenate([_tile_w(seg["ga"]), _tile_w(seg["gb"])], axis=0)
    wpa = _tile_w(np.asarray(w_proj_a, f)[0])
    wpb = _tile_w(np.asarray(w_proj_b, f)[0])
    wout = np.ascontiguousarray(np.asarray(w_out, f)[0].reshape(8, 128, D).transpose(1, 0, 2))
    c64, biasP, biasO, lm, rm = _const_tables()
    cw = np.ascontiguousarray(np.asarray(conv_w, f)[0].T.reshape(24, 128, 4).transpose(1, 0, 2))
    sk = np.zeros((1, 4, 512), f)
    sv = np.asarray(sinks, f)[0]
    for kh in range(4):
        for qt in range(2):
            for half in range(2):
                sk[0, kh, (2 * qt + half) * 128:(2 * qt + half + 1) * 128] = sv[4 * kh + 2 * qt + half]
    shared = {
        "metaT": np.ascontiguousarray(np.asarray(meta_tokens, f).T),
        "wg": wg, "wbd": wbd, "wkv": wkv, "wqz": wqz, "wgate": wgate, "wpa": wpa, "wpb": wpb, "wout": wout,
        "c64": c64, "ident": np.eye(128, dtype=f), "cw": cw,
        "gnw": np.ascontiguousarray(np.asarray(gdn_norm_w, f)[0].reshape(128, 1)),
        "bg": np.ascontiguousarray(np.asarray(b_gate, f)[0].reshape(16, 128).T),
        "alog": np.ascontiguousarray(np.broadcast_to(np.asarray(a_log, f)[0][None, :], (64, 8))),
        "dtb": np.ascontiguousarray(np.broadcast_to(np.asarray(dt_bias, f)[0][None, :], (64, 8))),
        "sk": sk,
        "lnw": np.ascontiguousarray(np.broadcast_to(np.asarray(ln_w, f)[0][None, :], (128, D))),
        "lnb": np.ascontiguousarray(np.broadcast_to(np.asarray(ln_b, f)[0][None, :], (128, D))),
        "biasP": biasP, "biasO": biasO, "lm": lm, "rm": rm,
    }
    xs = np.asarray(x, f)
    in_maps = []
    for b in range(xs.shape[0]):
        m = dict(shared)
        m["xT"] = np.ascontiguousarray(xs[b].T)
        m["xtok"] = np.ascontiguousarray(xs[b])
        in_maps.append(m)
    return in_maps


def kernel(x, meta_tokens, w_in, b_gate, conv_w, a_log, dt_bias, gdn_norm_w, sinks,
           w_proj_a, w_proj_b, w_out, ln_w, ln_b, _debug=False):
    in_maps = _prepare(x, meta_tokens, w_in, b_gate, conv_w, a_log, dt_bias, gdn_norm_w, sinks,
                       w_proj_a, w_proj_b, w_out, ln_w, ln_b)
    nc = _get_program(_debug)
    res = run_bass_kernel_spmd(nc, in_maps, core_ids=list(range(len(in_maps))))
    out = np.stack([np.asarray(r["out"], np.float32) for r in res.results], axis=0)
    if _debug:
        return out, res
    return out
```
